# Optimizing a Trainium2 kernel written in Bass

```python
import math
import jax, jax.numpy as jnp
from jax import lax
import numpy as np

D_MODEL = 2048
BATCH = 2
SEQ = 16384
DEPTH = 1

N_MEM = 256
HEAD_DIM = 128
N_ATTN_HEADS = 8
ATTN_WIDTH = N_ATTN_HEADS * HEAD_DIM
CONV_CH = D_MODEL // 2
MIX_WIDTH = ATTN_WIDTH + CONV_CH
IN_WIDTH = 3 * ATTN_WIDTH + 2 * CONV_CH
CONV_K = 31
MOBA_BLOCK = 256
MOBA_TOPK = 3
Q_CHUNK = 64
N_BUCKETS = 32
MAX_DISTANCE = 2048
N_CROSS_HEADS = 4
CROSS_HEAD_DIM = 128
CROSS_WIDTH = N_CROSS_HEADS * CROSS_HEAD_DIM
D_FF = 4 * D_MODEL
EPS = 1e-6

kernel_name = "hybrid_moba_conformer_xattn_block"


def rmsnorm(x, g):
    xf = x.astype(jnp.float32)
    y = xf * lax.rsqrt(jnp.mean(xf * xf, axis=-1, keepdims=True) + EPS)
    return (y * g.astype(jnp.float32)).astype(x.dtype)


def layernorm(x, g, b):
    xf = x.astype(jnp.float32)
    mu = jnp.mean(xf, axis=-1, keepdims=True)
    xc = xf - mu
    var = jnp.mean(xc * xc, axis=-1, keepdims=True)
    y = xc * lax.rsqrt(var + EPS) * g.astype(jnp.float32) + b.astype(jnp.float32)
    return y.astype(x.dtype)


def t5_bucket(dist):
    max_exact = N_BUCKETS // 2
    nf = jnp.maximum(dist, max_exact).astype(jnp.float32)
    large = max_exact + (jnp.log(nf / max_exact) / math.log(MAX_DISTANCE / max_exact)
                         * (N_BUCKETS - max_exact)).astype(jnp.int32)
    large = jnp.minimum(large, N_BUCKETS - 1)
    return jnp.where(dist < max_exact, dist, large)


def moba_attention(q, k, v, dist_bias):
    B, H, S, Dh = q.shape
    nb = -(-S // MOBA_BLOCK)
    pad = nb * MOBA_BLOCK - S
    kp = jnp.pad(k, ((0, 0), (0, 0), (0, pad), (0, 0)))
    vp = jnp.pad(v, ((0, 0), (0, 0), (0, pad), (0, 0)))
    kb = kp.reshape(B, H, nb, MOBA_BLOCK, Dh)
    vb = vp.reshape(B, H, nb, MOBA_BLOCK, Dh)
    kbar = jnp.mean(kb.astype(jnp.float32), axis=3)
    k_eff = min(MOBA_TOPK, nb)
    scale = HEAD_DIM ** -0.5
    bi = jnp.arange(B)[:, None, None, None]
    hi = jnp.arange(H)[None, :, None, None]
    blk_offs = jnp.arange(MOBA_BLOCK)

    def chunk(ci):
        start = ci * Q_CHUNK
        qc = lax.dynamic_slice_in_dim(q, start, Q_CHUNK, axis=2)
        own = start // MOBA_BLOCK
        qpos = start + jnp.arange(Q_CHUNK)
        gate = jnp.einsum('bhqd,bhnd->bhqn', qc.astype(jnp.float32), kbar)
        gate = jnp.where(jnp.arange(nb) < own, gate, -jnp.inf)
        _, idx = lax.top_k(gate, k_eff)
        valid = jnp.arange(k_eff) < own
        ks = kb[bi, hi, idx]
        vs = vb[bi, hi, idx]
        kpos = idx[..., None] * MOBA_BLOCK + blk_offs
        d_sel = jnp.maximum(qpos[:, None, None] - kpos, 0)
        s_sel = (jnp.einsum('bhqd,bhqjkd->bhqjk', qc, ks).astype(jnp.float32) * scale
                 + dist_bias[hi[..., None], d_sel])
        s_sel = jnp.where(valid[:, None], s_sel, -jnp.inf)
        s_sel = s_sel.reshape(B, H, Q_CHUNK, k_eff * MOBA_BLOCK)
        ko = lax.dynamic_slice_in_dim(kp, own * MOBA_BLOCK, MOBA_BLOCK, axis=2)
        vo = lax.dynamic_slice_in_dim(vp, own * MOBA_BLOCK, MOBA_BLOCK, axis=2)
        d_own = qpos[:, None] - (own * MOBA_BLOCK + blk_offs)[None, :]
        s_own = (jnp.einsum('bhqd,bhkd->bhqk', qc, ko).astype(jnp.float32) * scale
                 + dist_bias[:, jnp.maximum(d_own, 0)][None])
        s_own = jnp.where(d_own >= 0, s_own, -jnp.inf)
        p = jax.nn.softmax(jnp.concatenate([s_sel, s_own], axis=-1), axis=-1)
        p_sel = p[..., :k_eff * MOBA_BLOCK].reshape(B, H, Q_CHUNK, k_eff, MOBA_BLOCK)
        p_own = p[..., k_eff * MOBA_BLOCK:]
        out = (jnp.einsum('bhqjk,bhqjkd->bhqd', p_sel.astype(vs.dtype), vs)
               + jnp.einsum('bhqk,bhkd->bhqd', p_own.astype(vo.dtype), vo))
        return out.astype(q.dtype)

    outs = lax.map(chunk, jnp.arange(S // Q_CHUNK))
    return outs.transpose(1, 0, 3, 2, 4).reshape(B, S, H * Dh)


def conformer_conv(u, conv_w, conv_b, ln_g, ln_b):
    val, gt = jnp.split(u, 2, axis=-1)
    h = val * jax.nn.sigmoid(gt)
    h = lax.conv_general_dilated(h, conv_w[:, None, :].astype(h.dtype), window_strides=(1,),
                                 padding=[(CONV_K - 1, 0)],
                                 dimension_numbers=('NWC', 'WIO', 'NWC'),
                                 feature_group_count=CONV_CH) + conv_b
    h = layernorm(h, ln_g, ln_b)
    return jax.nn.silu(h)


def cross_attention(n, m, wq, wk, wv, wo):
    B, S, _ = n.shape
    M = m.shape[1]
    q = (n @ wq).reshape(B, S, N_CROSS_HEADS, CROSS_HEAD_DIM)
    k = (m @ wk).reshape(B, M, N_CROSS_HEADS, CROSS_HEAD_DIM)
    v = (m @ wv).reshape(B, M, N_CROSS_HEADS, CROSS_HEAD_DIM)
    s = jnp.einsum('bshd,bmhd->bhsm', q, k).astype(jnp.float32) * CROSS_HEAD_DIM ** -0.5
    p = jax.nn.softmax(s, axis=-1).astype(v.dtype)
    o = jnp.einsum('bhsm,bmhd->bshd', p, v).reshape(B, S, CROSS_WIDTH)
    return o @ wo


def setup_inputs(seed: int = 0) -> dict:
    key = jax.random.key(seed)
    ks = jax.random.split(key, 24)
    f32 = jnp.float32
    nrm = lambda k, shape, s: jax.random.normal(k, shape, f32) * s
    gain = lambda k, shape: 1.0 + 0.02 * jax.random.normal(k, shape, f32)
    L = DEPTH
    return {
        "x": jax.random.normal(ks[0], (BATCH, SEQ, D_MODEL), f32),
        "mem": jax.random.normal(ks[1], (BATCH, N_MEM, D_MODEL), f32),
        "g_mix": gain(ks[2], (L, D_MODEL)),
        "w_in": nrm(ks[3], (L, D_MODEL, IN_WIDTH), D_MODEL ** -0.5),
        "conv_w": nrm(ks[4], (L, CONV_K, CONV_CH), CONV_K ** -0.5),
        "conv_b": nrm(ks[5], (L, CONV_CH), 0.02),
        "conv_ln_g": gain(ks[6], (L, CONV_CH)),
        "conv_ln_b": nrm(ks[7], (L, CONV_CH), 0.02),
        "w_out": nrm(ks[8], (L, MIX_WIDTH, D_MODEL), MIX_WIDTH ** -0.5),
        "rel_bias": nrm(ks[9], (N_BUCKETS, N_ATTN_HEADS), 0.5),
        "g_cross": gain(ks[10], (L, D_MODEL)),
        "g_mem": gain(ks[11], (L, D_MODEL)),
        "wq_c": nrm(ks[12], (L, D_MODEL, CROSS_WIDTH), D_MODEL ** -0.5),
        "wk_c": nrm(ks[13], (L, D_MODEL, CROSS_WIDTH), D_MODEL ** -0.5),
        "wv_c": nrm(ks[14], (L, D_MODEL, CROSS_WIDTH), D_MODEL ** -0.5),
        "wo_c": nrm(ks[15], (L, CROSS_WIDTH, D_MODEL), CROSS_WIDTH ** -0.5),
        "g_mlp": gain(ks[16], (L, D_MODEL)),
        "w1": nrm(ks[17], (L, D_MODEL, D_FF), D_MODEL ** -0.5),
        "w2": nrm(ks[18], (L, D_FF, D_MODEL), D_FF ** -0.5),
        "g_final": gain(ks[19], (D_MODEL,)),
    }


def reference(x, mem, g_mix, w_in, conv_w, conv_b, conv_ln_g, conv_ln_b, w_out, rel_bias,
              g_cross, g_mem, wq_c, wk_c, wv_c, wo_c, g_mlp, w1, w2, g_final):
    B, S, _ = x.shape
    dist_bias = rel_bias[t5_bucket(jnp.arange(S))].T.astype(jnp.float32)

    def to_heads(t):
        return t.reshape(B, S, N_ATTN_HEADS, HEAD_DIM).transpose(0, 2, 1, 3)

    h = x
    for l in range(DEPTH):
        n = rmsnorm(h, g_mix[l])
        z = n @ w_in[l]
        q, k, v, u = jnp.split(z, [ATTN_WIDTH, 2 * ATTN_WIDTH, 3 * ATTN_WIDTH], axis=-1)
        a = moba_attention(to_heads(q), to_heads(k), to_heads(v), dist_bias)
        c = conformer_conv(u, conv_w[l], conv_b[l], conv_ln_g[l], conv_ln_b[l])
        h = h + jnp.concatenate([a, c], axis=-1) @ w_out[l]
        h = h + cross_attention(rmsnorm(h, g_cross[l]), rmsnorm(mem, g_mem[l]),
                                wq_c[l], wk_c[l], wv_c[l], wo_c[l])
        nm = rmsnorm(h, g_mlp[l])
        h = h + jnp.square(jax.nn.relu(nm @ w1[l])) @ w2[l]
    return rmsnorm(h, g_final)
```

```python
import contextlib
import math
import numpy as np
import concourse.bass as bass
import concourse.mybir as mybir
from concourse.bass_utils import run_bass_kernel_spmd

F32 = mybir.dt.float32
BF16 = mybir.dt.bfloat16
ALU = mybir.AluOpType
AF = mybir.ActivationFunctionType
AX = mybir.AxisListType

D = 2048
DC = 16
NH = 8
NMEM = 256
KCONV = 31
EPS = 1e-6
NEG = -1000.0
SCALE = 128.0 ** -0.5
SQ = 128.0 ** 0.5
LFP = 2432
NSLAB = 50
RING = 4

COMPUTE = ("pe", "act", "dve", "pool")
NDMASEM = 14


class Buf:
    __slots__ = ("name", "lw", "rd")

    def __init__(self, name):
        self.name = name
        self.lw = None
        self.rd = []


class Ins:
    __slots__ = ("eng", "fn", "deps", "idx", "sig", "seq", "dsem", "dcnt", "waits", "src")

    def __init__(self, eng, fn):
        self.eng = eng
        self.fn = fn
        self.deps = []
        self.idx = -1
        self.sig = False
        self.seq = 0
        self.dsem = -1
        self.dcnt = 0
        self.waits = []
        self.src = None


class Prog:
    def __init__(self, same_engine_sync=True):
        self.streams = {e: [] for e in COMPUTE + ("sp",)}
        self.order = []
        self.same = same_engine_sync
        self.ndma = 0
        self.dma_last = [None] * NDMASEM
        self.dma_cnt = [0] * NDMASEM
        self.dry = False

    def _track(self, ins, reads, writes):
        deps = {}
        for b in reads:
            if b.lw is not None:
                deps[id(b.lw)] = b.lw
        for b in writes:
            if b.lw is not None:
                deps[id(b.lw)] = b.lw
            for r in b.rd:
                deps[id(r)] = r
        for b in reads:
            b.rd.append(ins)
        for b in writes:
            b.lw = ins
            b.rd = []
        ins.deps = list(deps.values())

    def op(self, eng, fn, reads=(), writes=()):
        if self.dry:
            return None
        ins = Ins(eng, fn)
        ins.src = eng
        self._track(ins, reads, writes)
        ins.idx = len(self.streams[eng])
        self.streams[eng].append(ins)
        self.order.append(ins)
        return ins

    def dma(self, fn, reads=(), writes=(), queue="sp"):
        if self.dry:
            return None
        ins = Ins(queue, fn)
        k = self.ndma % NDMASEM
        self.ndma += 1
        ins.dsem = k
        self.dma_cnt[k] += 1
        ins.dcnt = self.dma_cnt[k]
        ins.src = "d%d" % k
        self._track(ins, reads, writes)
        if self.dma_last[k] is not None:
            ins.deps.append(self.dma_last[k])
        self.dma_last[k] = ins
        ins.idx = len(self.streams[queue])
        self.streams[queue].append(ins)
        self.order.append(ins)
        return ins

    def analyze(self):
        vdone = {}
        last_start = {e: {} for e in self.streams}
        for ins in self.order:
            vc = dict(last_start[ins.eng])
            for d in ins.deps:
                if d.dsem >= 0:
                    src, val = d.src, d.dcnt
                else:
                    src, val = d.eng, d.idx + 1
                    if d.eng == ins.eng and ins.dsem < 0 and (d.eng == "pe" or not self.same):
                        continue
                if vc.get(src, 0) >= val:
                    continue
                d.sig = True
                ins.waits.append(d)
                for s, v in vdone[id(d)].items():
                    if vc.get(s, 0) < v:
                        vc[s] = v
            last_start[ins.eng] = vc
            vd = dict(vc)
            if ins.dsem >= 0:
                vd[ins.src] = ins.dcnt
            else:
                vd[ins.eng] = ins.idx + 1
            vdone[id(ins)] = vd
            ins.deps = None
        for ins in self.order:
            if len(ins.waits) > 1:
                best = {}
                for d in ins.waits:
                    val = d.dcnt if d.dsem >= 0 else d.idx
                    if d.src not in best or val > best[d.src][0]:
                        best[d.src] = (val, d)
                ins.waits = [v[1] for v in best.values()]
        for e in COMPUTE:
            n = 0
            for ins in self.streams[e]:
                if ins.dsem < 0 and ins.sig:
                    n += 1
                    ins.seq = n

    def emit(self, nc, final_wait_queue="sp"):
        self.analyze()
        with contextlib.ExitStack() as es:
            sems = {e: es.enter_context(nc.semaphore("s_" + e)) for e in COMPUTE}
            dsems = [es.enter_context(nc.semaphore("dq%d" % k)) for k in range(NDMASEM)]
            block = es.enter_context(nc.Block())

            def run(engname, eng):
                for ins in self.streams[engname]:
                    for d in ins.waits:
                        if d.dsem >= 0:
                            eng.wait_ge(dsems[d.dsem], 16 * d.dcnt)
                        else:
                            eng.wait_ge(sems[d.eng], d.seq)
                    r = ins.fn(eng)
                    if ins.dsem >= 0:
                        r.then_inc(dsems[ins.dsem], 16)
                    elif ins.sig:
                        r.then_inc(sems[ins.eng], 1)
                if engname == final_wait_queue:
                    for k in range(NDMASEM):
                        if self.dma_cnt[k]:
                            eng.wait_ge(dsems[k], 16 * self.dma_cnt[k])

            @block.tensor
            def _(e):
                run("pe", e)

            @block.scalar
            def _(e):
                run("act", e)

            @block.vector
            def _(e):
                run("dve", e)

            @block.gpsimd
            def _(e):
                run("pool", e)

            @block.sync
            def _(e):
                run("sp", e)


def build_program(NBLK, debug=False):
    S = NBLK * 256
    NG = NBLK // 8
    NT = NBLK // 2
    NAG = NBLK // 2
    nc = bass.Bass("TRN2", target_bir_lowering=False)
    dt_in = lambda name, shape: nc.dram_tensor(name, shape, F32, kind="ExternalInput").ap()
    xo_d = dt_in("xo", [NT, 128, D])
    xh_d = dt_in("xh", [NG * 2, 128, D])
    xa_d = dt_in("xa", [S // 128, 128, D])
    mem_d = dt_in("mem", [2, 128, D])
    wall_d = dt_in("wall", [NSLAB, 128, 8192])
    gv_d = dt_in("gvec", [128, 64])
    gfin_d = dt_in("gfin", [128, D])
    cw_d = dt_in("convw", [128, 8 * KCONV])
    cp_d = dt_in("convp", [128, 24])
    rel_d = dt_in("relaug", [33, 8])
    c31_d = dt_in("c31", [128, 8])
    oh_d = dt_in("oh", [33, LFP])
    out_d = nc.dram_tensor("out", [NT, 128, D], F32, kind="ExternalOutput").ap()
    wbf_d = nc.dram_tensor("wbf", [NSLAB, 128, 8192], BF16, kind="Internal").ap()
    kt_d = nc.dram_tensor("ktd", [NH, 128, S], BF16, kind="Internal").ap()
    v_d = nc.dram_tensor("vd", [NH, 128, S // 128, 128], BF16, kind="Internal").ap()
    fp_h = nc.dram_tensor("fpd", [NH, LFP], BF16, kind="Internal")
    fp_d = fp_h.ap()

    P = Prog()
    es = contextlib.ExitStack()
    with es:
        def SB(name, shape, dt):
            return es.enter_context(nc.sbuf_tensor("sb_" + name, shape, dt))

        def PS(name, shape, dt):
            return es.enter_context(nc.psum_tensor("ps_" + name, shape, dt))

        H = [SB("H%d" % i, [128, D], F32) for i in range(4)]
        bH = [Buf("H%d" % i) for i in range(4)]
        ring = [SB("ring%d" % i, [128, 8192], BF16) for i in range(RING)]
        bring = [Buf("ring%d" % i) for i in range(RING)]
        nT2 = SB("nT", [128, DC * 512], BF16)
        nT = nT2[:, :].rearrange("p (c t) -> p c t", c=DC)
        bnT = Buf("nT")
        QT2 = SB("QT", [128, NH * 512], BF16)
        QT = QT2[:, :].rearrange("p (h t) -> p h t", h=NH)
        bQT = [Buf("QT%d" % h) for h in range(NH)]
        CT2 = SB("CT", [128, 8 * 512], BF16)
        CT = CT2[:, :].rearrange("p (c t) -> p c t", c=8)
        nTh = CT2[:, :].rearrange("p (c t) -> p c t", c=DC)
        bCT = [Buf("CT%d" % c) for c in range(8)]
        GA = SB("GA", [128, 6144], F32)
        YA = SB("YA", [128, 4096], F32)
        bGA = [Buf("GA%d" % c) for c in range(8)]
        bYA = [Buf("YA%d" % c) for c in range(8)]
        xs = SB("xs", [128, D], BF16)
        bxs = Buf("xs")
        ident = SB("ident", [128, 128], BF16)
        onesf = SB("onesf", [128, 128], F32)
        gvec = SB("gvec", [128, 64], F32)
        cw = SB("cw", [128, 8 * KCONV], F32)
        cp = SB("cp", [128, 24], F32)
        c31m = SB("c31m", [128, 8], F32)
        relaug = SB("relaug", [33, 8], F32)
        ohs = YA[0:33, 0:LFP]
        fps = GA[:, :].bitcast(BF16)[0:8, 0:LFP]
        ksum = YA[:, 0:NH * NBLK].rearrange("p (h k) -> p h k", h=NH)
        kb32 = YA[:, 512:512 + NH * NBLK].rearrange("p (h k) -> p h k", h=NH)
        khi = SB("khi", [128, NH, NBLK], BF16)
        klo = SB("klo", [128, NH, NBLK], BF16)
        kcT = SB("kcT", [128, 4, 256], BF16)
        vcs = SB("vcs", [128, 2, 512], BF16)
        TT = SB("TT", [128, NH, 136], F32)
        st = SB("st", [128, 16], F32)
        sgt = SB("sgt", [128, 1024], F32)
        mean_sb = sgt[:, 0:512]
        rstd_sb = sgt[:, 512:1024]
        gsb = [SB("gsb%d" % i, [128, 64], F32) for i in range(2)]
        m30 = [SB("m30%d" % i, [128, 64], F32) for i in range(2)]
        mx8 = [SB("mx8%d" % i, [128, 8], F32) for i in range(2)]
        rs = [SB("rs%d" % i, [128, 64], F32) for i in range(4)]
        Oacc = [SB("Oacc%d" % i, [128, 128], F32) for i in range(4)]
        Pb = [xs[:, i * 512:(i + 1) * 512] for i in range(2)]
        PTs = [xs[:, 1024 + i * 512:1024 + (i + 1) * 512] for i in range(2)]
        rtmp = [sgt[:, i * 512:(i + 1) * 512] for i in range(2)]
        zcol = SB("zcol", [128, 1], F32)
        epsc = SB("epsc", [128, 1], F32)
        B = {n: Buf(n) for n in ("ident onesf gvec cw cp c31m relaug khi klo kcT vcs TT st sgt zcol fpd").split()}
        B["mean_sb"] = B["sgt"]
        B["rstd_sb"] = B["sgt"]
        B["ohs"] = bYA[0]
        B["fps"] = bGA[0]
        B["ksum"] = bYA[0]
        B["kb32"] = bYA[0]
        brtmp = [B["sgt"], B["sgt"]]
        bgsb = [Buf("gsb%d" % i) for i in range(2)]
        bm30 = [Buf("m30%d" % i) for i in range(2)]
        bmx8 = [Buf("mx8%d" % i) for i in range(2)]
        brs = [Buf("rs%d" % i) for i in range(4)]
        bOacc = [Buf("Oacc%d" % i) for i in range(4)]
        bPb = [Buf("Pb%d" % i) for i in range(2)]
        bPTs = [Buf("PTs%d" % i) for i in range(2)]
        bxs_all = [bxs] + bPb + bPTs
        bctmp = [Buf("ctmp0"), Buf("ctmp1")]
        bHK = [Buf("HK%d" % h) for h in range(2)]
        bbm = [[Buf("bm%d_%d" % (h, t)) for t in range(4)] for h in range(NH)]
        bA = [Buf("A%d" % t) for t in range(4)]
        bwd = [Buf("wd%d" % s) for s in range(NSLAB)]
        bkv = [Buf("kv%d" % g) for g in range(NAG)]
        G4 = GA[:, :].rearrange("p (c q t) -> p c q t", c=8, q=8)
        Y3 = YA[:, :].rearrange("p (c t) -> p c t", c=8)
        Y4 = YA[:, :].rearrange("p (c q t) -> p c q t", c=8, q=8)
        GAb = GA[:, :].bitcast(BF16)
        YAb = YA[:, :].bitcast(BF16)
        HK = GAb[:, 0:2 * 8 * 256].rearrange("p (h s k) -> p h s k", h=2, s=8)
        bmv = YA[:, 0:2048].rearrange("p (h t k) -> p h t k", h=NH, t=4)
        Av = YAb[:, 4096:8192].rearrange("p (t f) -> p t f", t=4)
        h1a = YAb[:, :].rearrange("p (c t) -> p c t", c=16)
        h1b = GAb[:, 0:8192].rearrange("p (c t) -> p c t", c=16)
        gfin_v = GA[:, 4096:6144]
        mm = [PS("mm%d" % i, [128, 512], F32) for i in range(3)]
        bmm = [Buf("mm%d" % i) for i in range(3)]
        tr = [PS("tr%d" % i, [128, 1024], BF16) for i in range(2)]
        btr = [Buf("tr%d" % i) for i in range(2)]
        sp_ = [PS("sps%d" % i, [128, 512], F32) for i in range(2)]
        bsp = [Buf("sps%d" % i) for i in range(2)]
        misc = PS("misc", [128, 512], F32)
        bmisc = [Buf("misc%d" % i) for i in range(4)]

        cnt = {"mm": 0, "tr": 0, "sp": 0, "ev": 0, "misc": 0, "pb": 0, "pt": 0, "g": 0, "rt": 0}

        def rot(key, n):
            v = cnt[key] % n
            cnt[key] += 1
            return v

        def MM(out, lhsT, rhs, start, stop, reads, writes):
            P.op("pe", lambda e, o=out, l=lhsT, r=rhs, s=start, t=stop: e.matmul(o, lhsT=l, rhs=r, start=s, stop=t),
                 reads, writes)

        def TR(out, in_, idn, reads, writes):
            P.op("pe", lambda e, o=out, i=in_, d=idn: e.transpose(out=o, in_=i, identity=d), reads, writes)

        def ACTV(out, in_, func, reads, writes, bias=None, scale=1.0, accum=None):
            def f(e, o=out, i=in_, fn=func, b=bias, s=scale, a=accum):
                kw = {}
                if b is not None:
                    kw["bias"] = b
                if a is not None:
                    kw["accum_out"] = a
                return e.activation(out=o, in_=i, func=fn, scale=s, **kw)
            P.op("act", f, reads, writes)

        def CPY(eng, out, in_, reads, writes):
            if eng == "act":
                P.op("act", lambda e, o=out, i=in_: e.copy(out=o, in_=i), reads, writes)
            else:
                P.op(eng, lambda e, o=out, i=in_: e.tensor_copy(out=o, in_=i), reads, writes)

        def TS(eng, out, in0, s1, s2, op0, op1, reads, writes):
            if op1 is None:
                P.op(eng, lambda e, o=out, i=in0, a=s1, p0=op0: e.tensor_scalar(out=o, in0=i, scalar1=a, scalar2=None, op0=p0),
                     reads, writes)
            else:
                P.op(eng, lambda e, o=out, i=in0, a=s1, b=s2, p0=op0, p1=op1:
                     e.tensor_scalar(out=o, in0=i, scalar1=a, scalar2=b, op0=p0, op1=p1), reads, writes)

        def TTO(eng, out, in0, in1, op, reads, writes):
            P.op(eng, lambda e, o=out, a=in0, b=in1, p=op: e.tensor_tensor(out=o, in0=a, in1=b, op=p), reads, writes)

        def STT(eng, out, in0, sc, in1, op0, op1, reads, writes):
            P.op(eng, lambda e, o=out, a=in0, s=sc, b=in1, p0=op0, p1=op1:
                 e.scalar_tensor_tensor(out=o, in0=a, scalar=s, in1=b, op0=p0, op1=p1), reads, writes)

        def MSET(eng, out, val, writes):
            P.op(eng, lambda e, o=out, v=val: e.memset(o, v), (), writes)

        def DMA(out, in_, reads, writes, queue="sp"):
            P.dma(lambda e, o=out, i=in_: e.dma_start(out=o, in_=i), reads, writes, queue)

        def evac_eng():
            return ("act", "dve")[rot("ev", 2)]

        plan = []
        stream_state = {"next_issue": 0, "pos": 0}

        def issue_item(i):
            kind, args = plan[i]
            rb = i % RING
            dst = ring[rb]
            if kind == "slab":
                s = args
                DMA(dst[:, :], wbf_d[s, :, :], [bwd[s]], [bring[rb]])
            else:
                h, b0, nb = args
                DMA(dst[:, 0:nb * 256], kt_d[h, :, b0 * 256:(b0 + nb) * 256],
                    [bkv[g] for g in range(b0 // 2, (b0 + nb + 1) // 2)], [bring[rb]])
                DMA(dst[:, 4096:4096 + nb * 256].rearrange("p (c d) -> p c d", d=128),
                    v_d[h, :, b0 * 2:(b0 + nb) * 2, :],
                    [bkv[g] for g in range(b0 // 2, (b0 + nb + 1) // 2)], [bring[rb]])

        def fetch(kind, args):
            if P.dry:
                plan.append((kind, args))
                return 0
            i = stream_state["pos"]
            stream_state["pos"] += 1
            assert plan[i] == (kind, args), (plan[i], kind, args)
            while stream_state["next_issue"] < min(len(plan), i + RING - 1) or stream_state["next_issue"] <= i:
                issue_item(stream_state["next_issue"])
                stream_state["next_issue"] += 1
            return i % RING

        def norm_transpose(X, bX, npart, tokoff):
            MSET("pool", st[0:npart, 0:1], 0.0, [B["st"]])
            ACTV(xs[0:npart, :], X[0:npart, :], AF.Square, [bX, B["st"]], bxs_all + [B["st"]], accum=st[0:npart, 0:1])
            ACTV(st[0:npart, 1:2], st[0:npart, 0:1], AF.Sqrt, [B["st"], B["zcol"]], [B["st"]], bias=epsc[0:npart, 0:1], scale=1.0 / D)
            P.op("dve", lambda e, o=st[0:npart, 1:2], i=st[0:npart, 1:2]: e.reciprocal(out=o, in_=i), [B["st"]], [B["st"]])
            TS("dve", xs[0:npart, :], X[0:npart, :], st[0:npart, 1:2], None, ALU.mult, None, [bX, B["st"]], bxs_all)
            for half in range(2):
                tb = rot("tr", 2)
                for c in range(8):
                    dc = half * 8 + c
                    TR(tr[tb][:, c * 128:c * 128 + npart], xs[0:npart, dc * 128:(dc + 1) * 128], ident[0:npart, 0:npart],
                       bxs_all + [B["ident"]], [btr[tb]])
                src = tr[tb][:, :].rearrange("p (c t) -> p c t", c=8)[:, :, 0:npart]
                CPY(evac_eng(), nT[:, half * 8:(half + 1) * 8, tokoff:tokoff + npart], src, [btr[tb]], [bnT])

        def emit_all():
            MSET("pool", ident[:, :], 1.0, [B["ident"]])
            P.op("pool", lambda e: e.affine_select(out=ident[:, :], in_=ident[:, :], pattern=[[-1, 128]],
                                                   compare_op=ALU.is_equal, fill=0.0, base=0, channel_multiplier=1),
                 [B["ident"]], [B["ident"]])
            MSET("pool", onesf[:, :], 1.0 / 1024.0, [B["onesf"]])
            MSET("pool", zcol[:, :], 0.0, [B["zcol"]])
            MSET("pool", epsc[:, :], EPS, [B["zcol"]])
            DMA(gvec[:, :], gv_d[:, :], [], [B["gvec"]])
            DMA(cw[:, :], cw_d[:, :], [], [B["cw"]])
            DMA(cp[:, :], cp_d[:, :], [], [B["cp"]])
            DMA(c31m[:, :], c31_d[:, :], [], [B["c31m"]])
            DMA(relaug[:, :], rel_d[:, :], [], [B["relaug"]])
            DMA(ohs[:, :], oh_d[:, :], [], [B["ohs"]])
            for c0 in range(0, LFP, 512):
                w = min(512, LFP - c0)
                mb = rot("mm", 3)
                MM(mm[mb][0:8, 0:w], relaug[:, :], ohs[:, c0:c0 + w], True, True, [B["relaug"], B["ohs"]], [bmm[mb]])
                ACTV(fps[:, c0:c0 + w], mm[mb][0:8, 0:w], AF.Copy, [bmm[mb]], [B["fps"]], scale=SQ)
            DMA(fp_d[:, :], fps[:, :], [B["fps"]], [B["fpd"]], queue="pool")
            MSET("pool", TT[:, :, :], 0.0, [B["TT"]])
            for h in range(NH):
                TS("dve", TT[:, h, :], TT[:, h, :], c31m[:, h:h + 1], None, ALU.add, None,
                   [B["TT"], B["c31m"]], [B["TT"]])
            MSET("pool", TT[0:64, :, 58:64], 0.0, [B["TT"]])
            MSET("pool", TT[0:64, :, 64:65], -NEG, [B["TT"]])
            MSET("pool", TT[0:64, :, 65:136], NEG, [B["TT"]])
            MSET("pool", TT[64:128, :, 59:65], 0.0, [B["TT"]])
            MSET("pool", TT[64:128, :, 65:66], -NEG, [B["TT"]])
            MSET("pool", TT[64:128, :, 66:136], NEG, [B["TT"]])

            gcol = {}
            for s in range(0, 10):
                gcol[s] = 0
            gcol[14] = 16
            gcol[15] = 32
            gcol[16] = 32
            for s in range(18, 34):
                gcol[s] = 48
            stg = [GA[:, 0:4096], YA[:, 0:4096]]
            bstg = [bGA[0], bYA[0]]
            k = 0
            for s in [2, 3, 4, 5] + [s for s in range(NSLAB) if s not in (2, 3, 4, 5)]:
                for hs in range(2):
                    sb_i = k % 2
                    ob_i = 2 + (k % 2)
                    k += 1
                    DMA(stg[sb_i], wall_d[s, :, hs * 4096:(hs + 1) * 4096], [], [bstg[sb_i]])
                    if s in gcol:
                        for dl in range(8):
                            eng = ("dve", "pool", "act")[(k + dl) % 3]
                            gc = gvec[:, gcol[s] + hs * 8 + dl:gcol[s] + hs * 8 + dl + 1]
                            o = ring[ob_i][:, dl * 512:(dl + 1) * 512]
                            i_ = stg[sb_i][:, dl * 512:(dl + 1) * 512]
                            if eng == "act":
                                ACTV(o, i_, AF.Copy, [bstg[sb_i], B["gvec"]], [bring[ob_i]], scale=gc)
                            else:
                                TS(eng, o, i_, gc, None, ALU.mult, None, [bstg[sb_i], B["gvec"]], [bring[ob_i]])
                    else:
                        for q in range(2):
                            eng = ("dve", "pool")[q]
                            CPY(eng, ring[ob_i][:, q * 2048:(q + 1) * 2048], stg[sb_i][:, q * 2048:(q + 1) * 2048],
                                [bstg[sb_i]], [bring[ob_i]])
                    DMA(wbf_d[s, :, hs * 4096:(hs + 1) * 4096], ring[ob_i][:, 0:4096], [bring[ob_i]], [bwd[s]], queue="pool")

            MSET("pool", ksum[:, :, :], 0.0, [B["ksum"]])
            for s_i, s in enumerate((2, 3, 4, 5)):
                DMA(ring[s_i][:, :], wbf_d[s, :, :], [bwd[s]], [bring[s_i]])
            Wk = [ring[0][:, :].rearrange("p (c f) -> p c f", c=16), ring[1][:, :].rearrange("p (c f) -> p c f", c=16)]
            Wv = [ring[2][:, :].rearrange("p (c f) -> p c f", c=16), ring[3][:, :].rearrange("p (c f) -> p c f", c=16)]
            ntile_a = S // 128
            for i in range(min(3, ntile_a)):
                DMA(H[i % 4][:, :], xa_d[i, :, :], [], [bH[i % 4]])
            for ag in range(NAG):
                for t in range(4):
                    ti = ag * 4 + t
                    if ti + 3 < ntile_a:
                        DMA(H[(ti + 3) % 4][:, :], xa_d[ti + 3, :, :], [], [bH[(ti + 3) % 4]])
                    norm_transpose(H[ti % 4], bH[ti % 4], 128, t * 128)
                for h in range(NH):
                    mb = rot("mm", 3)
                    for dc in range(DC):
                        MM(mm[mb][:, :], Wk[h // 4][:, dc, (h % 4) * 128:(h % 4 + 1) * 128], nT[:, dc, 0:512],
                           dc == 0, dc == DC - 1, [bring[h // 4], bnT], [bmm[mb]])
                    for bl in range(2):
                        ACTV(QT[:, h, bl * 256:(bl + 1) * 256], mm[mb][:, bl * 256:(bl + 1) * 256], AF.Copy,
                             [bmm[mb], B["ksum"]], [bQT[h], B["ksum"]],
                             accum=ksum[:, h, ag * 2 + bl:ag * 2 + bl + 1])
                for t in range(4):
                    for sl in range(2):
                        mb = rot("mm", 3)
                        for dc in range(DC):
                            MM(mm[mb][:, :], nT[:, dc, t * 128:(t + 1) * 128], Wv[sl][:, dc, :],
                               dc == 0, dc == DC - 1, [bring[2 + sl], bnT], [bmm[mb]])
                        CPY("dve", CT[:, t * 2 + sl, :], mm[mb][:, :], [bmm[mb]], [bCT[t * 2 + sl]])
                DMA(kt_d[:, :, ag * 512:(ag + 1) * 512].rearrange("h p t -> p h t"), QT[:, :, :],
                    bQT, [bkv[ag]], queue="pool")
                for t in range(4):
                    DMA(v_d[:, :, ag * 4 + t, :].rearrange("h p d -> p h d"),
                        CT2[:, t * 1024:(t + 1) * 1024].rearrange("p (h d) -> p h d", d=128),
                        bCT[2 * t:2 * t + 2], [bkv[ag]], queue="pool")
            TS("dve", kb32[:, :, :], ksum[:, :, :], 1.0 / 256.0, None, ALU.mult, None, [B["ksum"]], [B["kb32"]])
            CPY("dve", khi[:, :, :], kb32[:, :, :], [B["kb32"]], [B["khi"]])
            CPY("dve", ksum[:, :, :], khi[:, :, :], [B["khi"]], [B["ksum"]])
            TTO("dve", klo[:, :, :], kb32[:, :, :], ksum[:, :, :], ALU.subtract, [B["kb32"], B["ksum"]], [B["klo"]])

            for i in range(2):
                DMA(H[i][:, :], mem_d[i, :, :], [], [bH[i]])
                norm_transpose(H[i], bH[i], 128, i * 128)
            rk = fetch("slab", 15)
            wkc = ring[rk][:, :].rearrange("p (c f) -> p c f", c=16)
            for h4 in range(4):
                mb = rot("mm", 3)
                for dc in range(DC):
                    MM(mm[mb][:, 0:256], wkc[:, dc, h4 * 128:(h4 + 1) * 128], nT[:, dc, 0:256], dc == 0, dc == DC - 1,
                       [bring[rk], bnT], [bmm[mb]])
                CPY(evac_eng(), kcT[:, h4, :], mm[mb][:, 0:256], [bmm[mb]], [B["kcT"]])
            rv = fetch("slab", 16)
            wvc = ring[rv][:, :].rearrange("p (c f) -> p c f", c=16)
            for i in range(2):
                mb = rot("mm", 3)
                for dc in range(DC):
                    MM(mm[mb][:, :], nT[:, dc, i * 128:(i + 1) * 128], wvc[:, dc, :], dc == 0, dc == DC - 1,
                       [bring[rv], bnT], [bmm[mb]])
                CPY(evac_eng(), vcs[:, i, :], mm[mb][:, :], [bmm[mb]], [B["vcs"]])

            for m in range(NG):
                emit_group(m)

        def emit_group(m):
            allGA = bGA
            allYA = bYA
            for t in range(4):
                DMA(H[t][:, :], xo_d[m * 4 + t, :, :], [], [bH[t]])
            for t in range(4):
                norm_transpose(H[t], bH[t], 128, t * 128)
            for i in range(2):
                hx = YA[:, i * 2048:(i + 1) * 2048]
                DMA(hx, xh_d[m * 2 + i, :, :], [], [bYA[i * 4 + j] for j in range(4)])
                norm_transpose_from(hx, [bYA[i * 4 + j] for j in range(4)], i * 128)
            slab_of = {}
            for cc in range(8):
                if cc % 4 == 0:
                    slab_of["v"] = fetch("slab", 6 + cc // 4)
                    slab_of["g"] = fetch("slab", 8 + cc // 4)
                wv_ = ring[slab_of["v"]][:, :].rearrange("p (c f) -> p c f", c=16)
                wg_ = ring[slab_of["g"]][:, :].rearrange("p (c f) -> p c f", c=16)
                fo = (cc % 4) * 128
                mv = rot("mm", 3)
                mg = rot("mm", 3)
                for dc in range(DC):
                    MM(mm[mv][:, :], wv_[:, dc, fo:fo + 128], nT[:, dc, 0:512], dc == 0, dc == DC - 1,
                       [bring[slab_of["v"]], bnT], [bmm[mv]])
                    MM(misc[:, 0:256], wv_[:, dc, fo:fo + 128], nTh[:, dc, :], dc == 0, dc == DC - 1,
                       [bring[slab_of["v"]]] + bCT, [bmisc[0], bmisc[1]])
                for dc in range(DC):
                    MM(mm[mg][:, :], wg_[:, dc, fo:fo + 128], nT[:, dc, 0:512], dc == 0, dc == DC - 1,
                       [bring[slab_of["g"]], bnT], [bmm[mg]])
                    MM(misc[:, 256:512], wg_[:, dc, fo:fo + 128], nTh[:, dc, :], dc == 0, dc == DC - 1,
                       [bring[slab_of["g"]]] + bCT, [bmisc[2], bmisc[3]])
                ACTV(sgt[:, 0:512], mm[mg][:, :], AF.Sigmoid, [bmm[mg]], [B["sgt"]])
                ACTV(sgt[:, 512:768], misc[:, 256:512], AF.Sigmoid, [bmisc[2], bmisc[3]], [B["sgt"]])
                TTO("dve", G4[:, cc, :, 32:96], mm[mv][:, :].rearrange("p (q t) -> p q t", q=8),
                    sgt[:, 0:512].rearrange("p (q t) -> p q t", q=8), ALU.mult, [bmm[mv], B["sgt"]], [bGA[cc]])
                TTO("dve", G4[:, cc, :, 0:32], misc[:, 0:256].rearrange("p (q t) -> p q t", q=8),
                    sgt[:, 512:768].rearrange("p (q t) -> p q t", q=8), ALU.mult,
                    [bmisc[0], bmisc[1], B["sgt"]], [bGA[cc]])
            for k in range(KCONV):
                for cc in range(8):
                    eng = "dve" if cc < 5 else "pool"
                    src = G4[:, cc, :, k + 2:k + 66]
                    wk_ = cw[:, cc * KCONV + k:cc * KCONV + k + 1]
                    if k == 0:
                        TS(eng, Y4[:, cc, :, :], src, wk_, cp[:, cc:cc + 1], ALU.mult, ALU.add,
                           [bGA[cc], B["cw"], B["cp"]], [bYA[cc]])
                    elif eng == "dve":
                        STT(eng, Y4[:, cc, :, :], src, wk_, Y4[:, cc, :, :], ALU.mult, ALU.add,
                            [bGA[cc], B["cw"], bYA[cc]], [bYA[cc]])
                    else:
                        tmpc = sgt[:, (cc % 2) * 512:(cc % 2 + 1) * 512].rearrange("p (q t) -> p q t", q=8)
                        TS("pool", tmpc, src, wk_, None, ALU.mult, None, [bGA[cc], B["cw"]], [bctmp[cc % 2]])
                        TTO("pool", Y4[:, cc, :, :], Y4[:, cc, :, :], tmpc, ALU.add, [bctmp[cc % 2], bYA[cc]], [bYA[cc]])
            for h in range(NH):
                if h % 4 == 0:
                    rq = fetch("slab", h // 4)
                    wq_ = ring[rq][:, :].rearrange("p (c f) -> p c f", c=16)
                mb = rot("mm", 3)
                for dc in range(DC):
                    MM(mm[mb][:, :], wq_[:, dc, (h % 4) * 128:(h % 4 + 1) * 128], nT[:, dc, 0:512], dc == 0, dc == DC - 1,
                       [bring[rq], bnT], [bmm[mb]])
                CPY(evac_eng(), QT[:, h, :], mm[mb][:, :], [bmm[mb]], [bQT[h]])
            GS = GA[:, 0:4096].rearrange("p (c t) -> p c t", c=8)
            for cc in range(8):
                TTO("pool", GS[:, cc, :], Y3[:, cc, :], Y3[:, cc, :], ALU.mult, [bYA[cc]] + allGA, allGA)
            m1 = rot("mm", 3)
            m2 = rot("mm", 3)
            for cc in range(8):
                MM(mm[m1][:, :], onesf[:, :], Y3[:, cc, :], cc == 0, cc == 7, [B["onesf"], bYA[cc]], [bmm[m1]])
            for cc in range(8):
                MM(mm[m2][:, :], onesf[:, :], GS[:, cc, :], cc == 0, cc == 7, [B["onesf"]] + allGA, [bmm[m2]])
            CPY("dve", mean_sb[:, :], mm[m1][:, :], [bmm[m1]], [B["mean_sb"]])
            TTO("dve", rstd_sb[:, :], mean_sb[:, :], mean_sb[:, :], ALU.mult, [B["mean_sb"]], [B["rstd_sb"]])
            TTO("dve", rstd_sb[:, :], mm[m2][:, :], rstd_sb[:, :], ALU.subtract, [bmm[m2], B["rstd_sb"]], [B["rstd_sb"]])
            ACTV(rstd_sb[:, :], rstd_sb[:, :], AF.Sqrt, [B["rstd_sb"], B["zcol"]], [B["rstd_sb"]], bias=epsc[:, 0:1])
            P.op("dve", lambda e, o=rstd_sb[:, :], i=rstd_sb[:, :]: e.reciprocal(out=o, in_=i), [B["rstd_sb"]], [B["rstd_sb"]])
            for cc in range(8):
                eng = "pool" if cc % 2 == 0 else "dve"
                TTO(eng, Y3[:, cc, :], Y3[:, cc, :], mean_sb[:, :], ALU.subtract, [bYA[cc], B["mean_sb"]], [bYA[cc]])
                TTO(eng, Y3[:, cc, :], Y3[:, cc, :], rstd_sb[:, :], ALU.mult, [bYA[cc], B["rstd_sb"]], [bYA[cc]])
                ACTV(CT[:, cc, :], Y3[:, cc, :], AF.Silu, [bYA[cc], B["cp"]], [bCT[cc]],
                     bias=cp[:, 16 + cc:17 + cc], scale=cp[:, 8 + cc:9 + cc])
            for h in range(NH):
                for t in range(4):
                    tau = m * 4 + t
                    ncol = 2 * tau + 2
                    gi = rot("g", 2)
                    rg = rot("misc", 4)
                    MSET("pool", gsb[gi][:, :], -1e30, [bgsb[gi]])
                    gp = misc[:, rg * 128:rg * 128 + 64]
                    MM(gp[:, 0:ncol], QT[:, h, t * 128:(t + 1) * 128], khi[:, h, 0:ncol], True, False,
                       [bQT[h], B["khi"]], [bmisc[rg]])
                    MM(gp[:, 0:ncol], QT[:, h, t * 128:(t + 1) * 128], klo[:, h, 0:ncol], False, True,
                       [bQT[h], B["klo"]], [bmisc[rg]])
                    if tau > 0:
                        CPY("dve", gsb[gi][0:64, 0:2 * tau], gp[0:64, 0:2 * tau], [bmisc[rg]], [bgsb[gi]])
                    CPY("dve", gsb[gi][64:128, 0:2 * tau + 1], gp[64:128, 0:2 * tau + 1], [bmisc[rg]], [bgsb[gi]])
                    P.op("dve", lambda e, o=mx8[gi][:, :], i=gsb[gi][:, 0:max(8, ncol)]: e.max(out=o, in_=i),
                         [bgsb[gi]], [bmx8[gi]])
                    TS("dve", mx8[gi][:, 2:3], mx8[gi][:, 2:3], -1e29, None, ALU.max, None, [bmx8[gi]], [bmx8[gi]])
                    TS("dve", m30[gi][:, 0:ncol], gsb[gi][:, 0:ncol], mx8[gi][:, 2:3], 1e30, ALU.subtract, ALU.mult,
                       [bgsb[gi], bmx8[gi]], [bm30[gi]])
                    TS("dve", m30[gi][:, 0:ncol], m30[gi][:, 0:ncol], -1.0, 0.0, ALU.max, ALU.min, [bm30[gi]], [bm30[gi]])
                    STT("dve", bmv[:, h, t, 0:ncol], m30[gi][:, 0:ncol], -NEG, TT[:, h, 64 - 2 * tau:64 - 2 * tau + ncol],
                        ALU.mult, ALU.add, [bm30[gi], B["TT"]] + allYA[0:4], [bbm[h][t]] + allYA[0:4])
            nblk_g = 8 * m + 8
            for h in range(NH):
                hb = h % 2
                DMA(HK[0:64, hb, :, :], bass.AP(fp_h, h * LFP, [[1, 64], [256, 8], [1, 256]]),
                    [B["fpd"]], [bHK[hb]] + allGA[0:3])
                DMA(HK[64:128, hb, :, :], bass.AP(fp_h, h * LFP + 256, [[1, 64], [256, 8], [1, 256]]),
                    [B["fpd"]], [bHK[hb]] + allGA[0:3])
                for kc in range((nblk_g + 15) // 16):
                    b0 = kc * 16
                    nb = min(16, nblk_g - b0)
                    rb = fetch("kv", (h, b0, nb))
                    KTc = ring[rb][:, 0:4096]
                    Vc = ring[rb][:, 4096:8192].rearrange("p (c d) -> p c d", d=128)
                    for t in range(4):
                        tau = m * 4 + t
                        nbt = 2 * tau + 2
                        blks = list(range(b0, min(b0 + nb, nbt)))
                        if not blks:
                            continue
                        if kc == 0:
                            MSET("pool", rs[t][:, :], 0.0, [brs[t]])
                        ob = rot("misc", 4)
                        Ops = misc[:, ob * 128:(ob + 1) * 128]
                        npair = len(blks) // 2
                        for pi in range(npair):
                            sb_ = rot("sp", 2)
                            for bi in range(2):
                                kb = blks[pi * 2 + bi]
                                sprime = 2 * tau + 1 - kb
                                lo = (kb - b0) * 256
                                so = bi * 256
                                if sprime <= 7:
                                    MM(sp_[sb_][:, so:so + 256], QT[:, h, t * 128:(t + 1) * 128], KTc[:, lo:lo + 256],
                                       True, False, [bQT[h], bring[rb]], [bsp[sb_]])
                                    MM(sp_[sb_][:, so:so + 256], ident[:, :], HK[:, hb, sprime, :], False, True,
                                       [B["ident"], bHK[hb]] + allGA[0:3], [bsp[sb_]])
                                elif bi == 0 and (2 * tau + 1 - blks[pi * 2 + 1]) > 7:
                                    MM(sp_[sb_][:, :], QT[:, h, t * 128:(t + 1) * 128], KTc[:, lo:lo + 512],
                                       True, True, [bQT[h], bring[rb]], [bsp[sb_]])
                                elif bi == 0 or (2 * tau + 1 - blks[pi * 2]) <= 7:
                                    MM(sp_[sb_][:, so:so + 256], QT[:, h, t * 128:(t + 1) * 128], KTc[:, lo:lo + 256],
                                       True, True, [bQT[h], bring[rb]], [bsp[sb_]])
                            pb = rot("pb", 2)
                            for bi in range(2):
                                kb = blks[pi * 2 + bi]
                                ACTV(Pb[pb][:, bi * 256:(bi + 1) * 256], sp_[sb_][:, bi * 256:(bi + 1) * 256], AF.Exp,
                                     [bsp[sb_], bbm[h][t], brs[t]] + allYA[0:4], [bPb[pb], brs[t]],
                                     bias=bmv[:, h, t, kb:kb + 1], scale=SCALE, accum=rs[t][:, kb:kb + 1])
                            tb = rot("tr", 2)
                            for q4 in range(4):
                                TR(tr[tb][:, q4 * 128:(q4 + 1) * 128], Pb[pb][:, q4 * 128:(q4 + 1) * 128], ident[:, :],
                                   [bPb[pb], B["ident"]], [btr[tb]])
                            pt = rot("pt", 2)
                            CPY("dve", PTs[pt][:, :], tr[tb][:, 0:512], [btr[tb]], [bPTs[pt]])
                            for q4 in range(4):
                                kb = blks[pi * 2 + q4 // 2]
                                vch = (kb - b0) * 2 + (q4 % 2)
                                MM(Ops, PTs[pt][:, q4 * 128:(q4 + 1) * 128], Vc[:, vch, :],
                                   pi == 0 and q4 == 0, pi == npair - 1 and q4 == 3,
                                   [bPTs[pt], bring[rb]], [bmisc[ob]])
                        if kc == 0:
                            CPY("dve", Oacc[t][:, :], Ops, [bmisc[ob]], [bOacc[t]])
                        else:
                            TTO("dve", Oacc[t][:, :], Oacc[t][:, :], Ops, ALU.add, [bmisc[ob], bOacc[t]], [bOacc[t]])
                for t in range(4):
                    tau = m * 4 + t
                    P.op("dve", lambda e, o=st[:, 4 + t:5 + t], i=rs[t][:, 0:2 * tau + 2]:
                         e.tensor_reduce(out=o, in_=i, axis=AX.X, op=ALU.add), [brs[t]], [B["st"]])
                    P.op("dve", lambda e, o=st[:, 8 + t:9 + t], i=st[:, 4 + t:5 + t]: e.reciprocal(out=o, in_=i),
                         [B["st"]], [B["st"]])
                    TS("dve", Av[:, t, h * 128:(h + 1) * 128], Oacc[t][:, :], st[:, 8 + t:9 + t], None, ALU.mult, None,
                       [bOacc[t], B["st"]] + allYA[4:8], [bA[t]] + allYA[4:8])
            for h in range(NH):
                tb = rot("tr", 2)
                for t in range(4):
                    TR(tr[tb][:, t * 128:(t + 1) * 128], Av[:, t, h * 128:(h + 1) * 128], ident[:, :],
                       [bA[t], B["ident"]] + allYA[4:8], [btr[tb]])
                CPY(evac_eng(), QT[:, h, :], tr[tb][:, 0:512], [btr[tb]], [bQT[h]])
            for ds in range(4):
                rw = fetch("slab", 10 + ds)
                wo_ = ring[rw][:, :].rearrange("p (c f) -> p c f", c=16)
                for t in range(4):
                    mb = rot("mm", 3)
                    for ic in range(16):
                        lhs = QT[:, ic, t * 128:(t + 1) * 128] if ic < 8 else CT[:, ic - 8, t * 128:(t + 1) * 128]
                        MM(mm[mb][:, :], lhs, wo_[:, ic, :], ic == 0, ic == 15,
                           [bQT[ic] if ic < 8 else bCT[ic - 8], bring[rw]], [bmm[mb]])
                    TTO("dve", H[t][:, ds * 512:(ds + 1) * 512], H[t][:, ds * 512:(ds + 1) * 512], mm[mb][:, :], ALU.add,
                        [bH[t], bmm[mb]], [bH[t]])
            for t in range(4):
                norm_transpose(H[t], bH[t], 128, t * 128)
            rq = fetch("slab", 14)
            wqc = ring[rq][:, :].rearrange("p (c f) -> p c f", c=16)
            for h4 in range(4):
                mb = rot("mm", 3)
                for dc in range(DC):
                    MM(mm[mb][:, :], wqc[:, dc, h4 * 128:(h4 + 1) * 128], nT[:, dc, 0:512], dc == 0, dc == DC - 1,
                       [bring[rq], bnT], [bmm[mb]])
                CPY(evac_eng(), QT[:, h4, :], mm[mb][:, :], [bmm[mb]], [bQT[h4]])
            for t in range(4):
                MSET("pool", rs[t][:, 0:4], 0.0, [brs[t]])
                for hp in range(2):
                    sb_ = rot("sp", 2)
                    pb = rot("pb", 2)
                    for hh in range(2):
                        h4 = hp * 2 + hh
                        MM(sp_[sb_][:, hh * 256:(hh + 1) * 256], QT[:, h4, t * 128:(t + 1) * 128], kcT[:, h4, :], True, True,
                           [bQT[h4], B["kcT"]], [bsp[sb_]])
                    for hh in range(2):
                        h4 = hp * 2 + hh
                        ACTV(Pb[pb][:, hh * 256:(hh + 1) * 256], sp_[sb_][:, hh * 256:(hh + 1) * 256], AF.Exp,
                             [bsp[sb_], brs[t], B["zcol"]], [bPb[pb], brs[t]], bias=zcol[:, 0:1], scale=SCALE,
                             accum=rs[t][:, h4:h4 + 1])
                    tb = rot("tr", 2)
                    for q4 in range(4):
                        TR(tr[tb][:, q4 * 128:(q4 + 1) * 128], Pb[pb][:, q4 * 128:(q4 + 1) * 128], ident[:, :],
                           [bPb[pb], B["ident"]], [btr[tb]])
                    pt = rot("pt", 2)
                    CPY("dve", PTs[pt][:, :], tr[tb][:, 0:512], [btr[tb]], [bPTs[pt]])
                    for hh in range(2):
                        h4 = hp * 2 + hh
                        ob = rot("misc", 4)
                        Ops = misc[:, ob * 128:(ob + 1) * 128]
                        for mc in range(2):
                            MM(Ops, PTs[pt][:, (hh * 2 + mc) * 128:(hh * 2 + mc + 1) * 128], vcs[:, mc, h4 * 128:(h4 + 1) * 128],
                               mc == 0, mc == 1, [bPTs[pt], B["vcs"]], [bmisc[ob]])
                        P.op("dve", lambda e, o=st[:, 12:13], i=rs[t][:, h4:h4 + 1]: e.reciprocal(out=o, in_=i),
                             [brs[t]], [B["st"]])
                        TS("dve", Av[:, t, h4 * 128:(h4 + 1) * 128], Ops, st[:, 12:13], None, ALU.mult, None,
                           [bmisc[ob], B["st"]] + allYA[4:8], [bA[t]] + allYA[4:8])
            for h4 in range(4):
                tb = rot("tr", 2)
                for t in range(4):
                    TR(tr[tb][:, t * 128:(t + 1) * 128], Av[:, t, h4 * 128:(h4 + 1) * 128], ident[:, :],
                       [bA[t], B["ident"]] + allYA[4:8], [btr[tb]])
                CPY(evac_eng(), QT[:, 4 + h4, :], tr[tb][:, 0:512], [btr[tb]], [bQT[4 + h4]])
            rw = fetch("slab", 17)
            woc = ring[rw][:, :].rearrange("p (c f) -> p c f", c=4)
            for t in range(4):
                for ds in range(4):
                    mb = rot("mm", 3)
                    for ic in range(4):
                        MM(mm[mb][:, :], QT[:, 4 + ic, t * 128:(t + 1) * 128], woc[:, ic, ds * 512:(ds + 1) * 512],
                           ic == 0, ic == 3, [bQT[4 + ic], bring[rw]], [bmm[mb]])
                    TTO("dve", H[t][:, ds * 512:(ds + 1) * 512], H[t][:, ds * 512:(ds + 1) * 512], mm[mb][:, :], ALU.add,
                        [bH[t], bmm[mb]], [bH[t]])
            for t in range(4):
                norm_transpose(H[t], bH[t], 128, t * 128)
            for fh in range(2):
                for sl in range(8):
                    rw = fetch("slab", 18 + fh * 8 + sl)
                    w1_ = ring[rw][:, :].rearrange("p (c f) -> p c f", c=16)
                    for f4 in range(4):
                        fl = sl * 4 + f4
                        mb = rot("mm", 3)
                        for dc in range(DC):
                            MM(mm[mb][:, :], w1_[:, dc, f4 * 128:(f4 + 1) * 128], nT[:, dc, 0:512], dc == 0, dc == DC - 1,
                               [bring[rw], bnT], [bmm[mb]])
                        ri = rot("rt", 2)
                        ACTV(rtmp[ri][:, :], mm[mb][:, :], AF.Relu, [bmm[mb]], [brtmp[ri]])
                        dst = h1a[:, fl, :] if fl < 16 else h1b[:, fl - 16, :]
                        dbuf = allYA if fl < 16 else allGA
                        TTO("pool", dst, rtmp[ri][:, :], rtmp[ri][:, :], ALU.mult, [brtmp[ri]] + dbuf, dbuf)
                for ds in range(4):
                    r2 = [fetch("slab", 34 + ds * 4 + fh * 2 + q) for q in range(2)]
                    for t in range(4):
                        mb = rot("mm", 3)
                        for fl in range(32):
                            src = h1a[:, fl, t * 128:(t + 1) * 128] if fl < 16 else h1b[:, fl - 16, t * 128:(t + 1) * 128]
                            w2_ = ring[r2[fl // 16]][:, :].rearrange("p (c f) -> p c f", c=16)
                            MM(mm[mb][:, :], src, w2_[:, fl % 16, :], fl == 0, fl == 31,
                               (allYA if fl < 16 else allGA) + [bring[r2[fl // 16]]], [bmm[mb]])
                        TTO("dve", H[t][:, ds * 512:(ds + 1) * 512], H[t][:, ds * 512:(ds + 1) * 512], mm[mb][:, :], ALU.add,
                            [bH[t], bmm[mb]], [bH[t]])
            DMA(gfin_v, gfin_d[:, :], allGA, allGA)
            for t in range(4):
                MSET("pool", st[:, 0:1], 0.0, [B["st"]])
                ACTV(xs[:, :], H[t][:, :], AF.Square, [bH[t], B["st"]], bxs_all + [B["st"]], accum=st[:, 0:1])
                ACTV(st[:, 1:2], st[:, 0:1], AF.Sqrt, [B["st"], B["zcol"]], [B["st"]], bias=epsc[:, 0:1], scale=1.0 / D)
                P.op("dve", lambda e, o=st[:, 1:2], i=st[:, 1:2]: e.reciprocal(out=o, in_=i), [B["st"]], [B["st"]])
                STT("dve", H[t][:, :], H[t][:, :], st[:, 1:2], gfin_v, ALU.mult, ALU.mult,
                    [bH[t], B["st"]] + allGA, [bH[t]])
                DMA(out_d[m * 4 + t, :, :], H[t][:, :], [bH[t]], [], queue="pool")

        def norm_transpose_from(X, bXl, tokoff):
            MSET("pool", st[:, 0:1], 0.0, [B["st"]])
            ACTV(xs[:, :], X, AF.Square, bXl + [B["st"]], bxs_all + [B["st"]], accum=st[:, 0:1])
            ACTV(st[:, 1:2], st[:, 0:1], AF.Sqrt, [B["st"], B["zcol"]], [B["st"]], bias=epsc[:, 0:1], scale=1.0 / D)
            P.op("dve", lambda e, o=st[:, 1:2], i=st[:, 1:2]: e.reciprocal(out=o, in_=i), [B["st"]], [B["st"]])
            TS("dve", xs[:, :], X, st[:, 1:2], None, ALU.mult, None, bXl + [B["st"]], bxs_all)
            for half in range(2):
                tb = rot("tr", 2)
                for c in range(8):
                    dc = half * 8 + c
                    TR(tr[tb][:, c * 128:(c + 1) * 128], xs[:, dc * 128:(dc + 1) * 128], ident[:, :],
                       bxs_all + [B["ident"]], [btr[tb]])
                CPY(evac_eng(), nTh[:, half * 8:(half + 1) * 8, tokoff:tokoff + 128],
                    tr[tb][:, :].rearrange("p (c t) -> p c t", c=8), [btr[tb]], bCT)

        P.dry = True
        saved = dict(cnt)
        emit_all()
        P.dry = False
        cnt.update(saved)
        emit_all()
        assert stream_state["pos"] == len(plan)
        P.emit(nc)
    return nc


def _t5_bucket_np(d):
    d = np.asarray(d, dtype=np.int64)
    max_exact = 16
    nf = np.maximum(d, max_exact).astype(np.float32)
    large = max_exact + (np.log(nf / np.float32(max_exact)) / np.float32(math.log(2048 / max_exact))
                         * np.float32(16)).astype(np.int32)
    large = np.minimum(large, 31)
    return np.where(d < max_exact, d, large)


def _slab(W, f0):
    return np.ascontiguousarray(W[:, f0:f0 + 512].reshape(16, 128, 512).transpose(1, 0, 2)).reshape(128, 8192)


def _prep_shared(inp):
    w_in = inp["w_in"][0]
    slabs = [_slab(w_in, f0) for f0 in range(0, 5120, 512)]
    slabs += [_slab(inp["w_out"][0], f0) for f0 in range(0, 2048, 512)]
    slabs.append(_slab(inp["wq_c"][0], 0))
    slabs.append(_slab(inp["wk_c"][0], 0))
    slabs.append(_slab(inp["wv_c"][0], 0))
    slabs.append(np.ascontiguousarray(inp["wo_c"][0].reshape(4, 128, 2048).transpose(1, 0, 2)).reshape(128, 8192))
    w1 = inp["w1"][0]
    slabs += [_slab(w1, f0) for f0 in range(0, 8192, 512)]
    w2 = inp["w2"][0]
    for ds in range(4):
        for fq in range(4):
            slabs.append(_slab(w2[fq * 2048:(fq + 1) * 2048], ds * 512))
    wall = np.stack(slabs).astype(np.float32)
    pc = lambda v: np.ascontiguousarray(v.reshape(-1, 128).T)
    gvec = np.concatenate([pc(inp["g_mix"][0]), pc(inp["g_cross"][0]), pc(inp["g_mem"][0]), pc(inp["g_mlp"][0])], axis=1)
    gfin = np.ascontiguousarray(np.broadcast_to(inp["g_final"][None, :], (128, D)))
    cw = np.ascontiguousarray(inp["conv_w"][0].reshape(KCONV, 8, 128).transpose(2, 1, 0)).reshape(128, 8 * KCONV)
    cp = np.concatenate([pc(inp["conv_b"][0]), pc(inp["conv_ln_g"][0]), pc(inp["conv_ln_b"][0])], axis=1)
    rel = inp["rel_bias"]
    relaug = np.concatenate([rel, np.full((1, 8), NEG, np.float32)], axis=0)
    c31 = np.ascontiguousarray(np.broadcast_to(rel[31][None, :], (128, 8)))
    return dict(wall=wall, gvec=gvec.astype(np.float32), gfin=gfin.astype(np.float32), convw=cw.astype(np.float32),
                convp=cp.astype(np.float32), relaug=relaug.astype(np.float32), c31=c31.astype(np.float32))


def _oh_table(r):
    oh = np.zeros((33, LFP), np.float32)
    y = np.arange(LFP)
    x = y + r * 64
    neg = x < 511
    oh[32, neg] = 1.0
    d = np.maximum(x - 511, 0)
    bk = _t5_bucket_np(d)
    oh[bk[~neg], y[~neg]] = 1.0
    return oh


def _core_inputs(inp, b, r, NBLK, shared, xa_cache):
    S = NBLK * 256
    x = inp["x"][b]
    xb = x.reshape(NBLK, 256, D)
    own = xb[:, r * 64:(r + 1) * 64, :]
    xo = np.ascontiguousarray(own.reshape(NBLK // 2, 128, D))
    xpad = np.concatenate([np.zeros((32, D), np.float32), x], axis=0)
    starts = (np.arange(NBLK) * 256 + r * 64)
    idx = starts[:, None] + np.arange(32)[None, :]
    xh = np.ascontiguousarray(xpad[idx].reshape(NBLK // 4, 128, D))
    if b not in xa_cache:
        xa_cache[b] = np.ascontiguousarray(xb[:, ::-1, :].reshape(S // 128, 128, D))
    d = dict(shared)
    d.update(xo=xo, xh=xh, xa=xa_cache[b], mem=np.ascontiguousarray(inp["mem"][b].reshape(2, 128, D)), oh=_oh_table(r))
    return d


_NC_CACHE = {}


def kernel(**inputs):
    inp = {k: np.asarray(v) for k, v in inputs.items()}
    Bn, S, _ = inp["x"].shape
    NBLK = S // 256
    shared = _prep_shared(inp)
    xa_cache = {}
    in_maps = []
    for b in range(Bn):
        for r in range(4):
            in_maps.append(_core_inputs(inp, b, r, NBLK, shared, xa_cache))
    if NBLK not in _NC_CACHE:
        _NC_CACHE[NBLK] = build_program(NBLK)
    nc = _NC_CACHE[NBLK]
    res = run_bass_kernel_spmd(nc, in_maps, core_ids=list(range(len(in_maps))))
    out = np.zeros((Bn, S, D), np.float32)
    ov = out.reshape(Bn, NBLK, 256, D)
    for b in range(Bn):
        for r in range(4):
            o = np.asarray(res.results[b * 4 + r]["out"]).reshape(NBLK, 64, D)
            ov[b, :, r * 64:(r + 1) * 64, :] = o
    return out
```

```python
import contextlib
import math
import numpy as np
import concourse.bass as bass
import concourse.mybir as mybir
from concourse.bass_utils import run_bass_kernel_spmd

F32 = mybir.dt.float32
BF16 = mybir.dt.bfloat16
ALU = mybir.AluOpType
AF = mybir.ActivationFunctionType
AX = mybir.AxisListType

D = 2048
DC = 16
NH = 8
NMEM = 256
KCONV = 31
EPS = 1e-6
NEG = -1000.0
SCALE = 128.0 ** -0.5
SQ = 128.0 ** 0.5
LFP = 2432
NSLAB = 50
RING = 4

COMPUTE = ("pe", "act", "dve", "pool")
NDMASEM = 14


class Buf:
    __slots__ = ("name", "lw", "rd")

    def __init__(self, name):
        self.name = name
        self.lw = None
        self.rd = []


class Ins:
    __slots__ = ("eng", "fn", "deps", "idx", "sig", "seq", "dsem", "dcnt", "waits", "src")

    def __init__(self, eng, fn):
        self.eng = eng
        self.fn = fn
        self.deps = []
        self.idx = -1
        self.sig = False
        self.seq = 0
        self.dsem = -1
        self.dcnt = 0
        self.waits = []
        self.src = None


class Prog:
    def __init__(self, same_engine_sync=True):
        self.streams = {e: [] for e in COMPUTE + ("sp",)}
        self.order = []
        self.same = same_engine_sync
        self.ndma = 0
        self.dma_last = [None] * NDMASEM
        self.dma_cnt = [0] * NDMASEM
        self.dry = False

    def _track(self, ins, reads, writes):
        deps = {}
        for b in reads:
            if b.lw is not None:
                deps[id(b.lw)] = b.lw
        for b in writes:
            if b.lw is not None:
                deps[id(b.lw)] = b.lw
            for r in b.rd:
                deps[id(r)] = r
        for b in reads:
            b.rd.append(ins)
        for b in writes:
            b.lw = ins
            b.rd = []
        ins.deps = list(deps.values())

    def op(self, eng, fn, reads=(), writes=()):
        if self.dry:
            return None
        ins = Ins(eng, fn)
        ins.src = eng
        self._track(ins, reads, writes)
        ins.idx = len(self.streams[eng])
        self.streams[eng].append(ins)
        self.order.append(ins)
        return ins

    def dma(self, fn, reads=(), writes=(), queue="sp"):
        if self.dry:
            return None
        ins = Ins(queue, fn)
        k = self.ndma % NDMASEM
        self.ndma += 1
        ins.dsem = k
        self.dma_cnt[k] += 1
        ins.dcnt = self.dma_cnt[k]
        ins.src = "d%d" % k
        self._track(ins, reads, writes)
        if self.dma_last[k] is not None:
            ins.deps.append(self.dma_last[k])
        self.dma_last[k] = ins
        ins.idx = len(self.streams[queue])
        self.streams[queue].append(ins)
        self.order.append(ins)
        return ins

    def analyze(self):
        vdone = {}
        last_start = {e: {} for e in self.streams}
        for ins in self.order:
            vc = dict(last_start[ins.eng])
            for d in ins.deps:
                if d.dsem >= 0:
                    src, val = d.src, d.dcnt
                else:
                    src, val = d.eng, d.idx + 1
                    if d.eng == ins.eng and ins.dsem < 0 and (d.eng == "pe" or not self.same):
                        continue
                if vc.get(src, 0) >= val:
                    continue
                d.sig = True
                ins.waits.append(d)
                for s, v in vdone[id(d)].items():
                    if vc.get(s, 0) < v:
                        vc[s] = v
            last_start[ins.eng] = vc
            vd = dict(vc)
            if ins.dsem >= 0:
                vd[ins.src] = ins.dcnt
            else:
                vd[ins.eng] = ins.idx + 1
            vdone[id(ins)] = vd
            ins.deps = None
        for ins in self.order:
            if len(ins.waits) > 1:
                best = {}
                for d in ins.waits:
                    val = d.dcnt if d.dsem >= 0 else d.idx
                    if d.src not in best or val > best[d.src][0]:
                        best[d.src] = (val, d)
                ins.waits = [v[1] for v in best.values()]
        for e in COMPUTE:
            n = 0
            for ins in self.streams[e]:
                if ins.dsem < 0 and ins.sig:
                    n += 1
                    ins.seq = n

    def emit(self, nc, final_wait_queue="sp"):
        self.analyze()
        with contextlib.ExitStack() as es:
            sems = {e: es.enter_context(nc.semaphore("s_" + e)) for e in COMPUTE}
            dsems = [es.enter_context(nc.semaphore("dq%d" % k)) for k in range(NDMASEM)]
            block = es.enter_context(nc.Block())

            def run(engname, eng):
                for ins in self.streams[engname]:
                    for d in ins.waits:
                        if d.dsem >= 0:
                            eng.wait_ge(dsems[d.dsem], 16 * d.dcnt)
                        else:
                            eng.wait_ge(sems[d.eng], d.seq)
                    r = ins.fn(eng)
                    if ins.dsem >= 0:
                        r.then_inc(dsems[ins.dsem], 16)
                    elif ins.sig:
                        r.then_inc(sems[ins.eng], 1)
                if engname == final_wait_queue:
                    for k in range(NDMASEM):
                        if self.dma_cnt[k]:
                            eng.wait_ge(dsems[k], 16 * self.dma_cnt[k])

            @block.tensor
            def _(e):
                run("pe", e)

            @block.scalar
            def _(e):
                run("act", e)

            @block.vector
            def _(e):
                run("dve", e)

            @block.gpsimd
            def _(e):
                run("pool", e)

            @block.sync
            def _(e):
                run("sp", e)


def build_program(NBLK, debug=False):
    S = NBLK * 256
    NG = NBLK // 8
    NT = NBLK // 2
    NAG = NBLK // 2
    nc = bass.Bass("TRN2", target_bir_lowering=False)
    dt_in = lambda name, shape: nc.dram_tensor(name, shape, F32, kind="ExternalInput").ap()
    xo_d = dt_in("xo", [NT, 128, D])
    xh_d = dt_in("xh", [NG * 2, 128, D])
    xa_d = dt_in("xa", [S // 128, 128, D])
    mem_d = dt_in("mem", [2, 128, D])
    wall_d = dt_in("wall", [NSLAB, 128, 8192])
    gv_d = dt_in("gvec", [128, 64])
    gfin_d = dt_in("gfin", [128, D])
    cw_d = dt_in("convw", [128, 8 * KCONV])
    cp_d = dt_in("convp", [128, 24])
    rel_d = dt_in("relaug", [33, 8])
    c31_d = dt_in("c31", [128, 8])
    oh_d = dt_in("oh", [33, LFP])
    out_d = nc.dram_tensor("out", [NT, 128, D], F32, kind="ExternalOutput").ap()
    wbf_d = nc.dram_tensor("wbf", [NSLAB, 128, 8192], BF16, kind="Internal").ap()
    kt_d = nc.dram_tensor("ktd", [NH, 128, S], BF16, kind="Internal").ap()
    v_d = nc.dram_tensor("vd", [NH, 128, S // 128, 128], BF16, kind="Internal").ap()
    cd_d = nc.dram_tensor("cdd", [4, 128, 8192], BF16, kind="Internal").ap()
    fp_h = nc.dram_tensor("fpd", [NH, LFP], BF16, kind="Internal")
    fp_d = fp_h.ap()

    P = Prog()
    es = contextlib.ExitStack()
    with es:
        def SB(name, shape, dt):
            return es.enter_context(nc.sbuf_tensor("sb_" + name, shape, dt))

        def PS(name, shape, dt):
            return es.enter_context(nc.psum_tensor("ps_" + name, shape, dt))

        H = [SB("H%d" % i, [128, D], F32) for i in range(4)]
        bH = [Buf("H%d" % i) for i in range(4)]
        ring = [SB("ring%d" % i, [128, 8192], BF16) for i in range(RING)]
        bringK = [Buf("ringK%d" % i) for i in range(RING)]
        bringV = [Buf("ringV%d" % i) for i in range(RING)]
        bring = [[bringK[i], bringV[i]] for i in range(RING)]
        bcd = [Buf("cd%d" % j) for j in range(4)]
        nT2 = SB("nT", [128, DC * 512], BF16)
        nT = nT2[:, :].rearrange("p (c t) -> p c t", c=DC)
        bnT = Buf("nT")
        QT2 = SB("QT", [128, NH * 512], BF16)
        QT = QT2[:, :].rearrange("p (h t) -> p h t", h=NH)
        bQT = [Buf("QT%d" % h) for h in range(NH)]
        CT2 = SB("CT", [128, 8 * 512], BF16)
        CT = CT2[:, :].rearrange("p (c t) -> p c t", c=8)
        nTh = CT2[:, :].rearrange("p (c t) -> p c t", c=DC)
        bCT = [Buf("CT%d" % c) for c in range(8)]
        GA = SB("GA", [128, 6144], F32)
        YA = SB("YA", [128, 4096], F32)
        bGA = [Buf("GA%d" % c) for c in range(8)]
        bYA = [Buf("YA%d" % c) for c in range(8)]
        xs = SB("xs", [128, D], BF16)
        bxs = Buf("xs")
        ident = SB("ident", [128, 128], BF16)
        onesf = SB("onesf", [128, 128], F32)
        gvec = SB("gvec", [128, 64], F32)
        cw = SB("cw", [128, 8 * KCONV], F32)
        cp = SB("cp", [128, 24], F32)
        c31m = SB("c31m", [128, 8], F32)
        relaug = SB("relaug", [33, 8], F32)
        ohs = YA[0:33, 0:LFP]
        fps = GA[:, :].bitcast(BF16)[0:8, 0:LFP]
        ksum2 = SB("ksum", [128, NH * NBLK], F32)
        kb322 = SB("kb32", [128, NH * NBLK], F32)
        ksum = ksum2[:, :].rearrange("p (h k) -> p h k", h=NH)
        kb32 = kb322[:, :].rearrange("p (h k) -> p h k", h=NH)
        bmv2 = SB("bmv", [128, 2048], F32)
        khi = SB("khi", [128, NH, NBLK], BF16)
        klo = SB("klo", [128, NH, NBLK], BF16)
        kcT = SB("kcT", [128, 4, 256], BF16)
        vcs = SB("vcs", [128, 2, 512], BF16)
        TT = SB("TT", [128, NH, 136], F32)
        st = SB("st", [128, 16], F32)
        sgt = SB("sgt", [128, 1024], F32)
        mean_sb = sgt[:, 0:512]
        rstd_sb = sgt[:, 512:1024]
        gsb = [SB("gsb%d" % i, [128, 64], F32) for i in range(2)]
        m30 = [SB("m30%d" % i, [128, 64], F32) for i in range(2)]
        mx8 = [SB("mx8%d" % i, [128, 8], F32) for i in range(2)]
        rs = [SB("rs%d" % i, [128, 64], F32) for i in range(4)]
        Oacc = [SB("Oacc%d" % i, [128, 128], F32) for i in range(4)]
        Pb = [xs[:, i * 512:(i + 1) * 512] for i in range(2)]
        PTs = [xs[:, 1024 + i * 512:1024 + (i + 1) * 512] for i in range(2)]
        rtmp = [sgt[:, i * 512:(i + 1) * 512] for i in range(2)]
        zcol = SB("zcol", [128, 1], F32)
        dummy = SB("dummy", [128, 1], F32)
        epsc = SB("epsc", [128, 1], F32)
        B = {n: Buf(n) for n in ("ident onesf gvec cw cp c31m relaug khi klo kcT vcs TT st sgt zcol fpd").split()}
        B["mean_sb"] = B["sgt"]
        B["rstd_sb"] = B["sgt"]
        B["ohs"] = bYA[0]
        B["fps"] = bGA[0]
        B["ksum"] = Buf("ksum")
        B["kb32"] = Buf("kb32")
        pstg = [Buf("pstg%d" % i) for i in range(4)]
        pout = [Buf("pout%d" % i) for i in range(2)]
        B["dummy"] = Buf("dummy")
        brtmp = [B["sgt"], B["sgt"]]
        bgsb = [Buf("gsb%d" % i) for i in range(2)]
        bm30 = [Buf("m30%d" % i) for i in range(2)]
        bmx8 = [Buf("mx8%d" % i) for i in range(2)]
        brs = [Buf("rs%d" % i) for i in range(4)]
        bOacc = [Buf("Oacc%d" % i) for i in range(4)]
        bPb = [Buf("Pb%d" % i) for i in range(2)]
        bPTs = [Buf("PTs%d" % i) for i in range(2)]
        bxs_all = [bxs] + bPb + bPTs
        bctmp = [Buf("ctmp0"), Buf("ctmp1")]
        bHK = [Buf("HK%d" % h) for h in range(2)]
        bbm = [[Buf("bm%d_%d" % (h, t)) for t in range(4)] for h in range(NH)]
        bA = [Buf("A%d" % t) for t in range(4)]
        bwd = [Buf("wd%d" % s) for s in range(NSLAB)]
        bkv = [Buf("kv%d" % g) for g in range(NAG)]
        G4 = GA[:, :].bitcast(BF16)[:, 0:6144].rearrange("p (c q t) -> p c q t", c=8, q=8)
        Y3 = YA[:, :].rearrange("p (c t) -> p c t", c=8)
        Y4 = YA[:, :].rearrange("p (c q t) -> p c q t", c=8, q=8)
        GAb = GA[:, :].bitcast(BF16)
        YAb = YA[:, :].bitcast(BF16)
        HK = GAb[:, 0:2 * 8 * 256].rearrange("p (h s k) -> p h s k", h=2, s=8)
        bmv = bmv2[:, :].rearrange("p (h t k) -> p h t k", h=NH, t=4)
        Av = YAb[:, 4096:8192].rearrange("p (t f) -> p t f", t=4)
        h1a = YAb[:, :].rearrange("p (c t) -> p c t", c=16)
        h1b = GAb[:, 0:8192].rearrange("p (c t) -> p c t", c=16)
        gfin_v = GA[:, 4096:6144]
        mm = [PS("mm%d" % i, [128, 512], F32) for i in range(3)]
        bmm = [Buf("mm%d" % i) for i in range(3)]
        tr = [PS("tr%d" % i, [128, 1024], BF16) for i in range(2)]
        btr = [Buf("tr%d" % i) for i in range(2)]
        sp_ = [PS("sps%d" % i, [128, 512], F32) for i in range(2)]
        bsp = [Buf("sps%d" % i) for i in range(2)]
        misc = PS("misc", [128, 512], F32)
        bmisc = [Buf("misc%d" % i) for i in range(4)]

        print("SBUF bytes remaining per partition:", nc.sbuf_bytes_remaining)
        cnt = {"mm": 0, "tr": 0, "sp": 0, "ev": 0, "misc": 0, "pb": 0, "pt": 0, "g": 0, "rt": 0}

        def rot(key, n):
            v = cnt[key] % n
            cnt[key] += 1
            return v

        def MM(out, lhsT, rhs, start, stop, reads, writes):
            P.op("pe", lambda e, o=out, l=lhsT, r=rhs, s=start, t=stop: e.matmul(o, lhsT=l, rhs=r, start=s, stop=t),
                 reads, writes)

        def TR(out, in_, idn, reads, writes):
            P.op("pe", lambda e, o=out, i=in_, d=idn: e.transpose(out=o, in_=i, identity=d), reads, writes)

        def ACTV(out, in_, func, reads, writes, bias=None, scale=1.0, accum=None):
            def f(e, o=out, i=in_, fn=func, b=bias, s=scale, a=accum):
                kw = {}
                if b is not None:
                    kw["bias"] = b
                if a is not None:
                    kw["accum_out"] = a
                return e.activation(out=o, in_=i, func=fn, scale=s, **kw)
            P.op("act", f, reads, writes)

        def CPY(eng, out, in_, reads, writes):
            if eng == "act":
                P.op("act", lambda e, o=out, i=in_: e.copy(out=o, in_=i), reads, writes)
            else:
                P.op(eng, lambda e, o=out, i=in_: e.tensor_copy(out=o, in_=i), reads, writes)

        def TS(eng, out, in0, s1, s2, op0, op1, reads, writes):
            if op1 is None:
                P.op(eng, lambda e, o=out, i=in0, a=s1, p0=op0: e.tensor_scalar(out=o, in0=i, scalar1=a, scalar2=None, op0=p0),
                     reads, writes)
            else:
                P.op(eng, lambda e, o=out, i=in0, a=s1, b=s2, p0=op0, p1=op1:
                     e.tensor_scalar(out=o, in0=i, scalar1=a, scalar2=b, op0=p0, op1=p1), reads, writes)

        def TTO(eng, out, in0, in1, op, reads, writes):
            P.op(eng, lambda e, o=out, a=in0, b=in1, p=op: e.tensor_tensor(out=o, in0=a, in1=b, op=p), reads, writes)

        def STT(eng, out, in0, sc, in1, op0, op1, reads, writes):
            P.op(eng, lambda e, o=out, a=in0, s=sc, b=in1, p0=op0, p1=op1:
                 e.scalar_tensor_tensor(out=o, in0=a, scalar=s, in1=b, op0=p0, op1=p1), reads, writes)

        def MSET(eng, out, val, writes):
            P.op(eng, lambda e, o=out, v=val: e.memset(o, v), (), writes)

        def DMA(out, in_, reads, writes, queue="sp"):
            P.dma(lambda e, o=out, i=in_: e.dma_start(out=o, in_=i), reads, writes, queue)

        def evac_eng():
            return ("act", "dve")[rot("ev", 2)]

        plan = []
        stream_state = {"next_issue": 0, "pos": 0}

        def issue_item(i):
            kind, args = plan[i]
            rb = i % RING
            dst = ring[rb]
            if kind == "slab":
                s = args
                DMA(dst[:, :], wbf_d[s, :, :], [bwd[s]], bring[rb])
            elif kind == "cdiag":
                DMA(dst[:, :], cd_d[args, :, :], [bcd[args]], bring[rb])
            else:
                h, b0, nb = args
                DMA(dst[:, 0:nb * 256], kt_d[h, :, b0 * 256:(b0 + nb) * 256],
                    [bkv[g] for g in range(b0 // 2, (b0 + nb + 1) // 2)], [bringK[rb]])
                DMA(dst[:, 4096:4096 + nb * 256].rearrange("p (c d) -> p c d", d=128),
                    v_d[h, :, b0 * 2:(b0 + nb) * 2, :],
                    [bkv[g] for g in range(b0 // 2, (b0 + nb + 1) // 2)], [bringV[rb]])

        def fetch(kind, args):
            if P.dry:
                plan.append((kind, args))
                return 0
            i = stream_state["pos"]
            stream_state["pos"] += 1
            assert plan[i] == (kind, args), (plan[i], kind, args)
            while stream_state["next_issue"] < min(len(plan), i + RING - 2) or stream_state["next_issue"] <= i:
                issue_item(stream_state["next_issue"])
                stream_state["next_issue"] += 1
            return i % RING

        def norm_transpose(X, bX, npart, tokoff):
            MSET("pool", st[0:npart, 0:1], 0.0, [B["st"]])
            ACTV(xs[0:npart, :], X[0:npart, :], AF.Square, [bX, B["st"]], bxs_all + [B["st"]], accum=st[0:npart, 0:1])
            ACTV(st[0:npart, 1:2], st[0:npart, 0:1], AF.Sqrt, [B["st"], B["zcol"]], [B["st"]], bias=epsc[0:npart, 0:1], scale=1.0 / D)
            P.op("dve", lambda e, o=st[0:npart, 1:2], i=st[0:npart, 1:2]: e.reciprocal(out=o, in_=i), [B["st"]], [B["st"]])
            TS("dve", xs[0:npart, :], X[0:npart, :], st[0:npart, 1:2], None, ALU.mult, None, [bX, B["st"]], bxs_all)
            for half in range(2):
                tb = rot("tr", 2)
                for c in range(8):
                    dc = half * 8 + c
                    TR(tr[tb][:, c * 128:c * 128 + npart], xs[0:npart, dc * 128:(dc + 1) * 128], ident[0:npart, 0:npart],
                       bxs_all + [B["ident"]], [btr[tb]])
                src = tr[tb][:, :].rearrange("p (c t) -> p c t", c=8)[:, :, 0:npart]
                CPY(evac_eng(), nT[:, half * 8:(half + 1) * 8, tokoff:tokoff + npart], src, [btr[tb]], [bnT])

        allGA_ = bGA
        allYA_ = bYA

        def emit_all():
            MSET("pool", ident[:, :], 1.0, [B["ident"]])
            P.op("pool", lambda e: e.affine_select(out=ident[:, :], in_=ident[:, :], pattern=[[-1, 128]],
                                                   compare_op=ALU.is_equal, fill=0.0, base=0, channel_multiplier=1),
                 [B["ident"]], [B["ident"]])
            MSET("pool", onesf[:, :], 1.0 / 1024.0, [B["onesf"]])
            MSET("pool", zcol[:, :], 0.0, [B["zcol"]])
            MSET("pool", epsc[:, :], EPS, [B["zcol"]])
            DMA(gvec[:, :], gv_d[:, :], [], [B["gvec"]])
            DMA(cw[:, :], cw_d[:, :], [], [B["cw"]])
            DMA(cp[:, :], cp_d[:, :], [], [B["cp"]])
            DMA(c31m[:, :], c31_d[:, :], [], [B["c31m"]])
            DMA(relaug[:, :], rel_d[:, :], [], [B["relaug"]])
            DMA(ohs[:, :], oh_d[:, :], [], [B["ohs"]])
            for c0 in range(0, LFP, 512):
                w = min(512, LFP - c0)
                mb = rot("mm", 3)
                MM(mm[mb][0:8, 0:w], relaug[:, :], ohs[:, c0:c0 + w], True, True, [B["relaug"], B["ohs"]], [bmm[mb]])
                ACTV(fps[:, c0:c0 + w], mm[mb][0:8, 0:w], AF.Copy, [bmm[mb]], [B["fps"]], scale=SQ)
            DMA(fp_d[:, :], fps[:, :], [B["fps"]], [B["fpd"]], queue="pool")
            MSET("pool", TT[:, :, :], 0.0, [B["TT"]])
            for h in range(NH):
                TS("dve", TT[:, h, :], TT[:, h, :], c31m[:, h:h + 1], None, ALU.add, None,
                   [B["TT"], B["c31m"]], [B["TT"]])
            MSET("pool", TT[0:64, :, 58:64], 0.0, [B["TT"]])
            MSET("pool", TT[0:64, :, 64:65], -NEG, [B["TT"]])
            MSET("pool", TT[0:64, :, 65:136], NEG, [B["TT"]])
            MSET("pool", TT[64:128, :, 59:65], 0.0, [B["TT"]])
            MSET("pool", TT[64:128, :, 65:66], -NEG, [B["TT"]])
            MSET("pool", TT[64:128, :, 66:136], NEG, [B["TT"]])

            for j in range(4):
                for ccl in range(2):
                    for k in range(KCONV):
                        cc = j * 2 + ccl
                        di = (ccl * KCONV + k) * 128
                        TS("dve", ring[j][:, di:di + 128], ident[:, :], cw[:, cc * KCONV + k:cc * KCONV + k + 1], None,
                           ALU.mult, None, [B["ident"], B["cw"]], bring[j])
                DMA(cd_d[j, :, 0:2 * KCONV * 128], ring[j][:, 0:2 * KCONV * 128], bring[j], [bcd[j]], queue="pool")
            gcol = {}
            for s_ in range(0, 10):
                gcol[s_] = 0
            gcol[14] = 16
            gcol[15] = 32
            gcol[16] = 32
            for s_ in range(18, 34):
                gcol[s_] = 48
            stg = [GA[:, 0:2048], GA[:, 2048:4096], YA[:, 0:2048], YA[:, 2048:4096]]
            GAb_ = GA[:, :].bitcast(BF16)
            pob = [GAb_[:, 8192:10240], GAb_[:, 10240:12288]]
            pk = {"k": 0}
            P.op("pool", lambda e: e.memset(dummy[:, :], 0.0), [bYA[0], bGA[0]] + allGA_ + allYA_, pstg + pout + [B["dummy"]])

            def prep_item(s_, qs):
                k = pk["k"]
                pk["k"] += 1
                si = k % 4
                oi = k % 2
                DMA(stg[si], wall_d[s_, :, qs * 2048:(qs + 1) * 2048], [], [pstg[si]])
                if s_ in gcol:
                    for dl in range(4):
                        eng = ("dve", "act")[(k + dl) % 2]
                        gc = gvec[:, gcol[s_] + qs * 4 + dl:gcol[s_] + qs * 4 + dl + 1]
                        o = pob[oi][:, dl * 512:(dl + 1) * 512]
                        i_ = stg[si][:, dl * 512:(dl + 1) * 512]
                        if eng == "act":
                            ACTV(o, i_, AF.Copy, [pstg[si], B["gvec"]], [pout[oi]], scale=gc)
                        else:
                            TS(eng, o, i_, gc, None, ALU.mult, None, [pstg[si], B["gvec"]], [pout[oi]])
                else:
                    for q in range(2):
                        eng = ("dve", "act")[(k + q) % 2]
                        CPY(eng, pob[oi][:, q * 1024:(q + 1) * 1024], stg[si][:, q * 1024:(q + 1) * 1024],
                            [pstg[si]], [pout[oi]])
                DMA(wbf_d[s_, :, qs * 2048:(qs + 1) * 2048], pob[oi], [pout[oi]], [bwd[s_]], queue="pool")

            for s_ in (2, 3, 4, 5):
                for qs in range(4):
                    prep_item(s_, qs)
            rest_items = [(s_, qs) for s_ in range(NSLAB) if s_ not in (2, 3, 4, 5) for qs in range(4)]
            per_group = -(-len(rest_items) // NAG) if NAG >= 24 else 6

            MSET("pool", ksum[:, :, :], 0.0, [B["ksum"]])
            for s_i, s in enumerate((2, 3, 4, 5)):
                DMA(ring[s_i][:, :], wbf_d[s, :, :], [bwd[s]], bring[s_i])
            Wk = [ring[0][:, :].rearrange("p (c f) -> p c f", c=16), ring[1][:, :].rearrange("p (c f) -> p c f", c=16)]
            Wv = [ring[2][:, :].rearrange("p (c f) -> p c f", c=16), ring[3][:, :].rearrange("p (c f) -> p c f", c=16)]
            ntile_a = S // 128
            for i in range(min(3, ntile_a)):
                DMA(H[i % 4][:, :], xa_d[i, :, :], [], [bH[i % 4]])
            for ag in range(NAG):
                for _ in range(per_group):
                    if rest_items:
                        prep_item(*rest_items.pop(0))
                for t in range(4):
                    ti = ag * 4 + t
                    if ti + 3 < ntile_a:
                        DMA(H[(ti + 3) % 4][:, :], xa_d[ti + 3, :, :], [], [bH[(ti + 3) % 4]])
                    norm_transpose(H[ti % 4], bH[ti % 4], 128, t * 128)
                for h in range(NH):
                    mb = rot("mm", 3)
                    for dc in range(DC):
                        MM(mm[mb][:, :], Wk[h // 4][:, dc, (h % 4) * 128:(h % 4 + 1) * 128], nT[:, dc, 0:512],
                           dc == 0, dc == DC - 1, bring[h // 4] + [bnT], [bmm[mb]])
                    for bl in range(2):
                        ACTV(QT[:, h, bl * 256:(bl + 1) * 256], mm[mb][:, bl * 256:(bl + 1) * 256], AF.Copy,
                             [bmm[mb], B["ksum"]], [bQT[h], B["ksum"]],
                             accum=ksum[:, h, ag * 2 + bl:ag * 2 + bl + 1])
                for t in range(4):
                    for sl in range(2):
                        mb = rot("mm", 3)
                        for dc in range(DC):
                            MM(mm[mb][:, :], nT[:, dc, t * 128:(t + 1) * 128], Wv[sl][:, dc, :],
                               dc == 0, dc == DC - 1, bring[2 + sl] + [bnT], [bmm[mb]])
                        CPY("dve", CT[:, t * 2 + sl, :], mm[mb][:, :], [bmm[mb]], [bCT[t * 2 + sl]])
                DMA(kt_d[:, :, ag * 512:(ag + 1) * 512].rearrange("h p t -> p h t"), QT[:, :, :],
                    bQT, [bkv[ag]], queue="pool")
                for t in range(4):
                    DMA(v_d[:, :, ag * 4 + t, :].rearrange("h p d -> p h d"),
                        CT2[:, t * 1024:(t + 1) * 1024].rearrange("p (h d) -> p h d", d=128),
                        bCT[2 * t:2 * t + 2], [bkv[ag]], queue="pool")
            while rest_items:
                prep_item(*rest_items.pop(0))
            P.op("pool", lambda e: e.memset(dummy[:, :], 0.0), pstg + pout, pstg + pout + allGA_ + allYA_ + [B["dummy"]])
            TS("dve", kb32[:, :, :], ksum[:, :, :], 1.0 / 256.0, None, ALU.mult, None, [B["ksum"]], [B["kb32"]])
            CPY("dve", khi[:, :, :], kb32[:, :, :], [B["kb32"]], [B["khi"]])
            CPY("dve", ksum[:, :, :], khi[:, :, :], [B["khi"]], [B["ksum"]])
            TTO("dve", klo[:, :, :], kb32[:, :, :], ksum[:, :, :], ALU.subtract, [B["kb32"], B["ksum"]], [B["klo"]])

            for i in range(2):
                DMA(H[i][:, :], mem_d[i, :, :], [], [bH[i]])
                norm_transpose(H[i], bH[i], 128, i * 128)
            rk = fetch("slab", 15)
            wkc = ring[rk][:, :].rearrange("p (c f) -> p c f", c=16)
            for h4 in range(4):
                mb = rot("mm", 3)
                for dc in range(DC):
                    MM(mm[mb][:, 0:256], wkc[:, dc, h4 * 128:(h4 + 1) * 128], nT[:, dc, 0:256], dc == 0, dc == DC - 1,
                       bring[rk] + [bnT], [bmm[mb]])
                CPY(evac_eng(), kcT[:, h4, :], mm[mb][:, 0:256], [bmm[mb]], [B["kcT"]])
            rv = fetch("slab", 16)
            wvc = ring[rv][:, :].rearrange("p (c f) -> p c f", c=16)
            for i in range(2):
                mb = rot("mm", 3)
                for dc in range(DC):
                    MM(mm[mb][:, :], nT[:, dc, i * 128:(i + 1) * 128], wvc[:, dc, :], dc == 0, dc == DC - 1,
                       bring[rv] + [bnT], [bmm[mb]])
                CPY(evac_eng(), vcs[:, i, :], mm[mb][:, :], [bmm[mb]], [B["vcs"]])

            for m in range(NG):
                emit_group(m)

        def emit_group(m):
            allGA = bGA
            allYA = bYA
            for t in range(4):
                DMA(H[t][:, :], xo_d[m * 4 + t, :, :], [], [bH[t]])
            for t in range(4):
                norm_transpose(H[t], bH[t], 128, t * 128)
            for i in range(2):
                hx = YA[:, i * 2048:(i + 1) * 2048]
                DMA(hx, xh_d[m * 2 + i, :, :], [], [bYA[i * 4 + j] for j in range(4)])
                norm_transpose_from(hx, [bYA[i * 4 + j] for j in range(4)], i * 128)
            slab_of = {}
            for cc in range(8):
                if cc % 4 == 0:
                    slab_of["v"] = fetch("slab", 6 + cc // 4)
                    slab_of["g"] = fetch("slab", 8 + cc // 4)
                wv_ = ring[slab_of["v"]][:, :].rearrange("p (c f) -> p c f", c=16)
                wg_ = ring[slab_of["g"]][:, :].rearrange("p (c f) -> p c f", c=16)
                fo = (cc % 4) * 128
                mv = rot("mm", 3)
                mg = rot("mm", 3)
                for dc in range(DC):
                    MM(mm[mv][:, :], wv_[:, dc, fo:fo + 128], nT[:, dc, 0:512], dc == 0, dc == DC - 1,
                       bring[slab_of["v"]] + [bnT], [bmm[mv]])
                    MM(misc[:, 0:256], wv_[:, dc, fo:fo + 128], nTh[:, dc, :], dc == 0, dc == DC - 1,
                       bring[slab_of["v"]] + bCT, [bmisc[0], bmisc[1]])
                for dc in range(DC):
                    MM(mm[mg][:, :], wg_[:, dc, fo:fo + 128], nT[:, dc, 0:512], dc == 0, dc == DC - 1,
                       bring[slab_of["g"]] + [bnT], [bmm[mg]])
                    MM(misc[:, 256:512], wg_[:, dc, fo:fo + 128], nTh[:, dc, :], dc == 0, dc == DC - 1,
                       bring[slab_of["g"]] + bCT, [bmisc[2], bmisc[3]])
                ACTV(sgt[:, 0:512], mm[mg][:, :], AF.Sigmoid, [bmm[mg]], [B["sgt"]])
                ACTV(sgt[:, 512:768], misc[:, 256:512], AF.Sigmoid, [bmisc[2], bmisc[3]], [B["sgt"]])
                TTO("dve", G4[:, cc, :, 32:96], mm[mv][:, :].rearrange("p (q t) -> p q t", q=8),
                    sgt[:, 0:512].rearrange("p (q t) -> p q t", q=8), ALU.mult, [bmm[mv], B["sgt"]], [bGA[cc]])
                TTO("dve", G4[:, cc, :, 0:32], misc[:, 0:256].rearrange("p (q t) -> p q t", q=8),
                    sgt[:, 512:768].rearrange("p (q t) -> p q t", q=8), ALU.mult,
                    [bmisc[0], bmisc[1], B["sgt"]], [bGA[cc]])
            for cp_ in range(4):
                rcd = fetch("cdiag", cp_)
                for ccl in range(2):
                    cc = cp_ * 2 + ccl
                    mb = rot("mm", 3)
                    for k in range(KCONV):
                        di = (ccl * KCONV + k) * 128
                        MM(mm[mb][:, :].rearrange("p (q t) -> p q t", q=8), ring[rcd][:, di:di + 128],
                           G4[:, cc, :, k + 2:k + 66], k == 0, k == KCONV - 1, [bGA[cc]] + bring[rcd], [bmm[mb]])
                    if cc % 2 == 0:
                        ACTV(Y3[:, cc, :], mm[mb][:, :], AF.Identity, [bmm[mb], B["cp"]], [bYA[cc]], bias=cp[:, cc:cc + 1])
                    else:
                        TS("dve", Y3[:, cc, :], mm[mb][:, :], cp[:, cc:cc + 1], None, ALU.add, None,
                           [bmm[mb], B["cp"]], [bYA[cc]])
            for h in range(NH):
                if h % 4 == 0:
                    rq = fetch("slab", h // 4)
                    wq_ = ring[rq][:, :].rearrange("p (c f) -> p c f", c=16)
                mb = rot("mm", 3)
                for dc in range(DC):
                    MM(mm[mb][:, :], wq_[:, dc, (h % 4) * 128:(h % 4 + 1) * 128], nT[:, dc, 0:512], dc == 0, dc == DC - 1,
                       bring[rq] + [bnT], [bmm[mb]])
                CPY(evac_eng(), QT[:, h, :], mm[mb][:, :], [bmm[mb]], [bQT[h]])
            GS = GA[:, 0:4096].rearrange("p (c t) -> p c t", c=8)
            for cc in range(8):
                TTO("pool", GS[:, cc, :], Y3[:, cc, :], Y3[:, cc, :], ALU.mult, [bYA[cc]] + allGA, allGA)
            m1 = rot("mm", 3)
            m2 = rot("mm", 3)
            for cc in range(8):
                MM(mm[m1][:, :], onesf[:, :], Y3[:, cc, :], cc == 0, cc == 7, [B["onesf"], bYA[cc]], [bmm[m1]])
            for cc in range(8):
                MM(mm[m2][:, :], onesf[:, :], GS[:, cc, :], cc == 0, cc == 7, [B["onesf"]] + allGA, [bmm[m2]])
            CPY("dve", mean_sb[:, :], mm[m1][:, :], [bmm[m1]], [B["mean_sb"]])
            TTO("dve", rstd_sb[:, :], mean_sb[:, :], mean_sb[:, :], ALU.mult, [B["mean_sb"]], [B["rstd_sb"]])
            TTO("dve", rstd_sb[:, :], mm[m2][:, :], rstd_sb[:, :], ALU.subtract, [bmm[m2], B["rstd_sb"]], [B["rstd_sb"]])
            ACTV(rstd_sb[:, :], rstd_sb[:, :], AF.Sqrt, [B["rstd_sb"], B["zcol"]], [B["rstd_sb"]], bias=epsc[:, 0:1])
            P.op("dve", lambda e, o=rstd_sb[:, :], i=rstd_sb[:, :]: e.reciprocal(out=o, in_=i), [B["rstd_sb"]], [B["rstd_sb"]])
            for cc in range(8):
                eng = "pool" if cc % 2 == 0 else "dve"
                TTO(eng, Y3[:, cc, :], Y3[:, cc, :], mean_sb[:, :], ALU.subtract, [bYA[cc], B["mean_sb"]], [bYA[cc]])
                TTO(eng, Y3[:, cc, :], Y3[:, cc, :], rstd_sb[:, :], ALU.mult, [bYA[cc], B["rstd_sb"]], [bYA[cc]])
                ACTV(CT[:, cc, :], Y3[:, cc, :], AF.Silu, [bYA[cc], B["cp"]], [bCT[cc]],
                     bias=cp[:, 16 + cc:17 + cc], scale=cp[:, 8 + cc:9 + cc])
            for h in range(NH):
                for t in range(4):
                    tau = m * 4 + t
                    ncol = 2 * tau + 2
                    gi = rot("g", 2)
                    rg = rot("misc", 4)
                    MSET("pool", gsb[gi][:, :], -1e30, [bgsb[gi]])
                    gp = misc[:, rg * 128:rg * 128 + 64]
                    MM(gp[:, 0:ncol], QT[:, h, t * 128:(t + 1) * 128], khi[:, h, 0:ncol], True, False,
                       [bQT[h], B["khi"]], [bmisc[rg]])
                    MM(gp[:, 0:ncol], QT[:, h, t * 128:(t + 1) * 128], klo[:, h, 0:ncol], False, True,
                       [bQT[h], B["klo"]], [bmisc[rg]])
                    if tau > 0:
                        CPY("dve", gsb[gi][0:64, 0:2 * tau], gp[0:64, 0:2 * tau], [bmisc[rg]], [bgsb[gi]])
                    CPY("dve", gsb[gi][64:128, 0:2 * tau + 1], gp[64:128, 0:2 * tau + 1], [bmisc[rg]], [bgsb[gi]])
                    P.op("dve", lambda e, o=mx8[gi][:, :], i=gsb[gi][:, 0:max(8, ncol)]: e.max(out=o, in_=i),
                         [bgsb[gi]], [bmx8[gi]])
                    TS("dve", mx8[gi][:, 2:3], mx8[gi][:, 2:3], -1e29, None, ALU.max, None, [bmx8[gi]], [bmx8[gi]])
                    TS("dve", m30[gi][:, 0:ncol], gsb[gi][:, 0:ncol], mx8[gi][:, 2:3], 1e30, ALU.subtract, ALU.mult,
                       [bgsb[gi], bmx8[gi]], [bm30[gi]])
                    TS("dve", m30[gi][:, 0:ncol], m30[gi][:, 0:ncol], -1.0, 0.0, ALU.max, ALU.min, [bm30[gi]], [bm30[gi]])
                    STT("dve", bmv[:, h, t, 0:ncol], m30[gi][:, 0:ncol], -NEG, TT[:, h, 64 - 2 * tau:64 - 2 * tau + ncol],
                        ALU.mult, ALU.add, [bm30[gi], B["TT"]], [bbm[h][t]])
            nblk_g = 8 * m + 8
            for h in range(NH):
                hb = h % 2
                DMA(HK[0:64, hb, :, :], bass.AP(fp_h, h * LFP, [[1, 64], [256, 8], [1, 256]]),
                    [B["fpd"]], [bHK[hb]] + allGA)
                DMA(HK[64:128, hb, :, :], bass.AP(fp_h, h * LFP + 256, [[1, 64], [256, 8], [1, 256]]),
                    [B["fpd"]], [bHK[hb]] + allGA)
                steps = []
                for kc in range((nblk_g + 15) // 16):
                    b0 = kc * 16
                    nb = min(16, nblk_g - b0)
                    for t in range(4):
                        tau = m * 4 + t
                        blks = list(range(b0, min(b0 + nb, 2 * tau + 2)))
                        npair = len(blks) // 2
                        for pi in range(npair):
                            steps.append(dict(kc=kc, b0=b0, nb=nb, t=t, tau=tau, pi=pi, npair=npair,
                                              kb=(blks[2 * pi], blks[2 * pi + 1])))
                cur = {"kc": -1, "rb": 0}

                def stA(sx):
                    if sx["kc"] != cur["kc"]:
                        cur["kc"] = sx["kc"]
                        cur["rb"] = fetch("kv", (h, sx["b0"], sx["nb"]))
                    rb = cur["rb"]
                    sx["rb"] = rb
                    t, tau, b0 = sx["t"], sx["tau"], sx["b0"]
                    if sx["kc"] == 0 and sx["pi"] == 0:
                        MSET("pool", rs[t][:, :], 0.0, [brs[t]])
                    KTc = ring[rb][:, 0:4096]
                    sb_ = rot("sp", 2)
                    sx["sb"] = sb_
                    kb0, kb1 = sx["kb"]
                    lo = (kb0 - b0) * 256
                    qT = QT[:, h, t * 128:(t + 1) * 128]
                    if 2 * tau + 1 - kb0 <= 7:
                        for bi, kb in enumerate(sx["kb"]):
                            so = bi * 256
                            MM(sp_[sb_][:, so:so + 256], qT, KTc[:, lo + so:lo + so + 256], True, False,
                               [bQT[h], bringK[rb]], [bsp[sb_]])
                            MM(sp_[sb_][:, so:so + 256], ident[:, :], HK[:, hb, 2 * tau + 1 - kb, :], False, True,
                               [B["ident"], bHK[hb]] + allGA, [bsp[sb_]])
                    else:
                        MM(sp_[sb_][:, :], qT, KTc[:, lo:lo + 512], True, True, [bQT[h], bringK[rb]], [bsp[sb_]])
                    pb = rot("pb", 2)
                    sx["pb"] = pb
                    for bi, kb in enumerate(sx["kb"]):
                        ACTV(Pb[pb][:, bi * 256:(bi + 1) * 256], sp_[sb_][:, bi * 256:(bi + 1) * 256], AF.Exp,
                             [bsp[sb_], bbm[h][t], brs[t]], [bPb[pb], brs[t]],
                             bias=bmv[:, h, t, kb:kb + 1], scale=SCALE, accum=rs[t][:, kb:kb + 1])

                def stC(sx):
                    pb = sx["pb"]
                    tb = rot("tr", 2)
                    for q4 in range(4):
                        TR(tr[tb][:, q4 * 128:(q4 + 1) * 128], Pb[pb][:, q4 * 128:(q4 + 1) * 128], ident[:, :],
                           [bPb[pb], B["ident"]], [btr[tb]])
                    pt = rot("pt", 2)
                    sx["pt"] = pt
                    CPY("dve", PTs[pt][:, :], tr[tb][:, 0:512], [btr[tb]], [bPTs[pt]])

                def stE(sx):
                    rb, t, b0, pt = sx["rb"], sx["t"], sx["b0"], sx["pt"]
                    Vc = ring[rb][:, 4096:8192].rearrange("p (c d) -> p c d", d=128)
                    if sx["pi"] == 0:
                        cur["ob%d" % t] = rot("misc", 4)
                    ob = cur["ob%d" % t]
                    Ops = misc[:, ob * 128:(ob + 1) * 128]
                    for q4 in range(4):
                        kb = sx["kb"][q4 // 2]
                        vch = (kb - b0) * 2 + (q4 % 2)
                        MM(Ops, PTs[pt][:, q4 * 128:(q4 + 1) * 128], Vc[:, vch, :],
                           sx["pi"] == 0 and q4 == 0, sx["pi"] == sx["npair"] - 1 and q4 == 3,
                           [bPTs[pt], bringV[rb]], [bmisc[ob]])
                    if sx["pi"] == sx["npair"] - 1:
                        if sx["kc"] == 0:
                            CPY("dve", Oacc[t][:, :], Ops, [bmisc[ob]], [bOacc[t]])
                        else:
                            TTO("dve", Oacc[t][:, :], Oacc[t][:, :], Ops, ALU.add, [bmisc[ob], bOacc[t]], [bOacc[t]])

                ns = len(steps)
                for i in range(ns + 2):
                    if i < ns:
                        stA(steps[i])
                    if 0 <= i - 1 < ns:
                        stC(steps[i - 1])
                    if 0 <= i - 2 < ns:
                        stE(steps[i - 2])
                for t in range(4):
                    tau = m * 4 + t
                    P.op("dve", lambda e, o=st[:, 4 + t:5 + t], i=rs[t][:, 0:2 * tau + 2]:
                         e.tensor_reduce(out=o, in_=i, axis=AX.X, op=ALU.add), [brs[t]], [B["st"]])
                    P.op("dve", lambda e, o=st[:, 8 + t:9 + t], i=st[:, 4 + t:5 + t]: e.reciprocal(out=o, in_=i),
                         [B["st"]], [B["st"]])
                    TS("dve", Av[:, t, h * 128:(h + 1) * 128], Oacc[t][:, :], st[:, 8 + t:9 + t], None, ALU.mult, None,
                       [bOacc[t], B["st"]] + allYA[4:8], [bA[t]] + allYA[4:8])
            for h in range(NH):
                tb = rot("tr", 2)
                for t in range(4):
                    TR(tr[tb][:, t * 128:(t + 1) * 128], Av[:, t, h * 128:(h + 1) * 128], ident[:, :],
                       [bA[t], B["ident"]] + allYA[4:8], [btr[tb]])
                CPY(evac_eng(), QT[:, h, :], tr[tb][:, 0:512], [btr[tb]], [bQT[h]])
            for ds in range(4):
                rw = fetch("slab", 10 + ds)
                wo_ = ring[rw][:, :].rearrange("p (c f) -> p c f", c=16)
                for t in range(4):
                    mb = rot("mm", 3)
                    for ic in range(16):
                        lhs = QT[:, ic, t * 128:(t + 1) * 128] if ic < 8 else CT[:, ic - 8, t * 128:(t + 1) * 128]
                        MM(mm[mb][:, :], lhs, wo_[:, ic, :], ic == 0, ic == 15,
                           [bQT[ic] if ic < 8 else bCT[ic - 8]] + bring[rw], [bmm[mb]])
                    TTO("dve", H[t][:, ds * 512:(ds + 1) * 512], H[t][:, ds * 512:(ds + 1) * 512], mm[mb][:, :], ALU.add,
                        [bH[t], bmm[mb]], [bH[t]])
            for t in range(4):
                norm_transpose(H[t], bH[t], 128, t * 128)
            rq = fetch("slab", 14)
            wqc = ring[rq][:, :].rearrange("p (c f) -> p c f", c=16)
            for h4 in range(4):
                mb = rot("mm", 3)
                for dc in range(DC):
                    MM(mm[mb][:, :], wqc[:, dc, h4 * 128:(h4 + 1) * 128], nT[:, dc, 0:512], dc == 0, dc == DC - 1,
                       bring[rq] + [bnT], [bmm[mb]])
                CPY(evac_eng(), QT[:, h4, :], mm[mb][:, :], [bmm[mb]], [bQT[h4]])
            for t in range(4):
                MSET("pool", rs[t][:, 0:4], 0.0, [brs[t]])
                for hp in range(2):
                    sb_ = rot("sp", 2)
                    pb = rot("pb", 2)
                    for hh in range(2):
                        h4 = hp * 2 + hh
                        MM(sp_[sb_][:, hh * 256:(hh + 1) * 256], QT[:, h4, t * 128:(t + 1) * 128], kcT[:, h4, :], True, True,
                           [bQT[h4], B["kcT"]], [bsp[sb_]])
                    for hh in range(2):
                        h4 = hp * 2 + hh
                        ACTV(Pb[pb][:, hh * 256:(hh + 1) * 256], sp_[sb_][:, hh * 256:(hh + 1) * 256], AF.Exp,
                             [bsp[sb_], brs[t], B["zcol"]], [bPb[pb], brs[t]], bias=zcol[:, 0:1], scale=SCALE,
                             accum=rs[t][:, h4:h4 + 1])
                    tb = rot("tr", 2)
                    for q4 in range(4):
                        TR(tr[tb][:, q4 * 128:(q4 + 1) * 128], Pb[pb][:, q4 * 128:(q4 + 1) * 128], ident[:, :],
                           [bPb[pb], B["ident"]], [btr[tb]])
                    pt = rot("pt", 2)
                    CPY("dve", PTs[pt][:, :], tr[tb][:, 0:512], [btr[tb]], [bPTs[pt]])
                    for hh in range(2):
                        h4 = hp * 2 + hh
                        ob = rot("misc", 4)
                        Ops = misc[:, ob * 128:(ob + 1) * 128]
                        for mc in range(2):
                            MM(Ops, PTs[pt][:, (hh * 2 + mc) * 128:(hh * 2 + mc + 1) * 128], vcs[:, mc, h4 * 128:(h4 + 1) * 128],
                               mc == 0, mc == 1, [bPTs[pt], B["vcs"]], [bmisc[ob]])
                        P.op("dve", lambda e, o=st[:, 12:13], i=rs[t][:, h4:h4 + 1]: e.reciprocal(out=o, in_=i),
                             [brs[t]], [B["st"]])
                        TS("dve", Av[:, t, h4 * 128:(h4 + 1) * 128], Ops, st[:, 12:13], None, ALU.mult, None,
                           [bmisc[ob], B["st"]] + allYA[4:8], [bA[t]] + allYA[4:8])
            for h4 in range(4):
                tb = rot("tr", 2)
                for t in range(4):
                    TR(tr[tb][:, t * 128:(t + 1) * 128], Av[:, t, h4 * 128:(h4 + 1) * 128], ident[:, :],
                       [bA[t], B["ident"]] + allYA[4:8], [btr[tb]])
                CPY(evac_eng(), QT[:, 4 + h4, :], tr[tb][:, 0:512], [btr[tb]], [bQT[4 + h4]])
            rw = fetch("slab", 17)
            woc = ring[rw][:, :].rearrange("p (c f) -> p c f", c=4)
            for t in range(4):
                for ds in range(4):
                    mb = rot("mm", 3)
                    for ic in range(4):
                        MM(mm[mb][:, :], QT[:, 4 + ic, t * 128:(t + 1) * 128], woc[:, ic, ds * 512:(ds + 1) * 512],
                           ic == 0, ic == 3, [bQT[4 + ic]] + bring[rw], [bmm[mb]])
                    TTO("dve", H[t][:, ds * 512:(ds + 1) * 512], H[t][:, ds * 512:(ds + 1) * 512], mm[mb][:, :], ALU.add,
                        [bH[t], bmm[mb]], [bH[t]])
            for t in range(4):
                norm_transpose(H[t], bH[t], 128, t * 128)
            for fh in range(2):
                for sl in range(8):
                    rw = fetch("slab", 18 + fh * 8 + sl)
                    w1_ = ring[rw][:, :].rearrange("p (c f) -> p c f", c=16)
                    for f4 in range(4):
                        fl = sl * 4 + f4
                        mb = rot("mm", 3)
                        for dc in range(DC):
                            MM(mm[mb][:, :], w1_[:, dc, f4 * 128:(f4 + 1) * 128], nT[:, dc, 0:512], dc == 0, dc == DC - 1,
                               bring[rw] + [bnT], [bmm[mb]])
                        ri = rot("rt", 2)
                        ACTV(rtmp[ri][:, :], mm[mb][:, :], AF.Relu, [bmm[mb]], [brtmp[ri]])
                        dst = h1a[:, fl, :] if fl < 16 else h1b[:, fl - 16, :]
                        dbuf = allYA if fl < 16 else allGA
                        TTO("pool", dst, rtmp[ri][:, :], rtmp[ri][:, :], ALU.mult, [brtmp[ri]] + dbuf, dbuf)
                for ds in range(4):
                    r2 = [fetch("slab", 34 + ds * 4 + fh * 2 + q) for q in range(2)]
                    for t in range(4):
                        mb = rot("mm", 3)
                        for fl in range(32):
                            src = h1a[:, fl, t * 128:(t + 1) * 128] if fl < 16 else h1b[:, fl - 16, t * 128:(t + 1) * 128]
                            w2_ = ring[r2[fl // 16]][:, :].rearrange("p (c f) -> p c f", c=16)
                            MM(mm[mb][:, :], src, w2_[:, fl % 16, :], fl == 0, fl == 31,
                               (allYA if fl < 16 else allGA) + bring[r2[fl // 16]], [bmm[mb]])
                        TTO("dve", H[t][:, ds * 512:(ds + 1) * 512], H[t][:, ds * 512:(ds + 1) * 512], mm[mb][:, :], ALU.add,
                            [bH[t], bmm[mb]], [bH[t]])
            DMA(gfin_v, gfin_d[:, :], allGA, allGA)
            for t in range(4):
                MSET("pool", st[:, 0:1], 0.0, [B["st"]])
                ACTV(xs[:, :], H[t][:, :], AF.Square, [bH[t], B["st"]], bxs_all + [B["st"]], accum=st[:, 0:1])
                ACTV(st[:, 1:2], st[:, 0:1], AF.Sqrt, [B["st"], B["zcol"]], [B["st"]], bias=epsc[:, 0:1], scale=1.0 / D)
                P.op("dve", lambda e, o=st[:, 1:2], i=st[:, 1:2]: e.reciprocal(out=o, in_=i), [B["st"]], [B["st"]])
                STT("dve", H[t][:, :], H[t][:, :], st[:, 1:2], gfin_v, ALU.mult, ALU.mult,
                    [bH[t], B["st"]] + allGA, [bH[t]])
                DMA(out_d[m * 4 + t, :, :], H[t][:, :], [bH[t]], [], queue="pool")

        def norm_transpose_from(X, bXl, tokoff):
            MSET("pool", st[:, 0:1], 0.0, [B["st"]])
            ACTV(xs[:, :], X, AF.Square, bXl + [B["st"]], bxs_all + [B["st"]], accum=st[:, 0:1])
            ACTV(st[:, 1:2], st[:, 0:1], AF.Sqrt, [B["st"], B["zcol"]], [B["st"]], bias=epsc[:, 0:1], scale=1.0 / D)
            P.op("dve", lambda e, o=st[:, 1:2], i=st[:, 1:2]: e.reciprocal(out=o, in_=i), [B["st"]], [B["st"]])
            TS("dve", xs[:, :], X, st[:, 1:2], None, ALU.mult, None, bXl + [B["st"]], bxs_all)
            for half in range(2):
                tb = rot("tr", 2)
                for c in range(8):
                    dc = half * 8 + c
                    TR(tr[tb][:, c * 128:(c + 1) * 128], xs[:, dc * 128:(dc + 1) * 128], ident[:, :],
                       bxs_all + [B["ident"]], [btr[tb]])
                CPY(evac_eng(), nTh[:, half * 8:(half + 1) * 8, tokoff:tokoff + 128],
                    tr[tb][:, :].rearrange("p (c t) -> p c t", c=8), [btr[tb]], bCT)

        P.dry = True
        saved = dict(cnt)
        emit_all()
        P.dry = False
        cnt.update(saved)
        emit_all()
        assert stream_state["pos"] == len(plan)
        P.emit(nc)
    return nc


def _t5_bucket_np(d):
    d = np.asarray(d, dtype=np.int64)
    max_exact = 16
    nf = np.maximum(d, max_exact).astype(np.float32)
    large = max_exact + (np.log(nf / np.float32(max_exact)) / np.float32(math.log(2048 / max_exact))
                         * np.float32(16)).astype(np.int32)
    large = np.minimum(large, 31)
    return np.where(d < max_exact, d, large)


def _slab(W, f0):
    return np.ascontiguousarray(W[:, f0:f0 + 512].reshape(16, 128, 512).transpose(1, 0, 2)).reshape(128, 8192)


def _prep_shared(inp):
    w_in = inp["w_in"][0]
    slabs = [_slab(w_in, f0) for f0 in range(0, 5120, 512)]
    slabs += [_slab(inp["w_out"][0], f0) for f0 in range(0, 2048, 512)]
    slabs.append(_slab(inp["wq_c"][0], 0))
    slabs.append(_slab(inp["wk_c"][0], 0))
    slabs.append(_slab(inp["wv_c"][0], 0))
    slabs.append(np.ascontiguousarray(inp["wo_c"][0].reshape(4, 128, 2048).transpose(1, 0, 2)).reshape(128, 8192))
    w1 = inp["w1"][0]
    slabs += [_slab(w1, f0) for f0 in range(0, 8192, 512)]
    w2 = inp["w2"][0]
    for ds in range(4):
        for fq in range(4):
            slabs.append(_slab(w2[fq * 2048:(fq + 1) * 2048], ds * 512))
    wall = np.stack(slabs).astype(np.float32)
    pc = lambda v: np.ascontiguousarray(v.reshape(-1, 128).T)
    gvec = np.concatenate([pc(inp["g_mix"][0]), pc(inp["g_cross"][0]), pc(inp["g_mem"][0]), pc(inp["g_mlp"][0])], axis=1)
    gfin = np.ascontiguousarray(np.broadcast_to(inp["g_final"][None, :], (128, D)))
    cw = np.ascontiguousarray(inp["conv_w"][0].reshape(KCONV, 8, 128).transpose(2, 1, 0)).reshape(128, 8 * KCONV)
    cp = np.concatenate([pc(inp["conv_b"][0]), pc(inp["conv_ln_g"][0]), pc(inp["conv_ln_b"][0])], axis=1)
    rel = inp["rel_bias"]
    relaug = np.concatenate([rel, np.full((1, 8), NEG, np.float32)], axis=0)
    c31 = np.ascontiguousarray(np.broadcast_to(rel[31][None, :], (128, 8)))
    return dict(wall=wall, gvec=gvec.astype(np.float32), gfin=gfin.astype(np.float32), convw=cw.astype(np.float32),
                convp=cp.astype(np.float32), relaug=relaug.astype(np.float32), c31=c31.astype(np.float32))


def _oh_table(r):
    oh = np.zeros((33, LFP), np.float32)
    y = np.arange(LFP)
    x = y + r * 64
    neg = x < 511
    oh[32, neg] = 1.0
    d = np.maximum(x - 511, 0)
    bk = _t5_bucket_np(d)
    oh[bk[~neg], y[~neg]] = 1.0
    return oh


def _core_inputs(inp, b, r, NBLK, shared, xa_cache):
    S = NBLK * 256
    x = inp["x"][b]
    xb = x.reshape(NBLK, 256, D)
    own = xb[:, r * 64:(r + 1) * 64, :]
    xo = np.ascontiguousarray(own.reshape(NBLK // 2, 128, D))
    xpad = np.concatenate([np.zeros((32, D), np.float32), x], axis=0)
    starts = (np.arange(NBLK) * 256 + r * 64)
    idx = starts[:, None] + np.arange(32)[None, :]
    xh = np.ascontiguousarray(xpad[idx].reshape(NBLK // 4, 128, D))
    if b not in xa_cache:
        xa_cache[b] = np.ascontiguousarray(xb[:, ::-1, :].reshape(S // 128, 128, D))
    d = dict(shared)
    d.update(xo=xo, xh=xh, xa=xa_cache[b], mem=np.ascontiguousarray(inp["mem"][b].reshape(2, 128, D)), oh=_oh_table(r))
    return d


_NC_CACHE = {}


def kernel(**inputs):
    inp = {k: np.asarray(v) for k, v in inputs.items()}
    Bn, S, _ = inp["x"].shape
    NBLK = S // 256
    shared = _prep_shared(inp)
    xa_cache = {}
    in_maps = []
    for b in range(Bn):
        for r in range(4):
            in_maps.append(_core_inputs(inp, b, r, NBLK, shared, xa_cache))
    if NBLK not in _NC_CACHE:
        _NC_CACHE[NBLK] = build_program(NBLK)
    nc = _NC_CACHE[NBLK]
    res = run_bass_kernel_spmd(nc, in_maps, core_ids=list(range(len(in_maps))))
    out = np.zeros((Bn, S, D), np.float32)
    ov = out.reshape(Bn, NBLK, 256, D)
    for b in range(Bn):
        for r in range(4):
            o = np.asarray(res.results[b * 4 + r]["out"]).reshape(NBLK, 64, D)
            ov[b, :, r * 64:(r + 1) * 64, :] = o
    return out
```

```python
import contextlib
import math
import numpy as np
import concourse.bass as bass
import concourse.mybir as mybir
from concourse.bass_utils import run_bass_kernel_spmd

F32 = mybir.dt.float32
BF16 = mybir.dt.bfloat16
ALU = mybir.AluOpType
AF = mybir.ActivationFunctionType
AX = mybir.AxisListType

D = 2048
DC = 16
NH = 8
NMEM = 256
KCONV = 31
EPS = 1e-6
NEG = -1000.0
SCALE = 128.0 ** -0.5
SQ = 128.0 ** 0.5
LFP = 2432
NSLAB = 50
RING = 4

COMPUTE = ("pe", "act", "dve", "pool")
NDMASEM = 14


class Buf:
    __slots__ = ("name", "lw", "rd")

    def __init__(self, name):
        self.name = name
        self.lw = None
        self.rd = []


class Ins:
    __slots__ = ("eng", "fn", "deps", "idx", "sig", "seq", "dsem", "dcnt", "waits", "src")

    def __init__(self, eng, fn):
        self.eng = eng
        self.fn = fn
        self.deps = []
        self.idx = -1
        self.sig = False
        self.seq = 0
        self.dsem = -1
        self.dcnt = 0
        self.waits = []
        self.src = None


class Prog:
    def __init__(self, same_engine_sync=True):
        self.streams = {e: [] for e in COMPUTE + ("sp",)}
        self.order = []
        self.same = same_engine_sync
        self.ndma = 0
        self.dma_last = [None] * NDMASEM
        self.dma_cnt = [0] * NDMASEM
        self.dry = False

    def _track(self, ins, reads, writes):
        deps = {}
        for b in reads:
            if b.lw is not None:
                deps[id(b.lw)] = b.lw
        for b in writes:
            if b.lw is not None:
                deps[id(b.lw)] = b.lw
            for r in b.rd:
                deps[id(r)] = r
        for b in reads:
            b.rd.append(ins)
        for b in writes:
            b.lw = ins
            b.rd = []
        ins.deps = list(deps.values())

    def op(self, eng, fn, reads=(), writes=()):
        if self.dry:
            return None
        ins = Ins(eng, fn)
        ins.src = eng
        self._track(ins, reads, writes)
        ins.idx = len(self.streams[eng])
        self.streams[eng].append(ins)
        self.order.append(ins)
        return ins

    def dma(self, fn, reads=(), writes=(), queue="sp"):
        if self.dry:
            return None
        ins = Ins(queue, fn)
        k = self.ndma % NDMASEM
        self.ndma += 1
        ins.dsem = k
        self.dma_cnt[k] += 1
        ins.dcnt = self.dma_cnt[k]
        ins.src = "d%d" % k
        self._track(ins, reads, writes)
        if self.dma_last[k] is not None:
            ins.deps.append(self.dma_last[k])
        self.dma_last[k] = ins
        ins.idx = len(self.streams[queue])
        self.streams[queue].append(ins)
        self.order.append(ins)
        return ins

    def analyze(self):
        vdone = {}
        last_start = {e: {} for e in self.streams}
        for ins in self.order:
            vc = dict(last_start[ins.eng])
            for d in ins.deps:
                if d.dsem >= 0:
                    src, val = d.src, d.dcnt
                else:
                    src, val = d.eng, d.idx + 1
                    if d.eng == ins.eng and ins.dsem < 0 and (d.eng == "pe" or not self.same):
                        continue
                if vc.get(src, 0) >= val:
                    continue
                d.sig = True
                ins.waits.append(d)
                for s, v in vdone[id(d)].items():
                    if vc.get(s, 0) < v:
                        vc[s] = v
            last_start[ins.eng] = vc
            vd = dict(vc)
            if ins.dsem >= 0:
                vd[ins.src] = ins.dcnt
            else:
                vd[ins.eng] = ins.idx + 1
            vdone[id(ins)] = vd
            ins.deps = None
        for ins in self.order:
            if len(ins.waits) > 1:
                best = {}
                for d in ins.waits:
                    val = d.dcnt if d.dsem >= 0 else d.idx
                    if d.src not in best or val > best[d.src][0]:
                        best[d.src] = (val, d)
                ins.waits = [v[1] for v in best.values()]
        for e in COMPUTE:
            n = 0
            for ins in self.streams[e]:
                if ins.dsem < 0 and ins.sig:
                    n += 1
                    ins.seq = n

    def emit(self, nc, final_wait_queue="sp"):
        self.analyze()
        with contextlib.ExitStack() as es:
            sems = {e: es.enter_context(nc.semaphore("s_" + e)) for e in COMPUTE}
            dsems = [es.enter_context(nc.semaphore("dq%d" % k)) for k in range(NDMASEM)]
            block = es.enter_context(nc.Block())

            def run(engname, eng):
                for ins in self.streams[engname]:
                    for d in ins.waits:
                        if d.dsem >= 0:
                            eng.wait_ge(dsems[d.dsem], 16 * d.dcnt)
                        else:
                            eng.wait_ge(sems[d.eng], d.seq)
                    r = ins.fn(eng)
                    if ins.dsem >= 0:
                        r.then_inc(dsems[ins.dsem], 16)
                    elif ins.sig:
                        r.then_inc(sems[ins.eng], 1)
                if engname == final_wait_queue:
                    for k in range(NDMASEM):
                        if self.dma_cnt[k]:
                            eng.wait_ge(dsems[k], 16 * self.dma_cnt[k])

            @block.tensor
            def _(e):
                run("pe", e)

            @block.scalar
            def _(e):
                run("act", e)

            @block.vector
            def _(e):
                run("dve", e)

            @block.gpsimd
            def _(e):
                run("pool", e)

            @block.sync
            def _(e):
                run("sp", e)


def build_program(NBLK, debug=False):
    S = NBLK * 256
    NG = NBLK // 8
    NT = NBLK // 2
    NAG = NBLK // 2
    nc = bass.Bass("TRN2", target_bir_lowering=False)
    dt_in = lambda name, shape: nc.dram_tensor(name, shape, F32, kind="ExternalInput").ap()
    xo_d = dt_in("xo", [NT, 128, D])
    xh_d = dt_in("xh", [NG * 2, 128, D])
    xa_d = dt_in("xa", [S // 128, 128, D])
    mem_d = dt_in("mem", [2, 128, D])
    wall_d = dt_in("wall", [NSLAB, 128, 8192])
    gv_d = dt_in("gvec", [128, 64])
    gfin_d = dt_in("gfin", [128, D])
    cw_d = dt_in("convw", [128, 8 * KCONV])
    cp_d = dt_in("convp", [128, 24])
    rel_d = dt_in("relaug", [33, 8])
    c31_d = dt_in("c31", [128, 8])
    oh_d = dt_in("oh", [33, LFP])
    out_d = nc.dram_tensor("out", [NT, 128, D], F32, kind="ExternalOutput").ap()
    wbf_d = nc.dram_tensor("wbf", [NSLAB, 128, 8192], BF16, kind="Internal").ap()
    kt_d = nc.dram_tensor("ktd", [NH, 128, S], BF16, kind="Internal").ap()
    v_d = nc.dram_tensor("vd", [NH, 128, S // 128, 128], BF16, kind="Internal").ap()
    cd_d = nc.dram_tensor("cdd", [4, 128, 8192], BF16, kind="Internal").ap()
    fp_h = nc.dram_tensor("fpd", [NH, LFP], BF16, kind="Internal")
    fp_d = fp_h.ap()

    P = Prog()
    es = contextlib.ExitStack()
    with es:
        def SB(name, shape, dt):
            return es.enter_context(nc.sbuf_tensor("sb_" + name, shape, dt))

        def PS(name, shape, dt):
            return es.enter_context(nc.psum_tensor("ps_" + name, shape, dt))

        H = [SB("H%d" % i, [128, D], F32) for i in range(4)]
        bH = [Buf("H%d" % i) for i in range(4)]
        ring = [SB("ring%d" % i, [128, 8192], BF16) for i in range(RING)]
        bringK = [Buf("ringK%d" % i) for i in range(RING)]
        bringV = [Buf("ringV%d" % i) for i in range(RING)]
        bring = [[bringK[i], bringV[i]] for i in range(RING)]
        bcd = [Buf("cd%d" % j) for j in range(4)]
        nT2 = SB("nT", [128, DC * 512], BF16)
        nT = nT2[:, :].rearrange("p (c t) -> p c t", c=DC)
        bnT = Buf("nT")
        QT2 = SB("QT", [128, NH * 512], BF16)
        QT = QT2[:, :].rearrange("p (h t) -> p h t", h=NH)
        bQT = [Buf("QT%d" % h) for h in range(NH)]
        CT2 = SB("CT", [128, 8 * 512], BF16)
        CT = CT2[:, :].rearrange("p (c t) -> p c t", c=8)
        nTh = CT2[:, :].rearrange("p (c t) -> p c t", c=DC)
        bCT = [Buf("CT%d" % c) for c in range(8)]
        GA = SB("GA", [128, 6144], F32)
        YA = SB("YA", [128, 4096], F32)
        bGA = [Buf("GA%d" % c) for c in range(8)]
        bYA = [Buf("YA%d" % c) for c in range(8)]
        xs = SB("xs", [128, D], BF16)
        bxs = Buf("xs")
        ident = SB("ident", [128, 128], BF16)
        onesf = SB("onesf", [128, 128], F32)
        gvec = SB("gvec", [128, 64], F32)
        cw = SB("cw", [128, 8 * KCONV], F32)
        cp = SB("cp", [128, 24], F32)
        c31m = SB("c31m", [128, 8], F32)
        relaug = SB("relaug", [33, 8], F32)
        ohs = YA[0:33, 0:LFP]
        fps = GA[:, :].bitcast(BF16)[0:8, 0:LFP]
        ksum2 = SB("ksum", [128, NH * NBLK], F32)
        kb322 = SB("kb32", [128, NH * NBLK], F32)
        ksum = ksum2[:, :].rearrange("p (h k) -> p h k", h=NH)
        kb32 = kb322[:, :].rearrange("p (h k) -> p h k", h=NH)
        bmv2 = SB("bmv", [128, 2048], F32)
        khi = SB("khi", [128, NH, NBLK], BF16)
        klo = SB("klo", [128, NH, NBLK], BF16)
        kcT = SB("kcT", [128, 4, 256], BF16)
        vcs = SB("vcs", [128, 2, 512], BF16)
        TT = SB("TT", [128, NH, 136], F32)
        st = SB("st", [128, 16], F32)
        sgt = SB("sgt", [128, 1024], F32)
        mean_sb = sgt[:, 0:512]
        rstd_sb = sgt[:, 512:1024]
        gsb = [SB("gsb%d" % i, [128, 64], F32) for i in range(2)]
        m30 = [SB("m30%d" % i, [128, 64], F32) for i in range(2)]
        mx8 = [SB("mx8%d" % i, [128, 8], F32) for i in range(2)]
        rs = [SB("rs%d" % i, [128, 64], F32) for i in range(4)]
        Oacc = [SB("Oacc%d" % i, [128, 128], F32) for i in range(4)]
        Pb = [xs[:, i * 512:(i + 1) * 512] for i in range(2)]
        PTs = [xs[:, 1024 + i * 512:1024 + (i + 1) * 512] for i in range(2)]
        rtmp = [sgt[:, i * 512:(i + 1) * 512] for i in range(2)]
        zcol = SB("zcol", [128, 1], F32)
        dummy = SB("dummy", [128, 1], F32)
        epsc = SB("epsc", [128, 1], F32)
        B = {n: Buf(n) for n in ("ident onesf gvec cw cp c31m relaug khi klo kcT vcs TT st sgt zcol fpd").split()}
        B["mean_sb"] = B["sgt"]
        B["rstd_sb"] = B["sgt"]
        B["ohs"] = bYA[0]
        B["fps"] = bGA[0]
        B["ksum"] = Buf("ksum")
        B["kb32"] = Buf("kb32")
        pstg = [Buf("pstg%d" % i) for i in range(4)]
        pout = [Buf("pout%d" % i) for i in range(2)]
        B["dummy"] = Buf("dummy")
        brtmp = [B["sgt"], B["sgt"]]
        bgsb = [Buf("gsb%d" % i) for i in range(2)]
        bm30 = [Buf("m30%d" % i) for i in range(2)]
        bmx8 = [Buf("mx8%d" % i) for i in range(2)]
        brs = [Buf("rs%d" % i) for i in range(4)]
        bOacc = [Buf("Oacc%d" % i) for i in range(4)]
        bPb = [Buf("Pb%d" % i) for i in range(2)]
        bPTs = [Buf("PTs%d" % i) for i in range(2)]
        bxs_all = [bxs] + bPb + bPTs
        bctmp = [Buf("ctmp0"), Buf("ctmp1")]
        bHK = [Buf("HK%d" % h) for h in range(2)]
        bbm = [[Buf("bm%d_%d" % (h, t)) for t in range(4)] for h in range(NH)]
        bA = [Buf("A%d" % t) for t in range(4)]
        bwd = [Buf("wd%d" % s) for s in range(NSLAB)]
        bkv = [Buf("kv%d" % g) for g in range(NAG)]
        G4 = GA[:, :].bitcast(BF16)[:, 0:6144].rearrange("p (c q t) -> p c q t", c=8, q=8)
        Y3 = YA[:, :].rearrange("p (c t) -> p c t", c=8)
        Y4 = YA[:, :].rearrange("p (c q t) -> p c q t", c=8, q=8)
        GAb = GA[:, :].bitcast(BF16)
        YAb = YA[:, :].bitcast(BF16)
        HK = GAb[:, 0:2 * 8 * 256].rearrange("p (h s k) -> p h s k", h=2, s=8)
        bmv = bmv2[:, :].rearrange("p (h t k) -> p h t k", h=NH, t=4)
        Av = YAb[:, 4096:8192].rearrange("p (t f) -> p t f", t=4)
        h1a = YAb[:, :].rearrange("p (c t) -> p c t", c=16)
        h1b = GAb[:, 0:8192].rearrange("p (c t) -> p c t", c=16)
        gfin_v = GA[:, 4096:6144]
        mm = [PS("mm%d" % i, [128, 512], F32) for i in range(3)]
        bmm = [Buf("mm%d" % i) for i in range(3)]
        tr = [PS("tr%d" % i, [128, 1024], BF16) for i in range(2)]
        btr = [Buf("tr%d" % i) for i in range(2)]
        sp_ = [PS("sps%d" % i, [128, 512], F32) for i in range(2)]
        bsp = [Buf("sps%d" % i) for i in range(2)]
        misc = PS("misc", [128, 512], F32)
        bmisc = [Buf("misc%d" % i) for i in range(4)]

        print("SBUF bytes remaining per partition:", nc.sbuf_bytes_remaining)
        cnt = {"mm": 0, "tr": 0, "sp": 0, "ev": 0, "misc": 0, "pb": 0, "pt": 0, "g": 0, "rt": 0}

        def rot(key, n):
            v = cnt[key] % n
            cnt[key] += 1
            return v

        def MM(out, lhsT, rhs, start, stop, reads, writes):
            P.op("pe", lambda e, o=out, l=lhsT, r=rhs, s=start, t=stop: e.matmul(o, lhsT=l, rhs=r, start=s, stop=t),
                 reads, writes)

        def TR(out, in_, idn, reads, writes):
            P.op("pe", lambda e, o=out, i=in_, d=idn: e.transpose(out=o, in_=i, identity=d), reads, writes)

        def ACTV(out, in_, func, reads, writes, bias=None, scale=1.0, accum=None):
            def f(e, o=out, i=in_, fn=func, b=bias, s=scale, a=accum):
                kw = {}
                if b is not None:
                    kw["bias"] = b
                if a is not None:
                    kw["accum_out"] = a
                return e.activation(out=o, in_=i, func=fn, scale=s, **kw)
            P.op("act", f, reads, writes)

        def CPY(eng, out, in_, reads, writes):
            if eng == "act":
                P.op("act", lambda e, o=out, i=in_: e.copy(out=o, in_=i), reads, writes)
            else:
                P.op(eng, lambda e, o=out, i=in_: e.tensor_copy(out=o, in_=i), reads, writes)

        def TS(eng, out, in0, s1, s2, op0, op1, reads, writes):
            if op1 is None:
                P.op(eng, lambda e, o=out, i=in0, a=s1, p0=op0: e.tensor_scalar(out=o, in0=i, scalar1=a, scalar2=None, op0=p0),
                     reads, writes)
            else:
                P.op(eng, lambda e, o=out, i=in0, a=s1, b=s2, p0=op0, p1=op1:
                     e.tensor_scalar(out=o, in0=i, scalar1=a, scalar2=b, op0=p0, op1=p1), reads, writes)

        def TTO(eng, out, in0, in1, op, reads, writes):
            P.op(eng, lambda e, o=out, a=in0, b=in1, p=op: e.tensor_tensor(out=o, in0=a, in1=b, op=p), reads, writes)

        def STT(eng, out, in0, sc, in1, op0, op1, reads, writes):
            P.op(eng, lambda e, o=out, a=in0, s=sc, b=in1, p0=op0, p1=op1:
                 e.scalar_tensor_tensor(out=o, in0=a, scalar=s, in1=b, op0=p0, op1=p1), reads, writes)

        def MSET(eng, out, val, writes):
            P.op(eng, lambda e, o=out, v=val: e.memset(o, v), (), writes)

        def DMA(out, in_, reads, writes, queue="sp"):
            P.dma(lambda e, o=out, i=in_: e.dma_start(out=o, in_=i), reads, writes, queue)

        def evac_eng():
            return ("act", "dve")[rot("ev", 2)]

        plan = []
        stream_state = {"next_issue": 0, "pos": 0}

        def issue_item(i):
            kind, args = plan[i]
            rb = i % RING
            dst = ring[rb]
            if kind == "slab":
                s = args
                DMA(dst[:, :], wbf_d[s, :, :], [bwd[s]], bring[rb])
            elif kind == "cdiag":
                DMA(dst[:, :], cd_d[args, :, :], [bcd[args]], bring[rb])
            else:
                h, b0, nb = args
                DMA(dst[:, 0:nb * 256], kt_d[h, :, b0 * 256:(b0 + nb) * 256],
                    [bkv[g] for g in range(b0 // 2, (b0 + nb + 1) // 2)], [bringK[rb]])
                DMA(dst[:, 4096:4096 + nb * 256].rearrange("p (c d) -> p c d", d=128),
                    v_d[h, :, b0 * 2:(b0 + nb) * 2, :],
                    [bkv[g] for g in range(b0 // 2, (b0 + nb + 1) // 2)], [bringV[rb]])

        def fetch(kind, args):
            if P.dry:
                plan.append((kind, args))
                return 0
            i = stream_state["pos"]
            stream_state["pos"] += 1
            assert plan[i] == (kind, args), (plan[i], kind, args)
            while stream_state["next_issue"] < min(len(plan), i + RING - 2) or stream_state["next_issue"] <= i:
                issue_item(stream_state["next_issue"])
                stream_state["next_issue"] += 1
            return i % RING

        def norm_transpose(X, bX, npart, tokoff):
            MSET("dve", st[0:npart, 0:1], 0.0, [B["st"]])
            ACTV(xs[0:npart, :], X[0:npart, :], AF.Square, [bX, B["st"]], bxs_all + [B["st"]], accum=st[0:npart, 0:1])
            ACTV(st[0:npart, 1:2], st[0:npart, 0:1], AF.Sqrt, [B["st"], B["zcol"]], [B["st"]], bias=epsc[0:npart, 0:1], scale=1.0 / D)
            P.op("dve", lambda e, o=st[0:npart, 1:2], i=st[0:npart, 1:2]: e.reciprocal(out=o, in_=i), [B["st"]], [B["st"]])
            TS("dve", xs[0:npart, :], X[0:npart, :], st[0:npart, 1:2], None, ALU.mult, None, [bX, B["st"]], bxs_all)
            for half in range(2):
                tb = rot("tr", 2)
                for c in range(8):
                    dc = half * 8 + c
                    TR(tr[tb][:, c * 128:c * 128 + npart], xs[0:npart, dc * 128:(dc + 1) * 128], ident[0:npart, 0:npart],
                       bxs_all + [B["ident"]], [btr[tb]])
                src = tr[tb][:, :].rearrange("p (c t) -> p c t", c=8)[:, :, 0:npart]
                CPY(evac_eng(), nT[:, half * 8:(half + 1) * 8, tokoff:tokoff + npart], src, [btr[tb]], [bnT])

        allGA_ = bGA
        allYA_ = bYA

        def emit_all():
            MSET("pool", ident[:, :], 1.0, [B["ident"]])
            P.op("pool", lambda e: e.affine_select(out=ident[:, :], in_=ident[:, :], pattern=[[-1, 128]],
                                                   compare_op=ALU.is_equal, fill=0.0, base=0, channel_multiplier=1),
                 [B["ident"]], [B["ident"]])
            MSET("pool", onesf[:, :], 1.0 / 1024.0, [B["onesf"]])
            MSET("pool", zcol[:, :], 0.0, [B["zcol"]])
            MSET("pool", epsc[:, :], EPS, [B["zcol"]])
            DMA(gvec[:, :], gv_d[:, :], [], [B["gvec"]])
            DMA(cw[:, :], cw_d[:, :], [], [B["cw"]])
            DMA(cp[:, :], cp_d[:, :], [], [B["cp"]])
            DMA(c31m[:, :], c31_d[:, :], [], [B["c31m"]])
            DMA(relaug[:, :], rel_d[:, :], [], [B["relaug"]])
            DMA(ohs[:, :], oh_d[:, :], [], [B["ohs"]])
            for c0 in range(0, LFP, 512):
                w = min(512, LFP - c0)
                mb = rot("mm", 3)
                MM(mm[mb][0:8, 0:w], relaug[:, :], ohs[:, c0:c0 + w], True, True, [B["relaug"], B["ohs"]], [bmm[mb]])
                ACTV(fps[:, c0:c0 + w], mm[mb][0:8, 0:w], AF.Copy, [bmm[mb]], [B["fps"]], scale=SQ)
            DMA(fp_d[:, :], fps[:, :], [B["fps"]], [B["fpd"]], queue="pool")
            MSET("pool", TT[:, :, :], 0.0, [B["TT"]])
            for h in range(NH):
                TS("dve", TT[:, h, :], TT[:, h, :], c31m[:, h:h + 1], None, ALU.add, None,
                   [B["TT"], B["c31m"]], [B["TT"]])
            MSET("pool", TT[0:64, :, 58:64], 0.0, [B["TT"]])
            MSET("pool", TT[0:64, :, 64:65], -NEG, [B["TT"]])
            MSET("pool", TT[0:64, :, 65:136], NEG, [B["TT"]])
            MSET("pool", TT[64:128, :, 59:65], 0.0, [B["TT"]])
            MSET("pool", TT[64:128, :, 65:66], -NEG, [B["TT"]])
            MSET("pool", TT[64:128, :, 66:136], NEG, [B["TT"]])

            for j in range(4):
                for ccl in range(2):
                    for k in range(KCONV):
                        cc = j * 2 + ccl
                        di = (ccl * KCONV + k) * 128
                        TS("dve", ring[j][:, di:di + 128], ident[:, :], cw[:, cc * KCONV + k:cc * KCONV + k + 1], None,
                           ALU.mult, None, [B["ident"], B["cw"]], bring[j])
                DMA(cd_d[j, :, 0:2 * KCONV * 128], ring[j][:, 0:2 * KCONV * 128], bring[j], [bcd[j]], queue="pool")
            gcol = {}
            for s_ in range(0, 10):
                gcol[s_] = 0
            gcol[14] = 16
            gcol[15] = 32
            gcol[16] = 32
            for s_ in range(18, 34):
                gcol[s_] = 48
            stg = [GA[:, 0:2048], GA[:, 2048:4096], YA[:, 0:2048], YA[:, 2048:4096]]
            GAb_ = GA[:, :].bitcast(BF16)
            pob = [GAb_[:, 8192:10240], GAb_[:, 10240:12288]]
            pk = {"k": 0}
            P.op("pool", lambda e: e.memset(dummy[:, :], 0.0), [bYA[0], bGA[0]] + allGA_ + allYA_, pstg + pout + [B["dummy"]])

            def prep_item(s_, qs):
                k = pk["k"]
                pk["k"] += 1
                si = k % 4
                oi = k % 2
                DMA(stg[si], wall_d[s_, :, qs * 2048:(qs + 1) * 2048], [], [pstg[si]], queue="pool")
                if s_ in gcol:
                    for dl in range(4):
                        eng = ("dve", "act")[(k + dl) % 2]
                        gc = gvec[:, gcol[s_] + qs * 4 + dl:gcol[s_] + qs * 4 + dl + 1]
                        o = pob[oi][:, dl * 512:(dl + 1) * 512]
                        i_ = stg[si][:, dl * 512:(dl + 1) * 512]
                        if eng == "act":
                            ACTV(o, i_, AF.Copy, [pstg[si], B["gvec"]], [pout[oi]], scale=gc)
                        else:
                            TS(eng, o, i_, gc, None, ALU.mult, None, [pstg[si], B["gvec"]], [pout[oi]])
                else:
                    for q in range(2):
                        eng = ("dve", "act")[(k + q) % 2]
                        CPY(eng, pob[oi][:, q * 1024:(q + 1) * 1024], stg[si][:, q * 1024:(q + 1) * 1024],
                            [pstg[si]], [pout[oi]])
                DMA(wbf_d[s_, :, qs * 2048:(qs + 1) * 2048], pob[oi], [pout[oi]], [bwd[s_]], queue="pool")

            for s_ in (2, 3, 4, 5):
                for qs in range(4):
                    prep_item(s_, qs)
            rest_items = [(s_, qs) for s_ in range(NSLAB) if s_ not in (2, 3, 4, 5) for qs in range(4)]
            per_group = -(-len(rest_items) // NAG) if NAG >= 24 else 6

            MSET("pool", ksum[:, :, :], 0.0, [B["ksum"]])
            for s_i, s in enumerate((2, 3, 4, 5)):
                DMA(ring[s_i][:, :], wbf_d[s, :, :], [bwd[s]], bring[s_i])
            Wk = [ring[0][:, :].rearrange("p (c f) -> p c f", c=16), ring[1][:, :].rearrange("p (c f) -> p c f", c=16)]
            Wv = [ring[2][:, :].rearrange("p (c f) -> p c f", c=16), ring[3][:, :].rearrange("p (c f) -> p c f", c=16)]
            ntile_a = S // 128
            for i in range(min(3, ntile_a)):
                DMA(H[i % 4][:, :], xa_d[i, :, :], [], [bH[i % 4]])
            for ag in range(NAG):
                for _ in range(per_group):
                    if rest_items:
                        prep_item(*rest_items.pop(0))
                for t in range(4):
                    ti = ag * 4 + t
                    if ti + 3 < ntile_a:
                        DMA(H[(ti + 3) % 4][:, :], xa_d[ti + 3, :, :], [], [bH[(ti + 3) % 4]])
                    norm_transpose(H[ti % 4], bH[ti % 4], 128, t * 128)
                for h in range(NH):
                    mb = rot("mm", 3)
                    for dc in range(DC):
                        MM(mm[mb][:, :], Wk[h // 4][:, dc, (h % 4) * 128:(h % 4 + 1) * 128], nT[:, dc, 0:512],
                           dc == 0, dc == DC - 1, bring[h // 4] + [bnT], [bmm[mb]])
                    for bl in range(2):
                        ACTV(QT[:, h, bl * 256:(bl + 1) * 256], mm[mb][:, bl * 256:(bl + 1) * 256], AF.Copy,
                             [bmm[mb], B["ksum"]], [bQT[h], B["ksum"]],
                             accum=ksum[:, h, ag * 2 + bl:ag * 2 + bl + 1])
                for t in range(4):
                    for sl in range(2):
                        mb = rot("mm", 3)
                        for dc in range(DC):
                            MM(mm[mb][:, :], nT[:, dc, t * 128:(t + 1) * 128], Wv[sl][:, dc, :],
                               dc == 0, dc == DC - 1, bring[2 + sl] + [bnT], [bmm[mb]])
                        CPY("dve", CT[:, t * 2 + sl, :], mm[mb][:, :], [bmm[mb]], [bCT[t * 2 + sl]])
                DMA(kt_d[:, :, ag * 512:(ag + 1) * 512].rearrange("h p t -> p h t"), QT[:, :, :],
                    bQT, [bkv[ag]], queue="sp")
                for t in range(4):
                    DMA(v_d[:, :, ag * 4 + t, :].rearrange("h p d -> p h d"),
                        CT2[:, t * 1024:(t + 1) * 1024].rearrange("p (h d) -> p h d", d=128),
                        bCT[2 * t:2 * t + 2], [bkv[ag]], queue="sp")
            while rest_items:
                prep_item(*rest_items.pop(0))
            P.op("pool", lambda e: e.memset(dummy[:, :], 0.0), pstg + pout, pstg + pout + allGA_ + allYA_ + [B["dummy"]])
            TS("dve", kb32[:, :, :], ksum[:, :, :], 1.0 / 256.0, None, ALU.mult, None, [B["ksum"]], [B["kb32"]])
            CPY("dve", khi[:, :, :], kb32[:, :, :], [B["kb32"]], [B["khi"]])
            CPY("dve", ksum[:, :, :], khi[:, :, :], [B["khi"]], [B["ksum"]])
            TTO("dve", klo[:, :, :], kb32[:, :, :], ksum[:, :, :], ALU.subtract, [B["kb32"], B["ksum"]], [B["klo"]])

            for i in range(2):
                DMA(H[i][:, :], mem_d[i, :, :], [], [bH[i]])
                norm_transpose(H[i], bH[i], 128, i * 128)
            rk = fetch("slab", 15)
            wkc = ring[rk][:, :].rearrange("p (c f) -> p c f", c=16)
            for h4 in range(4):
                mb = rot("mm", 3)
                for dc in range(DC):
                    MM(mm[mb][:, 0:256], wkc[:, dc, h4 * 128:(h4 + 1) * 128], nT[:, dc, 0:256], dc == 0, dc == DC - 1,
                       bring[rk] + [bnT], [bmm[mb]])
                CPY(evac_eng(), kcT[:, h4, :], mm[mb][:, 0:256], [bmm[mb]], [B["kcT"]])
            rv = fetch("slab", 16)
            wvc = ring[rv][:, :].rearrange("p (c f) -> p c f", c=16)
            for i in range(2):
                mb = rot("mm", 3)
                for dc in range(DC):
                    MM(mm[mb][:, :], nT[:, dc, i * 128:(i + 1) * 128], wvc[:, dc, :], dc == 0, dc == DC - 1,
                       bring[rv] + [bnT], [bmm[mb]])
                CPY(evac_eng(), vcs[:, i, :], mm[mb][:, :], [bmm[mb]], [B["vcs"]])

            for m in range(NG):
                emit_group(m)

        def emit_group(m):
            allGA = bGA
            allYA = bYA
            for t in range(4):
                DMA(H[t][:, :], xo_d[m * 4 + t, :, :], [], [bH[t]])
            for t in range(4):
                norm_transpose(H[t], bH[t], 128, t * 128)
            for i in range(2):
                hx = YA[:, i * 2048:(i + 1) * 2048]
                DMA(hx, xh_d[m * 2 + i, :, :], [], [bYA[i * 4 + j] for j in range(4)])
                norm_transpose_from(hx, [bYA[i * 4 + j] for j in range(4)], i * 128)
            slab_of = {}
            for cc in range(8):
                if cc % 4 == 0:
                    slab_of["v"] = fetch("slab", 6 + cc // 4)
                    slab_of["g"] = fetch("slab", 8 + cc // 4)
                wv_ = ring[slab_of["v"]][:, :].rearrange("p (c f) -> p c f", c=16)
                wg_ = ring[slab_of["g"]][:, :].rearrange("p (c f) -> p c f", c=16)
                fo = (cc % 4) * 128
                mv = rot("mm", 3)
                mg = rot("mm", 3)
                for dc in range(DC):
                    MM(mm[mv][:, :], wv_[:, dc, fo:fo + 128], nT[:, dc, 0:512], dc == 0, dc == DC - 1,
                       bring[slab_of["v"]] + [bnT], [bmm[mv]])
                    MM(misc[:, 0:256], wv_[:, dc, fo:fo + 128], nTh[:, dc, :], dc == 0, dc == DC - 1,
                       bring[slab_of["v"]] + bCT, [bmisc[0], bmisc[1]])
                for dc in range(DC):
                    MM(mm[mg][:, :], wg_[:, dc, fo:fo + 128], nT[:, dc, 0:512], dc == 0, dc == DC - 1,
                       bring[slab_of["g"]] + [bnT], [bmm[mg]])
                    MM(misc[:, 256:512], wg_[:, dc, fo:fo + 128], nTh[:, dc, :], dc == 0, dc == DC - 1,
                       bring[slab_of["g"]] + bCT, [bmisc[2], bmisc[3]])
                ACTV(sgt[:, 0:512], mm[mg][:, :], AF.Sigmoid, [bmm[mg]], [B["sgt"]])
                ACTV(sgt[:, 512:768], misc[:, 256:512], AF.Sigmoid, [bmisc[2], bmisc[3]], [B["sgt"]])
                TTO("dve", G4[:, cc, :, 32:96], mm[mv][:, :].rearrange("p (q t) -> p q t", q=8),
                    sgt[:, 0:512].rearrange("p (q t) -> p q t", q=8), ALU.mult, [bmm[mv], B["sgt"]], [bGA[cc]])
                TTO("dve", G4[:, cc, :, 0:32], misc[:, 0:256].rearrange("p (q t) -> p q t", q=8),
                    sgt[:, 512:768].rearrange("p (q t) -> p q t", q=8), ALU.mult,
                    [bmisc[0], bmisc[1], B["sgt"]], [bGA[cc]])
            for cp_ in range(4):
                rcd = fetch("cdiag", cp_)
                for ccl in range(2):
                    cc = cp_ * 2 + ccl
                    mb = rot("mm", 3)
                    for k in range(KCONV):
                        di = (ccl * KCONV + k) * 128
                        MM(mm[mb][:, :].rearrange("p (q t) -> p q t", q=8), ring[rcd][:, di:di + 128],
                           G4[:, cc, :, k + 2:k + 66], k == 0, k == KCONV - 1, [bGA[cc]] + bring[rcd], [bmm[mb]])
                    if cc % 2 == 0:
                        ACTV(Y3[:, cc, :], mm[mb][:, :], AF.Identity, [bmm[mb], B["cp"]], [bYA[cc]], bias=cp[:, cc:cc + 1])
                    else:
                        TS("dve", Y3[:, cc, :], mm[mb][:, :], cp[:, cc:cc + 1], None, ALU.add, None,
                           [bmm[mb], B["cp"]], [bYA[cc]])
            for h in range(NH):
                if h % 4 == 0:
                    rq = fetch("slab", h // 4)
                    wq_ = ring[rq][:, :].rearrange("p (c f) -> p c f", c=16)
                mb = rot("mm", 3)
                for dc in range(DC):
                    MM(mm[mb][:, :], wq_[:, dc, (h % 4) * 128:(h % 4 + 1) * 128], nT[:, dc, 0:512], dc == 0, dc == DC - 1,
                       bring[rq] + [bnT], [bmm[mb]])
                CPY(evac_eng(), QT[:, h, :], mm[mb][:, :], [bmm[mb]], [bQT[h]])
            GS = GA[:, 0:4096].rearrange("p (c t) -> p c t", c=8)
            for cc in range(8):
                TTO("pool", GS[:, cc, :], Y3[:, cc, :], Y3[:, cc, :], ALU.mult, [bYA[cc]] + allGA, allGA)
            m1 = rot("mm", 3)
            m2 = rot("mm", 3)
            for cc in range(8):
                MM(mm[m1][:, :], onesf[:, :], Y3[:, cc, :], cc == 0, cc == 7, [B["onesf"], bYA[cc]], [bmm[m1]])
            for cc in range(8):
                MM(mm[m2][:, :], onesf[:, :], GS[:, cc, :], cc == 0, cc == 7, [B["onesf"]] + allGA, [bmm[m2]])
            CPY("dve", mean_sb[:, :], mm[m1][:, :], [bmm[m1]], [B["mean_sb"]])
            TTO("dve", rstd_sb[:, :], mean_sb[:, :], mean_sb[:, :], ALU.mult, [B["mean_sb"]], [B["rstd_sb"]])
            TTO("dve", rstd_sb[:, :], mm[m2][:, :], rstd_sb[:, :], ALU.subtract, [bmm[m2], B["rstd_sb"]], [B["rstd_sb"]])
            ACTV(rstd_sb[:, :], rstd_sb[:, :], AF.Sqrt, [B["rstd_sb"], B["zcol"]], [B["rstd_sb"]], bias=epsc[:, 0:1])
            P.op("dve", lambda e, o=rstd_sb[:, :], i=rstd_sb[:, :]: e.reciprocal(out=o, in_=i), [B["rstd_sb"]], [B["rstd_sb"]])
            for cc in range(8):
                eng = "pool" if cc % 2 == 0 else "dve"
                TTO(eng, Y3[:, cc, :], Y3[:, cc, :], mean_sb[:, :], ALU.subtract, [bYA[cc], B["mean_sb"]], [bYA[cc]])
                TTO(eng, Y3[:, cc, :], Y3[:, cc, :], rstd_sb[:, :], ALU.mult, [bYA[cc], B["rstd_sb"]], [bYA[cc]])
                ACTV(CT[:, cc, :], Y3[:, cc, :], AF.Silu, [bYA[cc], B["cp"]], [bCT[cc]],
                     bias=cp[:, 16 + cc:17 + cc], scale=cp[:, 8 + cc:9 + cc])
            for h in range(NH):
                for t in range(4):
                    tau = m * 4 + t
                    ncol = 2 * tau + 2
                    gi = rot("g", 2)
                    rg = rot("misc", 4)
                    MSET("pool", gsb[gi][:, :], -1e30, [bgsb[gi]])
                    gp = misc[:, rg * 128:rg * 128 + 64]
                    MM(gp[:, 0:ncol], QT[:, h, t * 128:(t + 1) * 128], khi[:, h, 0:ncol], True, False,
                       [bQT[h], B["khi"]], [bmisc[rg]])
                    MM(gp[:, 0:ncol], QT[:, h, t * 128:(t + 1) * 128], klo[:, h, 0:ncol], False, True,
                       [bQT[h], B["klo"]], [bmisc[rg]])
                    if tau > 0:
                        CPY("dve", gsb[gi][0:64, 0:2 * tau], gp[0:64, 0:2 * tau], [bmisc[rg]], [bgsb[gi]])
                    CPY("dve", gsb[gi][64:128, 0:2 * tau + 1], gp[64:128, 0:2 * tau + 1], [bmisc[rg]], [bgsb[gi]])
                    P.op("dve", lambda e, o=mx8[gi][:, :], i=gsb[gi][:, 0:max(8, ncol)]: e.max(out=o, in_=i),
                         [bgsb[gi]], [bmx8[gi]])
                    TS("dve", mx8[gi][:, 2:3], mx8[gi][:, 2:3], -1e29, None, ALU.max, None, [bmx8[gi]], [bmx8[gi]])
                    TS("dve", m30[gi][:, 0:ncol], gsb[gi][:, 0:ncol], mx8[gi][:, 2:3], 1e30, ALU.subtract, ALU.mult,
                       [bgsb[gi], bmx8[gi]], [bm30[gi]])
                    TS("dve", m30[gi][:, 0:ncol], m30[gi][:, 0:ncol], -1.0, 0.0, ALU.max, ALU.min, [bm30[gi]], [bm30[gi]])
                    STT("dve", bmv[:, h, t, 0:ncol], m30[gi][:, 0:ncol], -NEG, TT[:, h, 64 - 2 * tau:64 - 2 * tau + ncol],
                        ALU.mult, ALU.add, [bm30[gi], B["TT"]], [bbm[h][t]])
            nblk_g = 8 * m + 8
            for h in range(NH):
                hb = h % 2
                DMA(HK[0:64, hb, :, :], bass.AP(fp_h, h * LFP, [[1, 64], [256, 8], [1, 256]]),
                    [B["fpd"]], [bHK[hb]] + allGA)
                DMA(HK[64:128, hb, :, :], bass.AP(fp_h, h * LFP + 256, [[1, 64], [256, 8], [1, 256]]),
                    [B["fpd"]], [bHK[hb]] + allGA)
                steps = []
                for kc in range((nblk_g + 15) // 16):
                    b0 = kc * 16
                    nb = min(16, nblk_g - b0)
                    for t in range(4):
                        tau = m * 4 + t
                        blks = list(range(b0, min(b0 + nb, 2 * tau + 2)))
                        npair = len(blks) // 2
                        for pi in range(npair):
                            steps.append(dict(kc=kc, b0=b0, nb=nb, t=t, tau=tau, pi=pi, npair=npair,
                                              kb=(blks[2 * pi], blks[2 * pi + 1])))
                cur = {"kc": -1, "rb": 0}

                def stA(sx):
                    if sx["kc"] != cur["kc"]:
                        cur["kc"] = sx["kc"]
                        cur["rb"] = fetch("kv", (h, sx["b0"], sx["nb"]))
                    rb = cur["rb"]
                    sx["rb"] = rb
                    t, tau, b0 = sx["t"], sx["tau"], sx["b0"]
                    KTc = ring[rb][:, 0:4096]
                    sb_ = rot("sp", 2)
                    sx["sb"] = sb_
                    kb0, kb1 = sx["kb"]
                    lo = (kb0 - b0) * 256
                    qT = QT[:, h, t * 128:(t + 1) * 128]
                    if 2 * tau + 1 - kb0 <= 7:
                        for bi, kb in enumerate(sx["kb"]):
                            so = bi * 256
                            MM(sp_[sb_][:, so:so + 256], qT, KTc[:, lo + so:lo + so + 256], True, False,
                               [bQT[h], bringK[rb]], [bsp[sb_]])
                            MM(sp_[sb_][:, so:so + 256], ident[:, :], HK[:, hb, 2 * tau + 1 - kb, :], False, True,
                               [B["ident"], bHK[hb]] + allGA, [bsp[sb_]])
                    else:
                        MM(sp_[sb_][:, :], qT, KTc[:, lo:lo + 512], True, True, [bQT[h], bringK[rb]], [bsp[sb_]])
                    pb = rot("pb", 2)
                    sx["pb"] = pb
                    for bi, kb in enumerate(sx["kb"]):
                        ACTV(Pb[pb][:, bi * 256:(bi + 1) * 256], sp_[sb_][:, bi * 256:(bi + 1) * 256], AF.Exp,
                             [bsp[sb_], bbm[h][t]], [bPb[pb]], bias=bmv[:, h, t, kb:kb + 1], scale=SCALE)

                def stC(sx):
                    pb = sx["pb"]
                    kb0 = sx["kb"][0]
                    P.op("dve", lambda e, o=rs[sx["t"]][:, kb0:kb0 + 2], i=Pb[pb].rearrange("p (b k) -> p b k", b=2):
                         e.tensor_reduce(out=o, in_=i, axis=AX.X, op=ALU.add), [bPb[pb]], [brs[sx["t"]]])
                    tb = rot("tr", 2)
                    for q4 in range(4):
                        TR(tr[tb][:, q4 * 128:(q4 + 1) * 128], Pb[pb][:, q4 * 128:(q4 + 1) * 128], ident[:, :],
                           [bPb[pb], B["ident"]], [btr[tb]])
                    pt = rot("pt", 2)
                    sx["pt"] = pt
                    CPY("dve", PTs[pt][:, :], tr[tb][:, 0:512], [btr[tb]], [bPTs[pt]])

                def stE(sx):
                    rb, t, b0, pt = sx["rb"], sx["t"], sx["b0"], sx["pt"]
                    Vc = ring[rb][:, 4096:8192].rearrange("p (c d) -> p c d", d=128)
                    if sx["pi"] == 0:
                        cur["ob%d" % t] = rot("misc", 4)
                    ob = cur["ob%d" % t]
                    Ops = misc[:, ob * 128:(ob + 1) * 128]
                    for q4 in range(4):
                        kb = sx["kb"][q4 // 2]
                        vch = (kb - b0) * 2 + (q4 % 2)
                        MM(Ops, PTs[pt][:, q4 * 128:(q4 + 1) * 128], Vc[:, vch, :],
                           sx["pi"] == 0 and q4 == 0, sx["pi"] == sx["npair"] - 1 and q4 == 3,
                           [bPTs[pt], bringV[rb]], [bmisc[ob]])
                    if sx["pi"] == sx["npair"] - 1:
                        if sx["kc"] == 0:
                            CPY("dve", Oacc[t][:, :], Ops, [bmisc[ob]], [bOacc[t]])
                        else:
                            TTO("dve", Oacc[t][:, :], Oacc[t][:, :], Ops, ALU.add, [bmisc[ob], bOacc[t]], [bOacc[t]])

                ns = len(steps)
                for i in range(ns + 2):
                    if i < ns:
                        stA(steps[i])
                    if 0 <= i - 1 < ns:
                        stC(steps[i - 1])
                    if 0 <= i - 2 < ns:
                        stE(steps[i - 2])
                for t in range(4):
                    tau = m * 4 + t
                    P.op("dve", lambda e, o=st[:, 4 + t:5 + t], i=rs[t][:, 0:2 * tau + 2]:
                         e.tensor_reduce(out=o, in_=i, axis=AX.X, op=ALU.add), [brs[t]], [B["st"]])
                    P.op("dve", lambda e, o=st[:, 8 + t:9 + t], i=st[:, 4 + t:5 + t]: e.reciprocal(out=o, in_=i),
                         [B["st"]], [B["st"]])
                    TS("dve", Av[:, t, h * 128:(h + 1) * 128], Oacc[t][:, :], st[:, 8 + t:9 + t], None, ALU.mult, None,
                       [bOacc[t], B["st"]] + allYA[4:8], [bA[t]] + allYA[4:8])
            for h in range(NH):
                tb = rot("tr", 2)
                for t in range(4):
                    TR(tr[tb][:, t * 128:(t + 1) * 128], Av[:, t, h * 128:(h + 1) * 128], ident[:, :],
                       [bA[t], B["ident"]] + allYA[4:8], [btr[tb]])
                CPY(evac_eng(), QT[:, h, :], tr[tb][:, 0:512], [btr[tb]], [bQT[h]])
            for ds in range(4):
                rw = fetch("slab", 10 + ds)
                wo_ = ring[rw][:, :].rearrange("p (c f) -> p c f", c=16)
                for t in range(4):
                    mb = rot("mm", 3)
                    for ic in range(16):
                        lhs = QT[:, ic, t * 128:(t + 1) * 128] if ic < 8 else CT[:, ic - 8, t * 128:(t + 1) * 128]
                        MM(mm[mb][:, :], lhs, wo_[:, ic, :], ic == 0, ic == 15,
                           [bQT[ic] if ic < 8 else bCT[ic - 8]] + bring[rw], [bmm[mb]])
                    TTO("dve", H[t][:, ds * 512:(ds + 1) * 512], H[t][:, ds * 512:(ds + 1) * 512], mm[mb][:, :], ALU.add,
                        [bH[t], bmm[mb]], [bH[t]])
            for t in range(4):
                norm_transpose(H[t], bH[t], 128, t * 128)
            rq = fetch("slab", 14)
            wqc = ring[rq][:, :].rearrange("p (c f) -> p c f", c=16)
            for h4 in range(4):
                mb = rot("mm", 3)
                for dc in range(DC):
                    MM(mm[mb][:, :], wqc[:, dc, h4 * 128:(h4 + 1) * 128], nT[:, dc, 0:512], dc == 0, dc == DC - 1,
                       bring[rq] + [bnT], [bmm[mb]])
                CPY(evac_eng(), QT[:, h4, :], mm[mb][:, :], [bmm[mb]], [bQT[h4]])
            for t in range(4):
                MSET("pool", rs[t][:, 0:4], 0.0, [brs[t]])
                for hp in range(2):
                    sb_ = rot("sp", 2)
                    pb = rot("pb", 2)
                    for hh in range(2):
                        h4 = hp * 2 + hh
                        MM(sp_[sb_][:, hh * 256:(hh + 1) * 256], QT[:, h4, t * 128:(t + 1) * 128], kcT[:, h4, :], True, True,
                           [bQT[h4], B["kcT"]], [bsp[sb_]])
                    for hh in range(2):
                        h4 = hp * 2 + hh
                        ACTV(Pb[pb][:, hh * 256:(hh + 1) * 256], sp_[sb_][:, hh * 256:(hh + 1) * 256], AF.Exp,
                             [bsp[sb_], brs[t], B["zcol"]], [bPb[pb], brs[t]], bias=zcol[:, 0:1], scale=SCALE,
                             accum=rs[t][:, h4:h4 + 1])
                    tb = rot("tr", 2)
                    for q4 in range(4):
                        TR(tr[tb][:, q4 * 128:(q4 + 1) * 128], Pb[pb][:, q4 * 128:(q4 + 1) * 128], ident[:, :],
                           [bPb[pb], B["ident"]], [btr[tb]])
                    pt = rot("pt", 2)
                    CPY("dve", PTs[pt][:, :], tr[tb][:, 0:512], [btr[tb]], [bPTs[pt]])
                    for hh in range(2):
                        h4 = hp * 2 + hh
                        ob = rot("misc", 4)
                        Ops = misc[:, ob * 128:(ob + 1) * 128]
                        for mc in range(2):
                            MM(Ops, PTs[pt][:, (hh * 2 + mc) * 128:(hh * 2 + mc + 1) * 128], vcs[:, mc, h4 * 128:(h4 + 1) * 128],
                               mc == 0, mc == 1, [bPTs[pt], B["vcs"]], [bmisc[ob]])
                        P.op("dve", lambda e, o=st[:, 12:13], i=rs[t][:, h4:h4 + 1]: e.reciprocal(out=o, in_=i),
                             [brs[t]], [B["st"]])
                        TS("dve", Av[:, t, h4 * 128:(h4 + 1) * 128], Ops, st[:, 12:13], None, ALU.mult, None,
                           [bmisc[ob], B["st"]] + allYA[4:8], [bA[t]] + allYA[4:8])
            for h4 in range(4):
                tb = rot("tr", 2)
                for t in range(4):
                    TR(tr[tb][:, t * 128:(t + 1) * 128], Av[:, t, h4 * 128:(h4 + 1) * 128], ident[:, :],
                       [bA[t], B["ident"]] + allYA[4:8], [btr[tb]])
                CPY(evac_eng(), QT[:, 4 + h4, :], tr[tb][:, 0:512], [btr[tb]], [bQT[4 + h4]])
            rw = fetch("slab", 17)
            woc = ring[rw][:, :].rearrange("p (c f) -> p c f", c=4)
            for t in range(4):
                for ds in range(4):
                    mb = rot("mm", 3)
                    for ic in range(4):
                        MM(mm[mb][:, :], QT[:, 4 + ic, t * 128:(t + 1) * 128], woc[:, ic, ds * 512:(ds + 1) * 512],
                           ic == 0, ic == 3, [bQT[4 + ic]] + bring[rw], [bmm[mb]])
                    TTO("dve", H[t][:, ds * 512:(ds + 1) * 512], H[t][:, ds * 512:(ds + 1) * 512], mm[mb][:, :], ALU.add,
                        [bH[t], bmm[mb]], [bH[t]])
            for t in range(4):
                norm_transpose(H[t], bH[t], 128, t * 128)
            for fh in range(2):
                for sl in range(8):
                    rw = fetch("slab", 18 + fh * 8 + sl)
                    w1_ = ring[rw][:, :].rearrange("p (c f) -> p c f", c=16)
                    for f4 in range(4):
                        fl = sl * 4 + f4
                        mb = rot("mm", 3)
                        for dc in range(DC):
                            MM(mm[mb][:, :], w1_[:, dc, f4 * 128:(f4 + 1) * 128], nT[:, dc, 0:512], dc == 0, dc == DC - 1,
                               bring[rw] + [bnT], [bmm[mb]])
                        ri = rot("rt", 2)
                        ACTV(rtmp[ri][:, :], mm[mb][:, :], AF.Relu, [bmm[mb]], [brtmp[ri]])
                        dst = h1a[:, fl, :] if fl < 16 else h1b[:, fl - 16, :]
                        dbuf = allYA if fl < 16 else allGA
                        TTO("pool", dst, rtmp[ri][:, :], rtmp[ri][:, :], ALU.mult, [brtmp[ri]] + dbuf, dbuf)
                for ds in range(4):
                    r2 = [fetch("slab", 34 + ds * 4 + fh * 2 + q) for q in range(2)]
                    for t in range(4):
                        mb = rot("mm", 3)
                        for fl in range(32):
                            src = h1a[:, fl, t * 128:(t + 1) * 128] if fl < 16 else h1b[:, fl - 16, t * 128:(t + 1) * 128]
                            w2_ = ring[r2[fl // 16]][:, :].rearrange("p (c f) -> p c f", c=16)
                            MM(mm[mb][:, :], src, w2_[:, fl % 16, :], fl == 0, fl == 31,
                               (allYA if fl < 16 else allGA) + bring[r2[fl // 16]], [bmm[mb]])
                        TTO("dve", H[t][:, ds * 512:(ds + 1) * 512], H[t][:, ds * 512:(ds + 1) * 512], mm[mb][:, :], ALU.add,
                            [bH[t], bmm[mb]], [bH[t]])
            DMA(gfin_v, gfin_d[:, :], allGA, allGA)
            for t in range(4):
                MSET("dve", st[:, 0:1], 0.0, [B["st"]])
                ACTV(xs[:, :], H[t][:, :], AF.Square, [bH[t], B["st"]], bxs_all + [B["st"]], accum=st[:, 0:1])
                ACTV(st[:, 1:2], st[:, 0:1], AF.Sqrt, [B["st"], B["zcol"]], [B["st"]], bias=epsc[:, 0:1], scale=1.0 / D)
                P.op("dve", lambda e, o=st[:, 1:2], i=st[:, 1:2]: e.reciprocal(out=o, in_=i), [B["st"]], [B["st"]])
                STT("dve", H[t][:, :], H[t][:, :], st[:, 1:2], gfin_v, ALU.mult, ALU.mult,
                    [bH[t], B["st"]] + allGA, [bH[t]])
                DMA(out_d[m * 4 + t, :, :], H[t][:, :], [bH[t]], [], queue="pool")

        def norm_transpose_from(X, bXl, tokoff):
            MSET("dve", st[:, 0:1], 0.0, [B["st"]])
            ACTV(xs[:, :], X, AF.Square, bXl + [B["st"]], bxs_all + [B["st"]], accum=st[:, 0:1])
            ACTV(st[:, 1:2], st[:, 0:1], AF.Sqrt, [B["st"], B["zcol"]], [B["st"]], bias=epsc[:, 0:1], scale=1.0 / D)
            P.op("dve", lambda e, o=st[:, 1:2], i=st[:, 1:2]: e.reciprocal(out=o, in_=i), [B["st"]], [B["st"]])
            TS("dve", xs[:, :], X, st[:, 1:2], None, ALU.mult, None, bXl + [B["st"]], bxs_all)
            for half in range(2):
                tb = rot("tr", 2)
                for c in range(8):
                    dc = half * 8 + c
                    TR(tr[tb][:, c * 128:(c + 1) * 128], xs[:, dc * 128:(dc + 1) * 128], ident[:, :],
                       bxs_all + [B["ident"]], [btr[tb]])
                CPY(evac_eng(), nTh[:, half * 8:(half + 1) * 8, tokoff:tokoff + 128],
                    tr[tb][:, :].rearrange("p (c t) -> p c t", c=8), [btr[tb]], bCT)

        P.dry = True
        saved = dict(cnt)
        emit_all()
        P.dry = False
        cnt.update(saved)
        emit_all()
        assert stream_state["pos"] == len(plan)
        P.emit(nc)
    return nc


def _t5_bucket_np(d):
    d = np.asarray(d, dtype=np.int64)
    max_exact = 16
    nf = np.maximum(d, max_exact).astype(np.float32)
    large = max_exact + (np.log(nf / np.float32(max_exact)) / np.float32(math.log(2048 / max_exact))
                         * np.float32(16)).astype(np.int32)
    large = np.minimum(large, 31)
    return np.where(d < max_exact, d, large)


def _slab(W, f0):
    return np.ascontiguousarray(W[:, f0:f0 + 512].reshape(16, 128, 512).transpose(1, 0, 2)).reshape(128, 8192)


def _prep_shared(inp):
    w_in = inp["w_in"][0]
    slabs = [_slab(w_in, f0) for f0 in range(0, 5120, 512)]
    slabs += [_slab(inp["w_out"][0], f0) for f0 in range(0, 2048, 512)]
    slabs.append(_slab(inp["wq_c"][0], 0))
    slabs.append(_slab(inp["wk_c"][0], 0))
    slabs.append(_slab(inp["wv_c"][0], 0))
    slabs.append(np.ascontiguousarray(inp["wo_c"][0].reshape(4, 128, 2048).transpose(1, 0, 2)).reshape(128, 8192))
    w1 = inp["w1"][0]
    slabs += [_slab(w1, f0) for f0 in range(0, 8192, 512)]
    w2 = inp["w2"][0]
    for ds in range(4):
        for fq in range(4):
            slabs.append(_slab(w2[fq * 2048:(fq + 1) * 2048], ds * 512))
    wall = np.stack(slabs).astype(np.float32)
    pc = lambda v: np.ascontiguousarray(v.reshape(-1, 128).T)
    gvec = np.concatenate([pc(inp["g_mix"][0]), pc(inp["g_cross"][0]), pc(inp["g_mem"][0]), pc(inp["g_mlp"][0])], axis=1)
    gfin = np.ascontiguousarray(np.broadcast_to(inp["g_final"][None, :], (128, D)))
    cw = np.ascontiguousarray(inp["conv_w"][0].reshape(KCONV, 8, 128).transpose(2, 1, 0)).reshape(128, 8 * KCONV)
    cp = np.concatenate([pc(inp["conv_b"][0]), pc(inp["conv_ln_g"][0]), pc(inp["conv_ln_b"][0])], axis=1)
    rel = inp["rel_bias"]
    relaug = np.concatenate([rel, np.full((1, 8), NEG, np.float32)], axis=0)
    c31 = np.ascontiguousarray(np.broadcast_to(rel[31][None, :], (128, 8)))
    return dict(wall=wall, gvec=gvec.astype(np.float32), gfin=gfin.astype(np.float32), convw=cw.astype(np.float32),
                convp=cp.astype(np.float32), relaug=relaug.astype(np.float32), c31=c31.astype(np.float32))


def _oh_table(r):
    oh = np.zeros((33, LFP), np.float32)
    y = np.arange(LFP)
    x = y + r * 64
    neg = x < 511
    oh[32, neg] = 1.0
    d = np.maximum(x - 511, 0)
    bk = _t5_bucket_np(d)
    oh[bk[~neg], y[~neg]] = 1.0
    return oh


def _core_inputs(inp, b, r, NBLK, shared, xa_cache):
    S = NBLK * 256
    x = inp["x"][b]
    xb = x.reshape(NBLK, 256, D)
    own = xb[:, r * 64:(r + 1) * 64, :]
    xo = np.ascontiguousarray(own.reshape(NBLK // 2, 128, D))
    xpad = np.concatenate([np.zeros((32, D), np.float32), x], axis=0)
    starts = (np.arange(NBLK) * 256 + r * 64)
    idx = starts[:, None] + np.arange(32)[None, :]
    xh = np.ascontiguousarray(xpad[idx].reshape(NBLK // 4, 128, D))
    if b not in xa_cache:
        xa_cache[b] = np.ascontiguousarray(xb[:, ::-1, :].reshape(S // 128, 128, D))
    d = dict(shared)
    d.update(xo=xo, xh=xh, xa=xa_cache[b], mem=np.ascontiguousarray(inp["mem"][b].reshape(2, 128, D)), oh=_oh_table(r))
    return d


_NC_CACHE = {}


def kernel(**inputs):
    inp = {k: np.asarray(v) for k, v in inputs.items()}
    Bn, S, _ = inp["x"].shape
    NBLK = S // 256
    shared = _prep_shared(inp)
    xa_cache = {}
    in_maps = []
    for b in range(Bn):
        for r in range(4):
            in_maps.append(_core_inputs(inp, b, r, NBLK, shared, xa_cache))
    if NBLK not in _NC_CACHE:
        _NC_CACHE[NBLK] = build_program(NBLK)
    nc = _NC_CACHE[NBLK]
    res = run_bass_kernel_spmd(nc, in_maps, core_ids=list(range(len(in_maps))))
    out = np.zeros((Bn, S, D), np.float32)
    ov = out.reshape(Bn, NBLK, 256, D)
    for b in range(Bn):
        for r in range(4):
            o = np.asarray(res.results[b * 4 + r]["out"]).reshape(NBLK, 64, D)
            ov[b, :, r * 64:(r + 1) * 64, :] = o
    return out
```

```python
import contextlib
import math
import numpy as np
import concourse.bass as bass
import concourse.mybir as mybir
from concourse.bass_utils import run_bass_kernel_spmd

F32 = mybir.dt.float32
BF16 = mybir.dt.bfloat16
ALU = mybir.AluOpType
AF = mybir.ActivationFunctionType
AX = mybir.AxisListType

D = 2048
DC = 16
NH = 8
NMEM = 256
KCONV = 31
EPS = 1e-6
NEG = -1000.0
SCALE = 128.0 ** -0.5
SQ = 128.0 ** 0.5
LFP = 2432
NSLAB = 50
RING = 4

COMPUTE = ("pe", "act", "dve", "pool")
NDMASEM = 14


class Buf:
    __slots__ = ("name", "lw", "rd")

    def __init__(self, name):
        self.name = name
        self.lw = None
        self.rd = []


class Ins:
    __slots__ = ("eng", "fn", "deps", "idx", "sig", "seq", "dsem", "dcnt", "waits", "src")

    def __init__(self, eng, fn):
        self.eng = eng
        self.fn = fn
        self.deps = []
        self.idx = -1
        self.sig = False
        self.seq = 0
        self.dsem = -1
        self.dcnt = 0
        self.waits = []
        self.src = None


class Prog:
    def __init__(self, same_engine_sync=True):
        self.streams = {e: [] for e in COMPUTE + ("sp",)}
        self.order = []
        self.same = same_engine_sync
        self.ndma = 0
        self.dma_last = [None] * NDMASEM
        self.dma_cnt = [0] * NDMASEM
        self.dry = False

    def _track(self, ins, reads, writes):
        deps = {}
        for b in reads:
            if b.lw is not None:
                deps[id(b.lw)] = b.lw
        for b in writes:
            if b.lw is not None:
                deps[id(b.lw)] = b.lw
            for r in b.rd:
                deps[id(r)] = r
        for b in reads:
            b.rd.append(ins)
        for b in writes:
            b.lw = ins
            b.rd = []
        ins.deps = list(deps.values())

    def op(self, eng, fn, reads=(), writes=()):
        if self.dry:
            return None
        ins = Ins(eng, fn)
        ins.src = eng
        self._track(ins, reads, writes)
        ins.idx = len(self.streams[eng])
        self.streams[eng].append(ins)
        self.order.append(ins)
        return ins

    def dma(self, fn, reads=(), writes=(), queue="sp"):
        if self.dry:
            return None
        ins = Ins(queue, fn)
        k = self.ndma % NDMASEM
        self.ndma += 1
        ins.dsem = k
        self.dma_cnt[k] += 1
        ins.dcnt = self.dma_cnt[k]
        ins.src = "d%d" % k
        self._track(ins, reads, writes)
        if self.dma_last[k] is not None:
            ins.deps.append(self.dma_last[k])
        self.dma_last[k] = ins
        ins.idx = len(self.streams[queue])
        self.streams[queue].append(ins)
        self.order.append(ins)
        return ins

    def analyze(self):
        vdone = {}
        last_start = {e: {} for e in self.streams}
        for ins in self.order:
            vc = dict(last_start[ins.eng])
            for d in ins.deps:
                if d.dsem >= 0:
                    src, val = d.src, d.dcnt
                else:
                    src, val = d.eng, d.idx + 1
                    if d.eng == ins.eng and ins.dsem < 0 and (d.eng == "pe" or not self.same):
                        continue
                if vc.get(src, 0) >= val:
                    continue
                d.sig = True
                ins.waits.append(d)
                for s, v in vdone[id(d)].items():
                    if vc.get(s, 0) < v:
                        vc[s] = v
            last_start[ins.eng] = vc
            vd = dict(vc)
            if ins.dsem >= 0:
                vd[ins.src] = ins.dcnt
            else:
                vd[ins.eng] = ins.idx + 1
            vdone[id(ins)] = vd
            ins.deps = None
        for ins in self.order:
            if len(ins.waits) > 1:
                best = {}
                for d in ins.waits:
                    val = d.dcnt if d.dsem >= 0 else d.idx
                    if d.src not in best or val > best[d.src][0]:
                        best[d.src] = (val, d)
                ins.waits = [v[1] for v in best.values()]
        for e in COMPUTE:
            n = 0
            for ins in self.streams[e]:
                if ins.dsem < 0 and ins.sig:
                    n += 1
                    ins.seq = n

    def emit(self, nc, final_wait_queue="sp"):
        self.analyze()
        with contextlib.ExitStack() as es:
            sems = {e: es.enter_context(nc.semaphore("s_" + e)) for e in COMPUTE}
            dsems = [es.enter_context(nc.semaphore("dq%d" % k)) for k in range(NDMASEM)]
            block = es.enter_context(nc.Block())

            def run(engname, eng):
                for ins in self.streams[engname]:
                    for d in ins.waits:
                        if d.dsem >= 0:
                            eng.wait_ge(dsems[d.dsem], 16 * d.dcnt)
                        else:
                            eng.wait_ge(sems[d.eng], d.seq)
                    r = ins.fn(eng)
                    if ins.dsem >= 0:
                        r.then_inc(dsems[ins.dsem], 16)
                    elif ins.sig:
                        r.then_inc(sems[ins.eng], 1)
                if engname == final_wait_queue:
                    for k in range(NDMASEM):
                        if self.dma_cnt[k]:
                            eng.wait_ge(dsems[k], 16 * self.dma_cnt[k])

            @block.tensor
            def _(e):
                run("pe", e)

            @block.scalar
            def _(e):
                run("act", e)

            @block.vector
            def _(e):
                run("dve", e)

            @block.gpsimd
            def _(e):
                run("pool", e)

            @block.sync
            def _(e):
                run("sp", e)


def build_program(NBLK, debug=False):
    S = NBLK * 256
    NG = NBLK // 8
    NT = NBLK // 2
    NAG = NBLK // 2
    nc = bass.Bass("TRN2", target_bir_lowering=False)
    dt_in = lambda name, shape: nc.dram_tensor(name, shape, F32, kind="ExternalInput").ap()
    xo_d = dt_in("xo", [NT, 128, D])
    xh_d = dt_in("xh", [NG * 2, 128, D])
    xa_d = dt_in("xa", [S // 128, 128, D])
    mem_d = dt_in("mem", [2, 128, D])
    wall_d = dt_in("wall", [NSLAB, 128, 8192])
    gv_d = dt_in("gvec", [128, 64])
    gfin_d = dt_in("gfin", [128, D])
    cw_d = dt_in("convw", [128, 8 * KCONV])
    cp_d = dt_in("convp", [128, 24])
    rel_d = dt_in("relaug", [33, 8])
    c31_d = dt_in("c31", [128, 8])
    oh_d = dt_in("oh", [33, LFP])
    out_d = nc.dram_tensor("out", [NT, 128, D], F32, kind="ExternalOutput").ap()
    wbf_d = nc.dram_tensor("wbf", [NSLAB, 128, 8192], BF16, kind="Internal").ap()
    kt_d = nc.dram_tensor("ktd", [NH, 128, S], BF16, kind="Internal").ap()
    v_d = nc.dram_tensor("vd", [NH, 128, S // 128, 128], BF16, kind="Internal").ap()
    cd_d = nc.dram_tensor("cdd", [4, 128, 8192], BF16, kind="Internal").ap()
    fp_h = nc.dram_tensor("fpd", [NH, LFP], BF16, kind="Internal")
    fp_d = fp_h.ap()

    P = Prog()
    es = contextlib.ExitStack()
    with es:
        def SB(name, shape, dt):
            return es.enter_context(nc.sbuf_tensor("sb_" + name, shape, dt))

        def PS(name, shape, dt):
            return es.enter_context(nc.psum_tensor("ps_" + name, shape, dt))

        H = [SB("H%d" % i, [128, D], F32) for i in range(4)]
        bH = [Buf("H%d" % i) for i in range(4)]
        ring = [SB("ring%d" % i, [128, 8192], BF16) for i in range(RING)]
        bringK = [Buf("ringK%d" % i) for i in range(RING)]
        bringV = [Buf("ringV%d" % i) for i in range(RING)]
        bring = [[bringK[i], bringV[i]] for i in range(RING)]
        bcd = [Buf("cd%d" % j) for j in range(4)]
        nT2 = SB("nT", [128, DC * 512], BF16)
        nT = nT2[:, :].rearrange("p (c t) -> p c t", c=DC)
        bnT = Buf("nT")
        QT2 = SB("QT", [128, NH * 512], BF16)
        QT = QT2[:, :].rearrange("p (h t) -> p h t", h=NH)
        bQT = [Buf("QT%d" % h) for h in range(NH)]
        CT2 = SB("CT", [128, 8 * 512], BF16)
        CT = CT2[:, :].rearrange("p (c t) -> p c t", c=8)
        nTh = CT2[:, :].rearrange("p (c t) -> p c t", c=DC)
        bCT = [Buf("CT%d" % c) for c in range(8)]
        GA = SB("GA", [128, 6144], F32)
        YA = SB("YA", [128, 4096], F32)
        bGA = [Buf("GA%d" % c) for c in range(8)]
        bYA = [Buf("YA%d" % c) for c in range(8)]
        xs = SB("xs", [128, D], BF16)
        bxs = Buf("xs")
        ident = SB("ident", [128, 128], BF16)
        onesf = SB("onesf", [128, 128], F32)
        gvec = SB("gvec", [128, 64], F32)
        cw = SB("cw", [128, 8 * KCONV], F32)
        cp = SB("cp", [128, 24], F32)
        c31m = SB("c31m", [128, 8], F32)
        relaug = SB("relaug", [33, 8], F32)
        ohs = YA[0:33, 0:LFP]
        fps = GA[:, :].bitcast(BF16)[0:8, 0:LFP]
        ksum2 = SB("ksum", [128, NH * NBLK], F32)
        kb322 = SB("kb32", [128, NH * NBLK], F32)
        ksum = ksum2[:, :].rearrange("p (h k) -> p h k", h=NH)
        kb32 = kb322[:, :].rearrange("p (h k) -> p h k", h=NH)
        bmv2 = SB("bmv", [128, 2048], F32)
        khi = SB("khi", [128, NH, NBLK], BF16)
        klo = SB("klo", [128, NH, NBLK], BF16)
        kcT = SB("kcT", [128, 4, 256], BF16)
        vcs = SB("vcs", [128, 2, 512], BF16)
        TT = SB("TT", [128, NH, 136], F32)
        st = SB("st", [128, 16], F32)
        sgt = SB("sgt", [128, 1024], F32)
        mean_sb = sgt[:, 0:512]
        rstd_sb = sgt[:, 512:1024]
        gsb = [SB("gsb%d" % i, [128, 64], F32) for i in range(2)]
        m30 = [SB("m30%d" % i, [128, 64], F32) for i in range(2)]
        mx8 = [SB("mx8%d" % i, [128, 8], F32) for i in range(2)]
        rs = [SB("rs%d" % i, [128, 64], F32) for i in range(4)]
        Oacc = [SB("Oacc%d" % i, [128, 128], F32) for i in range(4)]
        Pb = [xs[:, i * 512:(i + 1) * 512] for i in range(2)]
        PTs = [xs[:, 1024 + i * 512:1024 + (i + 1) * 512] for i in range(2)]
        rtmp = [sgt[:, i * 512:(i + 1) * 512] for i in range(2)]
        zcol = SB("zcol", [128, 1], F32)
        dummy = SB("dummy", [128, 1], F32)
        epsc = SB("epsc", [128, 1], F32)
        B = {n: Buf(n) for n in ("ident onesf gvec cw cp c31m relaug khi klo kcT vcs TT st sgt zcol fpd").split()}
        B["mean_sb"] = B["sgt"]
        B["rstd_sb"] = B["sgt"]
        B["ohs"] = bYA[0]
        B["fps"] = bGA[0]
        B["ksum"] = Buf("ksum")
        B["kb32"] = Buf("kb32")
        pstg = [Buf("pstg%d" % i) for i in range(4)]
        pout = [Buf("pout%d" % i) for i in range(2)]
        B["dummy"] = Buf("dummy")
        brtmp = [B["sgt"], B["sgt"]]
        bgsb = [Buf("gsb%d" % i) for i in range(2)]
        bm30 = [Buf("m30%d" % i) for i in range(2)]
        bmx8 = [Buf("mx8%d" % i) for i in range(2)]
        brs = [Buf("rs%d" % i) for i in range(4)]
        bOacc = [Buf("Oacc%d" % i) for i in range(4)]
        bPb = [Buf("Pb%d" % i) for i in range(2)]
        bPTs = [Buf("PTs%d" % i) for i in range(2)]
        bxs_all = [bxs] + bPb + bPTs
        bctmp = [Buf("ctmp0"), Buf("ctmp1")]
        bHK = [Buf("HK%d" % h) for h in range(2)]
        bbm = [[Buf("bm%d_%d" % (h, t)) for t in range(4)] for h in range(NH)]
        bA = [Buf("A%d" % t) for t in range(4)]
        bwd = [Buf("wd%d" % s) for s in range(NSLAB)]
        bkv = [Buf("kv%d" % g) for g in range(NAG)]
        G4 = GA[:, :].bitcast(BF16)[:, 0:6144].rearrange("p (c q t) -> p c q t", c=8, q=8)
        Y3 = YA[:, :].rearrange("p (c t) -> p c t", c=8)
        Y4 = YA[:, :].rearrange("p (c q t) -> p c q t", c=8, q=8)
        GAb = GA[:, :].bitcast(BF16)
        YAb = YA[:, :].bitcast(BF16)
        HK = GAb[:, 0:2 * 8 * 256].rearrange("p (h s k) -> p h s k", h=2, s=8)
        bmv = bmv2[:, :].rearrange("p (h t k) -> p h t k", h=NH, t=4)
        Av = YAb[:, 4096:8192].rearrange("p (t f) -> p t f", t=4)
        h1a = YAb[:, :].rearrange("p (c t) -> p c t", c=16)
        h1b = GAb[:, 0:8192].rearrange("p (c t) -> p c t", c=16)
        gfin_v = GA[:, 4096:6144]
        mm = [PS("mm%d" % i, [128, 512], F32) for i in range(3)]
        bmm = [Buf("mm%d" % i) for i in range(3)]
        tr = [PS("tr%d" % i, [128, 1024], BF16) for i in range(2)]
        btr = [Buf("tr%d" % i) for i in range(2)]
        sp_ = [PS("sps%d" % i, [128, 512], F32) for i in range(2)]
        bsp = [Buf("sps%d" % i) for i in range(2)]
        misc = PS("misc", [128, 512], F32)
        bmisc = [Buf("misc%d" % i) for i in range(4)]

        print("SBUF bytes remaining per partition:", nc.sbuf_bytes_remaining)
        cnt = {"mm": 0, "tr": 0, "sp": 0, "ev": 0, "misc": 0, "pb": 0, "pt": 0, "g": 0, "rt": 0}

        def rot(key, n):
            v = cnt[key] % n
            cnt[key] += 1
            return v

        def MM(out, lhsT, rhs, start, stop, reads, writes):
            P.op("pe", lambda e, o=out, l=lhsT, r=rhs, s=start, t=stop: e.matmul(o, lhsT=l, rhs=r, start=s, stop=t),
                 reads, writes)

        def TR(out, in_, idn, reads, writes):
            P.op("pe", lambda e, o=out, i=in_, d=idn: e.transpose(out=o, in_=i, identity=d), reads, writes)

        def ACTV(out, in_, func, reads, writes, bias=None, scale=1.0, accum=None):
            def f(e, o=out, i=in_, fn=func, b=bias, s=scale, a=accum):
                kw = {}
                if b is not None:
                    kw["bias"] = b
                if a is not None:
                    kw["accum_out"] = a
                return e.activation(out=o, in_=i, func=fn, scale=s, **kw)
            P.op("act", f, reads, writes)

        def CPY(eng, out, in_, reads, writes):
            if eng == "act":
                P.op("act", lambda e, o=out, i=in_: e.copy(out=o, in_=i), reads, writes)
            else:
                P.op(eng, lambda e, o=out, i=in_: e.tensor_copy(out=o, in_=i), reads, writes)

        def TS(eng, out, in0, s1, s2, op0, op1, reads, writes):
            if op1 is None:
                P.op(eng, lambda e, o=out, i=in0, a=s1, p0=op0: e.tensor_scalar(out=o, in0=i, scalar1=a, scalar2=None, op0=p0),
                     reads, writes)
            else:
                P.op(eng, lambda e, o=out, i=in0, a=s1, b=s2, p0=op0, p1=op1:
                     e.tensor_scalar(out=o, in0=i, scalar1=a, scalar2=b, op0=p0, op1=p1), reads, writes)

        def TTO(eng, out, in0, in1, op, reads, writes):
            P.op(eng, lambda e, o=out, a=in0, b=in1, p=op: e.tensor_tensor(out=o, in0=a, in1=b, op=p), reads, writes)

        def STT(eng, out, in0, sc, in1, op0, op1, reads, writes):
            P.op(eng, lambda e, o=out, a=in0, s=sc, b=in1, p0=op0, p1=op1:
                 e.scalar_tensor_tensor(out=o, in0=a, scalar=s, in1=b, op0=p0, op1=p1), reads, writes)

        def MSET(eng, out, val, writes):
            P.op(eng, lambda e, o=out, v=val: e.memset(o, v), (), writes)

        def DMA(out, in_, reads, writes, queue="sp"):
            P.dma(lambda e, o=out, i=in_: e.dma_start(out=o, in_=i), reads, writes, queue)

        def evac_eng():
            return ("act", "dve")[rot("ev", 2)]

        plan = []
        stream_state = {"next_issue": 0, "pos": 0}

        def issue_item(i):
            kind, args = plan[i]
            rb = i % RING
            dst = ring[rb]
            if kind == "slab":
                s = args
                DMA(dst[:, :], wbf_d[s, :, :], [bwd[s]], bring[rb])
            elif kind == "cdiag":
                DMA(dst[:, :], cd_d[args, :, :], [bcd[args]], bring[rb])
            else:
                h, b0, nb = args
                DMA(dst[:, 0:nb * 256], kt_d[h, :, b0 * 256:(b0 + nb) * 256],
                    [bkv[g] for g in range(b0 // 2, (b0 + nb + 1) // 2)], [bringK[rb]])
                DMA(dst[:, 4096:4096 + nb * 256].rearrange("p (c d) -> p c d", d=128),
                    v_d[h, :, b0 * 2:(b0 + nb) * 2, :],
                    [bkv[g] for g in range(b0 // 2, (b0 + nb + 1) // 2)], [bringV[rb]])

        def fetch(kind, args):
            if P.dry:
                plan.append((kind, args))
                return 0
            i = stream_state["pos"]
            stream_state["pos"] += 1
            assert plan[i] == (kind, args), (plan[i], kind, args)
            while stream_state["next_issue"] < min(len(plan), i + RING - 2) or stream_state["next_issue"] <= i:
                issue_item(stream_state["next_issue"])
                stream_state["next_issue"] += 1
            return i % RING

        def norm_transpose(X, bX, npart, tokoff):
            MSET("dve", st[0:npart, 0:1], 0.0, [B["st"]])
            ACTV(xs[0:npart, :], X[0:npart, :], AF.Square, [bX, B["st"]], bxs_all + [B["st"]], accum=st[0:npart, 0:1])
            ACTV(st[0:npart, 1:2], st[0:npart, 0:1], AF.Sqrt, [B["st"], B["zcol"]], [B["st"]], bias=epsc[0:npart, 0:1], scale=1.0 / D)
            P.op("dve", lambda e, o=st[0:npart, 1:2], i=st[0:npart, 1:2]: e.reciprocal(out=o, in_=i), [B["st"]], [B["st"]])
            TS("dve", xs[0:npart, :], X[0:npart, :], st[0:npart, 1:2], None, ALU.mult, None, [bX, B["st"]], bxs_all)
            for half in range(2):
                tb = rot("tr", 2)
                for c in range(8):
                    dc = half * 8 + c
                    TR(tr[tb][:, c * 128:c * 128 + npart], xs[0:npart, dc * 128:(dc + 1) * 128], ident[0:npart, 0:npart],
                       bxs_all + [B["ident"]], [btr[tb]])
                src = tr[tb][:, :].rearrange("p (c t) -> p c t", c=8)[:, :, 0:npart]
                CPY(evac_eng(), nT[:, half * 8:(half + 1) * 8, tokoff:tokoff + npart], src, [btr[tb]], [bnT])

        allGA_ = bGA
        allYA_ = bYA

        def emit_all():
            MSET("pool", ident[:, :], 1.0, [B["ident"]])
            P.op("pool", lambda e: e.affine_select(out=ident[:, :], in_=ident[:, :], pattern=[[-1, 128]],
                                                   compare_op=ALU.is_equal, fill=0.0, base=0, channel_multiplier=1),
                 [B["ident"]], [B["ident"]])
            MSET("pool", onesf[:, :], 1.0 / 1024.0, [B["onesf"]])
            MSET("pool", zcol[:, :], 0.0, [B["zcol"]])
            MSET("pool", epsc[:, :], EPS, [B["zcol"]])
            DMA(gvec[:, :], gv_d[:, :], [], [B["gvec"]])
            DMA(cw[:, :], cw_d[:, :], [], [B["cw"]])
            DMA(cp[:, :], cp_d[:, :], [], [B["cp"]])
            DMA(c31m[:, :], c31_d[:, :], [], [B["c31m"]])
            DMA(relaug[:, :], rel_d[:, :], [], [B["relaug"]])
            DMA(ohs[:, :], oh_d[:, :], [], [B["ohs"]])
            for c0 in range(0, LFP, 512):
                w = min(512, LFP - c0)
                mb = rot("mm", 3)
                MM(mm[mb][0:8, 0:w], relaug[:, :], ohs[:, c0:c0 + w], True, True, [B["relaug"], B["ohs"]], [bmm[mb]])
                ACTV(fps[:, c0:c0 + w], mm[mb][0:8, 0:w], AF.Copy, [bmm[mb]], [B["fps"]], scale=SQ)
            DMA(fp_d[:, :], fps[:, :], [B["fps"]], [B["fpd"]], queue="pool")
            MSET("pool", TT[:, :, :], 0.0, [B["TT"]])
            for h in range(NH):
                TS("dve", TT[:, h, :], TT[:, h, :], c31m[:, h:h + 1], None, ALU.add, None,
                   [B["TT"], B["c31m"]], [B["TT"]])
            MSET("pool", TT[0:64, :, 58:64], 0.0, [B["TT"]])
            MSET("pool", TT[0:64, :, 64:65], -NEG, [B["TT"]])
            MSET("pool", TT[0:64, :, 65:136], NEG, [B["TT"]])
            MSET("pool", TT[64:128, :, 59:65], 0.0, [B["TT"]])
            MSET("pool", TT[64:128, :, 65:66], -NEG, [B["TT"]])
            MSET("pool", TT[64:128, :, 66:136], NEG, [B["TT"]])

            for j in range(4):
                for ccl in range(2):
                    for k in range(KCONV):
                        cc = j * 2 + ccl
                        di = (ccl * KCONV + k) * 128
                        TS("dve", ring[j][:, di:di + 128], ident[:, :], cw[:, cc * KCONV + k:cc * KCONV + k + 1], None,
                           ALU.mult, None, [B["ident"], B["cw"]], bring[j])
                DMA(cd_d[j, :, 0:2 * KCONV * 128], ring[j][:, 0:2 * KCONV * 128], bring[j], [bcd[j]], queue="pool")
            gcol = {}
            for s_ in range(0, 10):
                gcol[s_] = 0
            gcol[14] = 16
            gcol[15] = 32
            gcol[16] = 32
            for s_ in range(18, 34):
                gcol[s_] = 48
            stg = [GA[:, 0:2048], GA[:, 2048:4096], YA[:, 0:2048], YA[:, 2048:4096]]
            GAb_ = GA[:, :].bitcast(BF16)
            pob = [GAb_[:, 8192:10240], GAb_[:, 10240:12288]]
            pk = {"k": 0}
            P.op("pool", lambda e: e.memset(dummy[:, :], 0.0), [bYA[0], bGA[0]] + allGA_ + allYA_, pstg + pout + [B["dummy"]])

            def prep_load(s_, qs):
                k = pk["k"]
                pk["k"] += 1
                DMA(stg[k % 4], wall_d[s_, :, qs * 2048:(qs + 1) * 2048], [], [pstg[k % 4]], queue="pool")
                return (s_, qs, k)

            def prep_compute(item):
                s_, qs, k = item
                si = k % 4
                oi = k % 2
                if s_ in gcol:
                    for dl in range(4):
                        eng = ("dve", "act")[(k + dl) % 2]
                        gc = gvec[:, gcol[s_] + qs * 4 + dl:gcol[s_] + qs * 4 + dl + 1]
                        o = pob[oi][:, dl * 512:(dl + 1) * 512]
                        i_ = stg[si][:, dl * 512:(dl + 1) * 512]
                        if eng == "act":
                            ACTV(o, i_, AF.Copy, [pstg[si], B["gvec"]], [pout[oi]], scale=gc)
                        else:
                            TS(eng, o, i_, gc, None, ALU.mult, None, [pstg[si], B["gvec"]], [pout[oi]])
                else:
                    for q in range(2):
                        eng = ("dve", "act")[(k + q) % 2]
                        CPY(eng, pob[oi][:, q * 1024:(q + 1) * 1024], stg[si][:, q * 1024:(q + 1) * 1024],
                            [pstg[si]], [pout[oi]])
                DMA(wbf_d[s_, :, qs * 2048:(qs + 1) * 2048], pob[oi], [pout[oi]], [bwd[s_]], queue="pool")

            pending = []

            def prep_tick(n_new=2):
                while pending:
                    prep_compute(pending.pop(0))
                for _ in range(n_new):
                    if rest_items:
                        pending.append(prep_load(*rest_items.pop(0)))

            for s_ in (2, 3, 4, 5):
                for qs in range(4):
                    prep_compute(prep_load(s_, qs))
            rest_items = [(s_, qs) for s_ in range(NSLAB) if s_ not in (2, 3, 4, 5) for qs in range(4)]

            MSET("pool", ksum[:, :, :], 0.0, [B["ksum"]])
            for s_i, s in enumerate((2, 3, 4, 5)):
                DMA(ring[s_i][:, :], wbf_d[s, :, :], [bwd[s]], bring[s_i])
            Wk = [ring[0][:, :].rearrange("p (c f) -> p c f", c=16), ring[1][:, :].rearrange("p (c f) -> p c f", c=16)]
            Wv = [ring[2][:, :].rearrange("p (c f) -> p c f", c=16), ring[3][:, :].rearrange("p (c f) -> p c f", c=16)]
            ntile_a = S // 128
            for i in range(min(3, ntile_a)):
                DMA(H[i % 4][:, :], xa_d[i, :, :], [], [bH[i % 4]])
            for ag in range(NAG):
                for t in range(4):
                    ti = ag * 4 + t
                    prep_tick()
                    if ti + 3 < ntile_a:
                        DMA(H[(ti + 3) % 4][:, :], xa_d[ti + 3, :, :], [], [bH[(ti + 3) % 4]])
                    norm_transpose(H[ti % 4], bH[ti % 4], 128, t * 128)
                for h in range(NH):
                    mb = rot("mm", 3)
                    for dc in range(DC):
                        MM(mm[mb][:, :], Wk[h // 4][:, dc, (h % 4) * 128:(h % 4 + 1) * 128], nT[:, dc, 0:512],
                           dc == 0, dc == DC - 1, bring[h // 4] + [bnT], [bmm[mb]])
                    for bl in range(2):
                        ACTV(QT[:, h, bl * 256:(bl + 1) * 256], mm[mb][:, bl * 256:(bl + 1) * 256], AF.Copy,
                             [bmm[mb], B["ksum"]], [bQT[h], B["ksum"]],
                             accum=ksum[:, h, ag * 2 + bl:ag * 2 + bl + 1])
                for t in range(4):
                    for sl in range(2):
                        mb = rot("mm", 3)
                        for dc in range(DC):
                            MM(mm[mb][:, :], nT[:, dc, t * 128:(t + 1) * 128], Wv[sl][:, dc, :],
                               dc == 0, dc == DC - 1, bring[2 + sl] + [bnT], [bmm[mb]])
                        CPY("dve", CT[:, t * 2 + sl, :], mm[mb][:, :], [bmm[mb]], [bCT[t * 2 + sl]])
                DMA(kt_d[:, :, ag * 512:(ag + 1) * 512].rearrange("h p t -> p h t"), QT[:, :, :],
                    bQT, [bkv[ag]], queue="sp")
                for t in range(4):
                    DMA(v_d[:, :, ag * 4 + t, :].rearrange("h p d -> p h d"),
                        CT2[:, t * 1024:(t + 1) * 1024].rearrange("p (h d) -> p h d", d=128),
                        bCT[2 * t:2 * t + 2], [bkv[ag]], queue="sp")
            while rest_items or pending:
                prep_tick()
            P.op("pool", lambda e: e.memset(dummy[:, :], 0.0), pstg + pout, pstg + pout + allGA_ + allYA_ + [B["dummy"]])
            TS("dve", kb32[:, :, :], ksum[:, :, :], 1.0 / 256.0, None, ALU.mult, None, [B["ksum"]], [B["kb32"]])
            CPY("dve", khi[:, :, :], kb32[:, :, :], [B["kb32"]], [B["khi"]])
            CPY("dve", ksum[:, :, :], khi[:, :, :], [B["khi"]], [B["ksum"]])
            TTO("dve", klo[:, :, :], kb32[:, :, :], ksum[:, :, :], ALU.subtract, [B["kb32"], B["ksum"]], [B["klo"]])

            for i in range(2):
                DMA(H[i][:, :], mem_d[i, :, :], [], [bH[i]])
                norm_transpose(H[i], bH[i], 128, i * 128)
            rk = fetch("slab", 15)
            wkc = ring[rk][:, :].rearrange("p (c f) -> p c f", c=16)
            for h4 in range(4):
                mb = rot("mm", 3)
                for dc in range(DC):
                    MM(mm[mb][:, 0:256], wkc[:, dc, h4 * 128:(h4 + 1) * 128], nT[:, dc, 0:256], dc == 0, dc == DC - 1,
                       bring[rk] + [bnT], [bmm[mb]])
                CPY(evac_eng(), kcT[:, h4, :], mm[mb][:, 0:256], [bmm[mb]], [B["kcT"]])
            rv = fetch("slab", 16)
            wvc = ring[rv][:, :].rearrange("p (c f) -> p c f", c=16)
            for i in range(2):
                mb = rot("mm", 3)
                for dc in range(DC):
                    MM(mm[mb][:, :], nT[:, dc, i * 128:(i + 1) * 128], wvc[:, dc, :], dc == 0, dc == DC - 1,
                       bring[rv] + [bnT], [bmm[mb]])
                CPY(evac_eng(), vcs[:, i, :], mm[mb][:, :], [bmm[mb]], [B["vcs"]])

            for m in range(NG):
                emit_group(m)

        def emit_group(m):
            allGA = bGA
            allYA = bYA
            for t in range(4):
                DMA(H[t][:, :], xo_d[m * 4 + t, :, :], [], [bH[t]])
            for t in range(4):
                norm_transpose(H[t], bH[t], 128, t * 128)
            for i in range(2):
                hx = YA[:, i * 2048:(i + 1) * 2048]
                DMA(hx, xh_d[m * 2 + i, :, :], [], [bYA[i * 4 + j] for j in range(4)])
                norm_transpose_from(hx, [bYA[i * 4 + j] for j in range(4)], i * 128)
            slab_of = {}
            for cc in range(8):
                if cc % 4 == 0:
                    slab_of["v"] = fetch("slab", 6 + cc // 4)
                    slab_of["g"] = fetch("slab", 8 + cc // 4)
                wv_ = ring[slab_of["v"]][:, :].rearrange("p (c f) -> p c f", c=16)
                wg_ = ring[slab_of["g"]][:, :].rearrange("p (c f) -> p c f", c=16)
                fo = (cc % 4) * 128
                mv = rot("mm", 3)
                mg = rot("mm", 3)
                for dc in range(DC):
                    MM(mm[mv][:, :], wv_[:, dc, fo:fo + 128], nT[:, dc, 0:512], dc == 0, dc == DC - 1,
                       bring[slab_of["v"]] + [bnT], [bmm[mv]])
                    MM(misc[:, 0:256], wv_[:, dc, fo:fo + 128], nTh[:, dc, :], dc == 0, dc == DC - 1,
                       bring[slab_of["v"]] + bCT, [bmisc[0], bmisc[1]])
                for dc in range(DC):
                    MM(mm[mg][:, :], wg_[:, dc, fo:fo + 128], nT[:, dc, 0:512], dc == 0, dc == DC - 1,
                       bring[slab_of["g"]] + [bnT], [bmm[mg]])
                    MM(misc[:, 256:512], wg_[:, dc, fo:fo + 128], nTh[:, dc, :], dc == 0, dc == DC - 1,
                       bring[slab_of["g"]] + bCT, [bmisc[2], bmisc[3]])
                ACTV(sgt[:, 0:512], mm[mg][:, :], AF.Sigmoid, [bmm[mg]], [B["sgt"]])
                ACTV(sgt[:, 512:768], misc[:, 256:512], AF.Sigmoid, [bmisc[2], bmisc[3]], [B["sgt"]])
                TTO("dve", G4[:, cc, :, 32:96], mm[mv][:, :].rearrange("p (q t) -> p q t", q=8),
                    sgt[:, 0:512].rearrange("p (q t) -> p q t", q=8), ALU.mult, [bmm[mv], B["sgt"]], [bGA[cc]])
                TTO("dve", G4[:, cc, :, 0:32], misc[:, 0:256].rearrange("p (q t) -> p q t", q=8),
                    sgt[:, 512:768].rearrange("p (q t) -> p q t", q=8), ALU.mult,
                    [bmisc[0], bmisc[1], B["sgt"]], [bGA[cc]])
            for cp_ in range(4):
                rcd = fetch("cdiag", cp_)
                for ccl in range(2):
                    cc = cp_ * 2 + ccl
                    mb = rot("mm", 3)
                    for k in range(KCONV):
                        di = (ccl * KCONV + k) * 128
                        MM(mm[mb][:, :].rearrange("p (q t) -> p q t", q=8), ring[rcd][:, di:di + 128],
                           G4[:, cc, :, k + 2:k + 66], k == 0, k == KCONV - 1, [bGA[cc]] + bring[rcd], [bmm[mb]])
                    if cc % 2 == 0:
                        ACTV(Y3[:, cc, :], mm[mb][:, :], AF.Identity, [bmm[mb], B["cp"]], [bYA[cc]], bias=cp[:, cc:cc + 1])
                    else:
                        TS("dve", Y3[:, cc, :], mm[mb][:, :], cp[:, cc:cc + 1], None, ALU.add, None,
                           [bmm[mb], B["cp"]], [bYA[cc]])
            for h in range(NH):
                if h % 4 == 0:
                    rq = fetch("slab", h // 4)
                    wq_ = ring[rq][:, :].rearrange("p (c f) -> p c f", c=16)
                mb = rot("mm", 3)
                for dc in range(DC):
                    MM(mm[mb][:, :], wq_[:, dc, (h % 4) * 128:(h % 4 + 1) * 128], nT[:, dc, 0:512], dc == 0, dc == DC - 1,
                       bring[rq] + [bnT], [bmm[mb]])
                CPY(evac_eng(), QT[:, h, :], mm[mb][:, :], [bmm[mb]], [bQT[h]])
            GS = GA[:, 0:4096].rearrange("p (c t) -> p c t", c=8)
            for cc in range(8):
                TTO("pool", GS[:, cc, :], Y3[:, cc, :], Y3[:, cc, :], ALU.mult, [bYA[cc]] + allGA, allGA)
            m1 = rot("mm", 3)
            m2 = rot("mm", 3)
            for cc in range(8):
                MM(mm[m1][:, :], onesf[:, :], Y3[:, cc, :], cc == 0, cc == 7, [B["onesf"], bYA[cc]], [bmm[m1]])
            for cc in range(8):
                MM(mm[m2][:, :], onesf[:, :], GS[:, cc, :], cc == 0, cc == 7, [B["onesf"]] + allGA, [bmm[m2]])
            CPY("dve", mean_sb[:, :], mm[m1][:, :], [bmm[m1]], [B["mean_sb"]])
            TTO("dve", rstd_sb[:, :], mean_sb[:, :], mean_sb[:, :], ALU.mult, [B["mean_sb"]], [B["rstd_sb"]])
            TTO("dve", rstd_sb[:, :], mm[m2][:, :], rstd_sb[:, :], ALU.subtract, [bmm[m2], B["rstd_sb"]], [B["rstd_sb"]])
            ACTV(rstd_sb[:, :], rstd_sb[:, :], AF.Sqrt, [B["rstd_sb"], B["zcol"]], [B["rstd_sb"]], bias=epsc[:, 0:1])
            P.op("dve", lambda e, o=rstd_sb[:, :], i=rstd_sb[:, :]: e.reciprocal(out=o, in_=i), [B["rstd_sb"]], [B["rstd_sb"]])
            for cc in range(8):
                eng = "pool" if cc % 2 == 0 else "dve"
                TTO(eng, Y3[:, cc, :], Y3[:, cc, :], mean_sb[:, :], ALU.subtract, [bYA[cc], B["mean_sb"]], [bYA[cc]])
                TTO(eng, Y3[:, cc, :], Y3[:, cc, :], rstd_sb[:, :], ALU.mult, [bYA[cc], B["rstd_sb"]], [bYA[cc]])
                ACTV(CT[:, cc, :], Y3[:, cc, :], AF.Silu, [bYA[cc], B["cp"]], [bCT[cc]],
                     bias=cp[:, 16 + cc:17 + cc], scale=cp[:, 8 + cc:9 + cc])
            for h in range(NH):
                for t in range(4):
                    tau = m * 4 + t
                    ncol = 2 * tau + 2
                    gi = rot("g", 2)
                    rg = rot("misc", 4)
                    MSET("pool", gsb[gi][:, :], -1e30, [bgsb[gi]])
                    gp = misc[:, rg * 128:rg * 128 + 64]
                    MM(gp[:, 0:ncol], QT[:, h, t * 128:(t + 1) * 128], khi[:, h, 0:ncol], True, False,
                       [bQT[h], B["khi"]], [bmisc[rg]])
                    MM(gp[:, 0:ncol], QT[:, h, t * 128:(t + 1) * 128], klo[:, h, 0:ncol], False, True,
                       [bQT[h], B["klo"]], [bmisc[rg]])
                    if tau > 0:
                        CPY("dve", gsb[gi][0:64, 0:2 * tau], gp[0:64, 0:2 * tau], [bmisc[rg]], [bgsb[gi]])
                    CPY("dve", gsb[gi][64:128, 0:2 * tau + 1], gp[64:128, 0:2 * tau + 1], [bmisc[rg]], [bgsb[gi]])
                    P.op("dve", lambda e, o=mx8[gi][:, :], i=gsb[gi][:, 0:max(8, ncol)]: e.max(out=o, in_=i),
                         [bgsb[gi]], [bmx8[gi]])
                    TS("dve", mx8[gi][:, 2:3], mx8[gi][:, 2:3], -1e29, None, ALU.max, None, [bmx8[gi]], [bmx8[gi]])
                    TS("dve", m30[gi][:, 0:ncol], gsb[gi][:, 0:ncol], mx8[gi][:, 2:3], 1e30, ALU.subtract, ALU.mult,
                       [bgsb[gi], bmx8[gi]], [bm30[gi]])
                    TS("dve", m30[gi][:, 0:ncol], m30[gi][:, 0:ncol], -1.0, 0.0, ALU.max, ALU.min, [bm30[gi]], [bm30[gi]])
                    STT("dve", bmv[:, h, t, 0:ncol], m30[gi][:, 0:ncol], -NEG, TT[:, h, 64 - 2 * tau:64 - 2 * tau + ncol],
                        ALU.mult, ALU.add, [bm30[gi], B["TT"]], [bbm[h][t]])
            nblk_g = 8 * m + 8
            for h in range(NH):
                hb = h % 2
                DMA(HK[0:64, hb, :, :], bass.AP(fp_h, h * LFP, [[1, 64], [256, 8], [1, 256]]),
                    [B["fpd"]], [bHK[hb]] + allGA)
                DMA(HK[64:128, hb, :, :], bass.AP(fp_h, h * LFP + 256, [[1, 64], [256, 8], [1, 256]]),
                    [B["fpd"]], [bHK[hb]] + allGA)
                steps = []
                for kc in range((nblk_g + 15) // 16):
                    b0 = kc * 16
                    nb = min(16, nblk_g - b0)
                    for t in range(4):
                        tau = m * 4 + t
                        blks = list(range(b0, min(b0 + nb, 2 * tau + 2)))
                        npair = len(blks) // 2
                        for pi in range(npair):
                            steps.append(dict(kc=kc, b0=b0, nb=nb, t=t, tau=tau, pi=pi, npair=npair,
                                              kb=(blks[2 * pi], blks[2 * pi + 1])))
                cur = {"kc": -1, "rb": 0}

                def stA(sx):
                    if sx["kc"] != cur["kc"]:
                        cur["kc"] = sx["kc"]
                        cur["rb"] = fetch("kv", (h, sx["b0"], sx["nb"]))
                    rb = cur["rb"]
                    sx["rb"] = rb
                    t, tau, b0 = sx["t"], sx["tau"], sx["b0"]
                    KTc = ring[rb][:, 0:4096]
                    sb_ = rot("sp", 2)
                    sx["sb"] = sb_
                    kb0, kb1 = sx["kb"]
                    lo = (kb0 - b0) * 256
                    qT = QT[:, h, t * 128:(t + 1) * 128]
                    if 2 * tau + 1 - kb0 <= 7:
                        for bi, kb in enumerate(sx["kb"]):
                            so = bi * 256
                            MM(sp_[sb_][:, so:so + 256], qT, KTc[:, lo + so:lo + so + 256], True, False,
                               [bQT[h], bringK[rb]], [bsp[sb_]])
                            MM(sp_[sb_][:, so:so + 256], ident[:, :], HK[:, hb, 2 * tau + 1 - kb, :], False, True,
                               [B["ident"], bHK[hb]] + allGA, [bsp[sb_]])
                    else:
                        MM(sp_[sb_][:, :], qT, KTc[:, lo:lo + 512], True, True, [bQT[h], bringK[rb]], [bsp[sb_]])
                    pb = rot("pb", 2)
                    sx["pb"] = pb
                    for bi, kb in enumerate(sx["kb"]):
                        ACTV(Pb[pb][:, bi * 256:(bi + 1) * 256], sp_[sb_][:, bi * 256:(bi + 1) * 256], AF.Exp,
                             [bsp[sb_], bbm[h][t]], [bPb[pb]], bias=bmv[:, h, t, kb:kb + 1], scale=SCALE)

                def stC(sx):
                    pb = sx["pb"]
                    kb0 = sx["kb"][0]
                    P.op("dve", lambda e, o=rs[sx["t"]][:, kb0:kb0 + 2], i=Pb[pb].rearrange("p (b k) -> p b k", b=2):
                         e.tensor_reduce(out=o, in_=i, axis=AX.X, op=ALU.add), [bPb[pb]], [brs[sx["t"]]])
                    tb = rot("tr", 2)
                    for q4 in range(4):
                        TR(tr[tb][:, q4 * 128:(q4 + 1) * 128], Pb[pb][:, q4 * 128:(q4 + 1) * 128], ident[:, :],
                           [bPb[pb], B["ident"]], [btr[tb]])
                    pt = rot("pt", 2)
                    sx["pt"] = pt
                    CPY("dve", PTs[pt][:, :], tr[tb][:, 0:512], [btr[tb]], [bPTs[pt]])

                def stE(sx):
                    rb, t, b0, pt = sx["rb"], sx["t"], sx["b0"], sx["pt"]
                    Vc = ring[rb][:, 4096:8192].rearrange("p (c d) -> p c d", d=128)
                    if sx["pi"] == 0:
                        cur["ob%d" % t] = rot("misc", 4)
                    ob = cur["ob%d" % t]
                    Ops = misc[:, ob * 128:(ob + 1) * 128]
                    for q4 in range(4):
                        kb = sx["kb"][q4 // 2]
                        vch = (kb - b0) * 2 + (q4 % 2)
                        MM(Ops, PTs[pt][:, q4 * 128:(q4 + 1) * 128], Vc[:, vch, :],
                           sx["pi"] == 0 and q4 == 0, sx["pi"] == sx["npair"] - 1 and q4 == 3,
                           [bPTs[pt], bringV[rb]], [bmisc[ob]])
                    if sx["pi"] == sx["npair"] - 1:
                        if sx["kc"] == 0:
                            CPY("dve", Oacc[t][:, :], Ops, [bmisc[ob]], [bOacc[t]])
                        else:
                            TTO("dve", Oacc[t][:, :], Oacc[t][:, :], Ops, ALU.add, [bmisc[ob], bOacc[t]], [bOacc[t]])

                ns = len(steps)
                for i in range(ns + 2):
                    if i < ns:
                        stA(steps[i])
                    if 0 <= i - 1 < ns:
                        stC(steps[i - 1])
                    if 0 <= i - 2 < ns:
                        stE(steps[i - 2])
                for t in range(4):
                    tau = m * 4 + t
                    P.op("dve", lambda e, o=st[:, 4 + t:5 + t], i=rs[t][:, 0:2 * tau + 2]:
                         e.tensor_reduce(out=o, in_=i, axis=AX.X, op=ALU.add), [brs[t]], [B["st"]])
                    P.op("dve", lambda e, o=st[:, 8 + t:9 + t], i=st[:, 4 + t:5 + t]: e.reciprocal(out=o, in_=i),
                         [B["st"]], [B["st"]])
                    TS("dve", Av[:, t, h * 128:(h + 1) * 128], Oacc[t][:, :], st[:, 8 + t:9 + t], None, ALU.mult, None,
                       [bOacc[t], B["st"]] + allYA[4:8], [bA[t]] + allYA[4:8])
            for h in range(NH):
                tb = rot("tr", 2)
                for t in range(4):
                    TR(tr[tb][:, t * 128:(t + 1) * 128], Av[:, t, h * 128:(h + 1) * 128], ident[:, :],
                       [bA[t], B["ident"]] + allYA[4:8], [btr[tb]])
                CPY(evac_eng(), QT[:, h, :], tr[tb][:, 0:512], [btr[tb]], [bQT[h]])
            for ds in range(4):
                rw = fetch("slab", 10 + ds)
                wo_ = ring[rw][:, :].rearrange("p (c f) -> p c f", c=16)
                for t in range(4):
                    mb = rot("mm", 3)
                    for ic in range(16):
                        lhs = QT[:, ic, t * 128:(t + 1) * 128] if ic < 8 else CT[:, ic - 8, t * 128:(t + 1) * 128]
                        MM(mm[mb][:, :], lhs, wo_[:, ic, :], ic == 0, ic == 15,
                           [bQT[ic] if ic < 8 else bCT[ic - 8]] + bring[rw], [bmm[mb]])
                    TTO("dve", H[t][:, ds * 512:(ds + 1) * 512], H[t][:, ds * 512:(ds + 1) * 512], mm[mb][:, :], ALU.add,
                        [bH[t], bmm[mb]], [bH[t]])
            for t in range(4):
                norm_transpose(H[t], bH[t], 128, t * 128)
            rq = fetch("slab", 14)
            wqc = ring[rq][:, :].rearrange("p (c f) -> p c f", c=16)
            for h4 in range(4):
                mb = rot("mm", 3)
                for dc in range(DC):
                    MM(mm[mb][:, :], wqc[:, dc, h4 * 128:(h4 + 1) * 128], nT[:, dc, 0:512], dc == 0, dc == DC - 1,
                       bring[rq] + [bnT], [bmm[mb]])
                CPY(evac_eng(), QT[:, h4, :], mm[mb][:, :], [bmm[mb]], [bQT[h4]])
            for t in range(4):
                MSET("pool", rs[t][:, 0:4], 0.0, [brs[t]])
                for hp in range(2):
                    sb_ = rot("sp", 2)
                    pb = rot("pb", 2)
                    for hh in range(2):
                        h4 = hp * 2 + hh
                        MM(sp_[sb_][:, hh * 256:(hh + 1) * 256], QT[:, h4, t * 128:(t + 1) * 128], kcT[:, h4, :], True, True,
                           [bQT[h4], B["kcT"]], [bsp[sb_]])
                    for hh in range(2):
                        h4 = hp * 2 + hh
                        ACTV(Pb[pb][:, hh * 256:(hh + 1) * 256], sp_[sb_][:, hh * 256:(hh + 1) * 256], AF.Exp,
                             [bsp[sb_], brs[t], B["zcol"]], [bPb[pb], brs[t]], bias=zcol[:, 0:1], scale=SCALE,
                             accum=rs[t][:, h4:h4 + 1])
                    tb = rot("tr", 2)
                    for q4 in range(4):
                        TR(tr[tb][:, q4 * 128:(q4 + 1) * 128], Pb[pb][:, q4 * 128:(q4 + 1) * 128], ident[:, :],
                           [bPb[pb], B["ident"]], [btr[tb]])
                    pt = rot("pt", 2)
                    CPY("dve", PTs[pt][:, :], tr[tb][:, 0:512], [btr[tb]], [bPTs[pt]])
                    for hh in range(2):
                        h4 = hp * 2 + hh
                        ob = rot("misc", 4)
                        Ops = misc[:, ob * 128:(ob + 1) * 128]
                        for mc in range(2):
                            MM(Ops, PTs[pt][:, (hh * 2 + mc) * 128:(hh * 2 + mc + 1) * 128], vcs[:, mc, h4 * 128:(h4 + 1) * 128],
                               mc == 0, mc == 1, [bPTs[pt], B["vcs"]], [bmisc[ob]])
                        P.op("dve", lambda e, o=st[:, 12:13], i=rs[t][:, h4:h4 + 1]: e.reciprocal(out=o, in_=i),
                             [brs[t]], [B["st"]])
                        TS("dve", Av[:, t, h4 * 128:(h4 + 1) * 128], Ops, st[:, 12:13], None, ALU.mult, None,
                           [bmisc[ob], B["st"]] + allYA[4:8], [bA[t]] + allYA[4:8])
            for h4 in range(4):
                tb = rot("tr", 2)
                for t in range(4):
                    TR(tr[tb][:, t * 128:(t + 1) * 128], Av[:, t, h4 * 128:(h4 + 1) * 128], ident[:, :],
                       [bA[t], B["ident"]] + allYA[4:8], [btr[tb]])
                CPY(evac_eng(), QT[:, 4 + h4, :], tr[tb][:, 0:512], [btr[tb]], [bQT[4 + h4]])
            rw = fetch("slab", 17)
            woc = ring[rw][:, :].rearrange("p (c f) -> p c f", c=4)
            for t in range(4):
                for ds in range(4):
                    mb = rot("mm", 3)
                    for ic in range(4):
                        MM(mm[mb][:, :], QT[:, 4 + ic, t * 128:(t + 1) * 128], woc[:, ic, ds * 512:(ds + 1) * 512],
                           ic == 0, ic == 3, [bQT[4 + ic]] + bring[rw], [bmm[mb]])
                    TTO("dve", H[t][:, ds * 512:(ds + 1) * 512], H[t][:, ds * 512:(ds + 1) * 512], mm[mb][:, :], ALU.add,
                        [bH[t], bmm[mb]], [bH[t]])
            for t in range(4):
                norm_transpose(H[t], bH[t], 128, t * 128)
            for fh in range(2):
                for sl in range(8):
                    rw = fetch("slab", 18 + fh * 8 + sl)
                    w1_ = ring[rw][:, :].rearrange("p (c f) -> p c f", c=16)
                    for f4 in range(4):
                        fl = sl * 4 + f4
                        mb = rot("mm", 3)
                        for dc in range(DC):
                            MM(mm[mb][:, :], w1_[:, dc, f4 * 128:(f4 + 1) * 128], nT[:, dc, 0:512], dc == 0, dc == DC - 1,
                               bring[rw] + [bnT], [bmm[mb]])
                        ri = rot("rt", 2)
                        ACTV(rtmp[ri][:, :], mm[mb][:, :], AF.Relu, [bmm[mb]], [brtmp[ri]])
                        dst = h1a[:, fl, :] if fl < 16 else h1b[:, fl - 16, :]
                        dbuf = allYA if fl < 16 else allGA
                        TTO("pool", dst, rtmp[ri][:, :], rtmp[ri][:, :], ALU.mult, [brtmp[ri]] + dbuf, dbuf)
                for ds in range(4):
                    r2 = [fetch("slab", 34 + ds * 4 + fh * 2 + q) for q in range(2)]
                    for t in range(4):
                        mb = rot("mm", 3)
                        for fl in range(32):
                            src = h1a[:, fl, t * 128:(t + 1) * 128] if fl < 16 else h1b[:, fl - 16, t * 128:(t + 1) * 128]
                            w2_ = ring[r2[fl // 16]][:, :].rearrange("p (c f) -> p c f", c=16)
                            MM(mm[mb][:, :], src, w2_[:, fl % 16, :], fl == 0, fl == 31,
                               (allYA if fl < 16 else allGA) + bring[r2[fl // 16]], [bmm[mb]])
                        TTO("dve", H[t][:, ds * 512:(ds + 1) * 512], H[t][:, ds * 512:(ds + 1) * 512], mm[mb][:, :], ALU.add,
                            [bH[t], bmm[mb]], [bH[t]])
            DMA(gfin_v, gfin_d[:, :], allGA, allGA)
            for t in range(4):
                MSET("dve", st[:, 0:1], 0.0, [B["st"]])
                ACTV(xs[:, :], H[t][:, :], AF.Square, [bH[t], B["st"]], bxs_all + [B["st"]], accum=st[:, 0:1])
                ACTV(st[:, 1:2], st[:, 0:1], AF.Sqrt, [B["st"], B["zcol"]], [B["st"]], bias=epsc[:, 0:1], scale=1.0 / D)
                P.op("dve", lambda e, o=st[:, 1:2], i=st[:, 1:2]: e.reciprocal(out=o, in_=i), [B["st"]], [B["st"]])
                STT("dve", H[t][:, :], H[t][:, :], st[:, 1:2], gfin_v, ALU.mult, ALU.mult,
                    [bH[t], B["st"]] + allGA, [bH[t]])
                DMA(out_d[m * 4 + t, :, :], H[t][:, :], [bH[t]], [], queue="pool")

        def norm_transpose_from(X, bXl, tokoff):
            MSET("dve", st[:, 0:1], 0.0, [B["st"]])
            ACTV(xs[:, :], X, AF.Square, bXl + [B["st"]], bxs_all + [B["st"]], accum=st[:, 0:1])
            ACTV(st[:, 1:2], st[:, 0:1], AF.Sqrt, [B["st"], B["zcol"]], [B["st"]], bias=epsc[:, 0:1], scale=1.0 / D)
            P.op("dve", lambda e, o=st[:, 1:2], i=st[:, 1:2]: e.reciprocal(out=o, in_=i), [B["st"]], [B["st"]])
            TS("dve", xs[:, :], X, st[:, 1:2], None, ALU.mult, None, bXl + [B["st"]], bxs_all)
            for half in range(2):
                tb = rot("tr", 2)
                for c in range(8):
                    dc = half * 8 + c
                    TR(tr[tb][:, c * 128:(c + 1) * 128], xs[:, dc * 128:(dc + 1) * 128], ident[:, :],
                       bxs_all + [B["ident"]], [btr[tb]])
                CPY(evac_eng(), nTh[:, half * 8:(half + 1) * 8, tokoff:tokoff + 128],
                    tr[tb][:, :].rearrange("p (c t) -> p c t", c=8), [btr[tb]], bCT)

        P.dry = True
        saved = dict(cnt)
        emit_all()
        P.dry = False
        cnt.update(saved)
        emit_all()
        assert stream_state["pos"] == len(plan)
        P.emit(nc)
    return nc


def _t5_bucket_np(d):
    d = np.asarray(d, dtype=np.int64)
    max_exact = 16
    nf = np.maximum(d, max_exact).astype(np.float32)
    large = max_exact + (np.log(nf / np.float32(max_exact)) / np.float32(math.log(2048 / max_exact))
                         * np.float32(16)).astype(np.int32)
    large = np.minimum(large, 31)
    return np.where(d < max_exact, d, large)


def _slab(W, f0):
    return np.ascontiguousarray(W[:, f0:f0 + 512].reshape(16, 128, 512).transpose(1, 0, 2)).reshape(128, 8192)


def _prep_shared(inp):
    w_in = inp["w_in"][0]
    slabs = [_slab(w_in, f0) for f0 in range(0, 5120, 512)]
    slabs += [_slab(inp["w_out"][0], f0) for f0 in range(0, 2048, 512)]
    slabs.append(_slab(inp["wq_c"][0], 0))
    slabs.append(_slab(inp["wk_c"][0], 0))
    slabs.append(_slab(inp["wv_c"][0], 0))
    slabs.append(np.ascontiguousarray(inp["wo_c"][0].reshape(4, 128, 2048).transpose(1, 0, 2)).reshape(128, 8192))
    w1 = inp["w1"][0]
    slabs += [_slab(w1, f0) for f0 in range(0, 8192, 512)]
    w2 = inp["w2"][0]
    for ds in range(4):
        for fq in range(4):
            slabs.append(_slab(w2[fq * 2048:(fq + 1) * 2048], ds * 512))
    wall = np.stack(slabs).astype(np.float32)
    pc = lambda v: np.ascontiguousarray(v.reshape(-1, 128).T)
    gvec = np.concatenate([pc(inp["g_mix"][0]), pc(inp["g_cross"][0]), pc(inp["g_mem"][0]), pc(inp["g_mlp"][0])], axis=1)
    gfin = np.ascontiguousarray(np.broadcast_to(inp["g_final"][None, :], (128, D)))
    cw = np.ascontiguousarray(inp["conv_w"][0].reshape(KCONV, 8, 128).transpose(2, 1, 0)).reshape(128, 8 * KCONV)
    cp = np.concatenate([pc(inp["conv_b"][0]), pc(inp["conv_ln_g"][0]), pc(inp["conv_ln_b"][0])], axis=1)
    rel = inp["rel_bias"]
    relaug = np.concatenate([rel, np.full((1, 8), NEG, np.float32)], axis=0)
    c31 = np.ascontiguousarray(np.broadcast_to(rel[31][None, :], (128, 8)))
    return dict(wall=wall, gvec=gvec.astype(np.float32), gfin=gfin.astype(np.float32), convw=cw.astype(np.float32),
                convp=cp.astype(np.float32), relaug=relaug.astype(np.float32), c31=c31.astype(np.float32))


def _oh_table(r):
    oh = np.zeros((33, LFP), np.float32)
    y = np.arange(LFP)
    x = y + r * 64
    neg = x < 511
    oh[32, neg] = 1.0
    d = np.maximum(x - 511, 0)
    bk = _t5_bucket_np(d)
    oh[bk[~neg], y[~neg]] = 1.0
    return oh


def _core_inputs(inp, b, r, NBLK, shared, xa_cache):
    S = NBLK * 256
    x = inp["x"][b]
    xb = x.reshape(NBLK, 256, D)
    own = xb[:, r * 64:(r + 1) * 64, :]
    xo = np.ascontiguousarray(own.reshape(NBLK // 2, 128, D))
    xpad = np.concatenate([np.zeros((32, D), np.float32), x], axis=0)
    starts = (np.arange(NBLK) * 256 + r * 64)
    idx = starts[:, None] + np.arange(32)[None, :]
    xh = np.ascontiguousarray(xpad[idx].reshape(NBLK // 4, 128, D))
    if b not in xa_cache:
        xa_cache[b] = np.ascontiguousarray(xb[:, ::-1, :].reshape(S // 128, 128, D))
    d = dict(shared)
    d.update(xo=xo, xh=xh, xa=xa_cache[b], mem=np.ascontiguousarray(inp["mem"][b].reshape(2, 128, D)), oh=_oh_table(r))
    return d


_NC_CACHE = {}


def kernel(**inputs):
    inp = {k: np.asarray(v) for k, v in inputs.items()}
    Bn, S, _ = inp["x"].shape
    NBLK = S // 256
    shared = _prep_shared(inp)
    xa_cache = {}
    in_maps = []
    for b in range(Bn):
        for r in range(4):
            in_maps.append(_core_inputs(inp, b, r, NBLK, shared, xa_cache))
    if NBLK not in _NC_CACHE:
        _NC_CACHE[NBLK] = build_program(NBLK)
    nc = _NC_CACHE[NBLK]
    res = run_bass_kernel_spmd(nc, in_maps, core_ids=list(range(len(in_maps))))
    out = np.zeros((Bn, S, D), np.float32)
    ov = out.reshape(Bn, NBLK, 256, D)
    for b in range(Bn):
        for r in range(4):
            o = np.asarray(res.results[b * 4 + r]["out"]).reshape(NBLK, 64, D)
            ov[b, :, r * 64:(r + 1) * 64, :] = o
    return out
```

```python
import contextlib
import math
import numpy as np
import concourse.bass as bass
import concourse.mybir as mybir
from concourse.bass_utils import run_bass_kernel_spmd

F32 = mybir.dt.float32
BF16 = mybir.dt.bfloat16
ALU = mybir.AluOpType
AF = mybir.ActivationFunctionType
AX = mybir.AxisListType

D = 2048
DC = 16
NH = 8
NMEM = 256
KCONV = 31
EPS = 1e-6
NEG = -1000.0
SCALE = 128.0 ** -0.5
SQ = 128.0 ** 0.5
LFP = 2432
NSLAB = 50
RING = 4

COMPUTE = ("pe", "act", "dve", "pool")
NDMASEM = 14


class Buf:
    __slots__ = ("name", "lw", "rd")

    def __init__(self, name):
        self.name = name
        self.lw = None
        self.rd = []


class Ins:
    __slots__ = ("eng", "fn", "deps", "idx", "sig", "seq", "dsem", "dcnt", "waits", "src")

    def __init__(self, eng, fn):
        self.eng = eng
        self.fn = fn
        self.deps = []
        self.idx = -1
        self.sig = False
        self.seq = 0
        self.dsem = -1
        self.dcnt = 0
        self.waits = []
        self.src = None


class Prog:
    def __init__(self, same_engine_sync=True):
        self.streams = {e: [] for e in COMPUTE + ("sp",)}
        self.order = []
        self.same = same_engine_sync
        self.ndma = 0
        self.dma_last = [None] * NDMASEM
        self.dma_cnt = [0] * NDMASEM
        self.dry = False

    def _track(self, ins, reads, writes):
        deps = {}
        for b in reads:
            if b.lw is not None:
                deps[id(b.lw)] = b.lw
        for b in writes:
            if b.lw is not None:
                deps[id(b.lw)] = b.lw
            for r in b.rd:
                deps[id(r)] = r
        for b in reads:
            b.rd.append(ins)
        for b in writes:
            b.lw = ins
            b.rd = []
        ins.deps = list(deps.values())

    def op(self, eng, fn, reads=(), writes=()):
        if self.dry:
            return None
        ins = Ins(eng, fn)
        ins.src = eng
        self._track(ins, reads, writes)
        ins.idx = len(self.streams[eng])
        self.streams[eng].append(ins)
        self.order.append(ins)
        return ins

    def dma(self, fn, reads=(), writes=(), queue="sp"):
        if self.dry:
            return None
        ins = Ins(queue, fn)
        k = self.ndma % NDMASEM
        self.ndma += 1
        ins.dsem = k
        self.dma_cnt[k] += 1
        ins.dcnt = self.dma_cnt[k]
        ins.src = "d%d" % k
        self._track(ins, reads, writes)
        if self.dma_last[k] is not None:
            ins.deps.append(self.dma_last[k])
        self.dma_last[k] = ins
        ins.idx = len(self.streams[queue])
        self.streams[queue].append(ins)
        self.order.append(ins)
        return ins

    def analyze(self):
        vdone = {}
        last_start = {e: {} for e in self.streams}
        for ins in self.order:
            vc = dict(last_start[ins.eng])
            for d in ins.deps:
                if d.dsem >= 0:
                    src, val = d.src, d.dcnt
                else:
                    src, val = d.eng, d.idx + 1
                    if d.eng == ins.eng and ins.dsem < 0 and (d.eng == "pe" or not self.same):
                        continue
                if vc.get(src, 0) >= val:
                    continue
                d.sig = True
                ins.waits.append(d)
                for s, v in vdone[id(d)].items():
                    if vc.get(s, 0) < v:
                        vc[s] = v
            last_start[ins.eng] = vc
            vd = dict(vc)
            if ins.dsem >= 0:
                vd[ins.src] = ins.dcnt
            else:
                vd[ins.eng] = ins.idx + 1
            vdone[id(ins)] = vd
            ins.deps = None
        for ins in self.order:
            if len(ins.waits) > 1:
                best = {}
                for d in ins.waits:
                    val = d.dcnt if d.dsem >= 0 else d.idx
                    if d.src not in best or val > best[d.src][0]:
                        best[d.src] = (val, d)
                ins.waits = [v[1] for v in best.values()]
        for e in COMPUTE:
            n = 0
            for ins in self.streams[e]:
                if ins.dsem < 0 and ins.sig:
                    n += 1
                    ins.seq = n

    def emit(self, nc, final_wait_queue="sp"):
        self.analyze()
        with contextlib.ExitStack() as es:
            sems = {e: es.enter_context(nc.semaphore("s_" + e)) for e in COMPUTE}
            dsems = [es.enter_context(nc.semaphore("dq%d" % k)) for k in range(NDMASEM)]
            block = es.enter_context(nc.Block())

            def run(engname, eng):
                for ins in self.streams[engname]:
                    for d in ins.waits:
                        if d.dsem >= 0:
                            eng.wait_ge(dsems[d.dsem], 16 * d.dcnt)
                        else:
                            eng.wait_ge(sems[d.eng], d.seq)
                    r = ins.fn(eng)
                    if ins.dsem >= 0:
                        r.then_inc(dsems[ins.dsem], 16)
                    elif ins.sig:
                        r.then_inc(sems[ins.eng], 1)
                if engname == final_wait_queue:
                    for k in range(NDMASEM):
                        if self.dma_cnt[k]:
                            eng.wait_ge(dsems[k], 16 * self.dma_cnt[k])

            @block.tensor
            def _(e):
                run("pe", e)

            @block.scalar
            def _(e):
                run("act", e)

            @block.vector
            def _(e):
                run("dve", e)

            @block.gpsimd
            def _(e):
                run("pool", e)

            @block.sync
            def _(e):
                run("sp", e)


def build_program(NBLK, debug=False):
    S = NBLK * 256
    NG = NBLK // 8
    NT = NBLK // 2
    NAG = NBLK // 2
    nc = bass.Bass("TRN2", target_bir_lowering=False)
    dt_in = lambda name, shape: nc.dram_tensor(name, shape, F32, kind="ExternalInput").ap()
    xo_d = dt_in("xo", [NT, 128, D])
    xh_d = dt_in("xh", [NG * 2, 128, D])
    xa_d = dt_in("xa", [S // 128, 128, D])
    mem_d = dt_in("mem", [2, 128, D])
    wall_d = dt_in("wall", [NSLAB, 128, 8192])
    gv_d = dt_in("gvec", [128, 64])
    gfin_d = dt_in("gfin", [128, D])
    cw_d = dt_in("convw", [128, 8 * KCONV])
    cp_d = dt_in("convp", [128, 24])
    rel_d = dt_in("relaug", [33, 8])
    c31_d = dt_in("c31", [128, 8])
    oh_d = dt_in("oh", [33, LFP])
    out_d = nc.dram_tensor("out", [NT, 128, D], F32, kind="ExternalOutput").ap()
    wbf_d = nc.dram_tensor("wbf", [NSLAB, 128, 8192], BF16, kind="Internal").ap()
    kt_d = nc.dram_tensor("ktd", [NH, 128, S], BF16, kind="Internal").ap()
    v_d = nc.dram_tensor("vd", [NH, 128, S // 128, 128], BF16, kind="Internal").ap()
    cd_d = nc.dram_tensor("cdd", [4, 128, 8192], BF16, kind="Internal").ap()
    fp_h = nc.dram_tensor("fpd", [NH, LFP], BF16, kind="Internal")
    fp_d = fp_h.ap()

    P = Prog()
    es = contextlib.ExitStack()
    with es:
        def SB(name, shape, dt):
            return es.enter_context(nc.sbuf_tensor("sb_" + name, shape, dt))

        def PS(name, shape, dt):
            return es.enter_context(nc.psum_tensor("ps_" + name, shape, dt))

        H = [SB("H%d" % i, [128, D], F32) for i in range(4)]
        bH = [Buf("H%d" % i) for i in range(4)]
        ring = [SB("ring%d" % i, [128, 8192], BF16) for i in range(RING)]
        bringK = [Buf("ringK%d" % i) for i in range(RING)]
        bringV = [Buf("ringV%d" % i) for i in range(RING)]
        bring = [[bringK[i], bringV[i]] for i in range(RING)]
        bcd = [Buf("cd%d" % j) for j in range(4)]
        nT2 = SB("nT", [128, DC * 512], BF16)
        nT = nT2[:, :].rearrange("p (c t) -> p c t", c=DC)
        bnT = Buf("nT")
        QT2 = SB("QT", [128, NH * 512], BF16)
        QT = QT2[:, :].rearrange("p (h t) -> p h t", h=NH)
        bQT = [Buf("QT%d" % h) for h in range(NH)]
        CT2 = SB("CT", [128, 8 * 512], BF16)
        CT = CT2[:, :].rearrange("p (c t) -> p c t", c=8)
        nTh = CT2[:, :].rearrange("p (c t) -> p c t", c=DC)
        bCT = [Buf("CT%d" % c) for c in range(8)]
        GA = SB("GA", [128, 6144], F32)
        YA = SB("YA", [128, 4096], F32)
        bGA = [Buf("GA%d" % c) for c in range(8)]
        bYA = [Buf("YA%d" % c) for c in range(8)]
        xs = SB("xs", [128, D], BF16)
        bxs = Buf("xs")
        ident = SB("ident", [128, 128], BF16)
        onesf = SB("onesf", [128, 128], F32)
        gvec = SB("gvec", [128, 64], F32)
        cw = SB("cw", [128, 8 * KCONV], F32)
        cp = SB("cp", [128, 24], F32)
        c31m = SB("c31m", [128, 8], F32)
        relaug = SB("relaug", [33, 8], F32)
        ohs = YA[0:33, 0:LFP]
        fps = GA[:, :].bitcast(BF16)[0:8, 0:LFP]
        ksum2 = SB("ksum", [128, NH * NBLK], F32)
        kb322 = SB("kb32", [128, NH * NBLK], F32)
        ksum = ksum2[:, :].rearrange("p (h k) -> p h k", h=NH)
        kb32 = kb322[:, :].rearrange("p (h k) -> p h k", h=NH)
        bmv2 = SB("bmv", [128, 2048], F32)
        khi = SB("khi", [128, NH, NBLK], BF16)
        klo = SB("klo", [128, NH, NBLK], BF16)
        kcT = SB("kcT", [128, 4, 256], BF16)
        vcs = SB("vcs", [128, 2, 512], BF16)
        TT = SB("TT", [128, NH, 136], F32)
        st = SB("st", [128, 16], F32)
        sgt = SB("sgt", [128, 1024], F32)
        mean_sb = sgt[:, 0:512]
        rstd_sb = sgt[:, 512:1024]
        gsb = [SB("gsb%d" % i, [128, 64], F32) for i in range(2)]
        m30 = [SB("m30%d" % i, [128, 64], F32) for i in range(2)]
        mx8 = [SB("mx8%d" % i, [128, 8], F32) for i in range(2)]
        rs = [SB("rs%d" % i, [128, 64], F32) for i in range(4)]
        Oacc = [SB("Oacc%d" % i, [128, 128], F32) for i in range(4)]
        Pb = [xs[:, i * 512:(i + 1) * 512] for i in range(2)]
        PTs = [xs[:, 1024 + i * 512:1024 + (i + 1) * 512] for i in range(2)]
        rtmp = [sgt[:, i * 512:(i + 1) * 512] for i in range(2)]
        zcol = SB("zcol", [128, 1], F32)
        dummy = SB("dummy", [128, 1], F32)
        epsc = SB("epsc", [128, 1], F32)
        B = {n: Buf(n) for n in ("ident onesf gvec cw cp c31m relaug khi klo kcT vcs TT st sgt zcol fpd").split()}
        B["mean_sb"] = B["sgt"]
        B["rstd_sb"] = B["sgt"]
        B["ohs"] = bYA[0]
        B["fps"] = bGA[0]
        B["ksum"] = Buf("ksum")
        B["kb32"] = Buf("kb32")
        pstg = [Buf("pstg%d" % i) for i in range(4)]
        pout = [Buf("pout%d" % i) for i in range(2)]
        B["dummy"] = Buf("dummy")
        brtmp = [B["sgt"], B["sgt"]]
        bgsb = [Buf("gsb%d" % i) for i in range(2)]
        bm30 = [Buf("m30%d" % i) for i in range(2)]
        bmx8 = [Buf("mx8%d" % i) for i in range(2)]
        brs = [Buf("rs%d" % i) for i in range(4)]
        bOacc = [Buf("Oacc%d" % i) for i in range(4)]
        bPb = [Buf("Pb%d" % i) for i in range(2)]
        bPTs = [Buf("PTs%d" % i) for i in range(2)]
        bxs_all = [bxs] + bPb + bPTs
        bctmp = [Buf("ctmp0"), Buf("ctmp1")]
        bHK = [Buf("HK%d" % h) for h in range(2)]
        bbm = [[Buf("bm%d_%d" % (h, t)) for t in range(4)] for h in range(NH)]
        bA = [Buf("A%d" % t) for t in range(4)]
        bwd = [Buf("wd%d" % s) for s in range(NSLAB)]
        bkv = [Buf("kv%d" % g) for g in range(NAG)]
        G4 = GA[:, :].bitcast(BF16)[:, 0:6144].rearrange("p (c q t) -> p c q t", c=8, q=8)
        Y3 = YA[:, :].rearrange("p (c t) -> p c t", c=8)
        Y4 = YA[:, :].rearrange("p (c q t) -> p c q t", c=8, q=8)
        GAb = GA[:, :].bitcast(BF16)
        YAb = YA[:, :].bitcast(BF16)
        HK = GAb[:, 0:2 * 8 * 256].rearrange("p (h s k) -> p h s k", h=2, s=8)
        bmv = bmv2[:, :].rearrange("p (h t k) -> p h t k", h=NH, t=4)
        Av = YAb[:, 4096:8192].rearrange("p (t f) -> p t f", t=4)
        h1a = YAb[:, :].rearrange("p (c t) -> p c t", c=16)
        h1b = GAb[:, 0:8192].rearrange("p (c t) -> p c t", c=16)
        gfin_v = GA[:, 4096:6144]
        mm = [PS("mm%d" % i, [128, 512], F32) for i in range(3)]
        bmm = [Buf("mm%d" % i) for i in range(3)]
        tr = [PS("tr%d" % i, [128, 1024], BF16) for i in range(2)]
        btr = [Buf("tr%d" % i) for i in range(2)]
        sp_ = [PS("sps%d" % i, [128, 512], F32) for i in range(2)]
        bsp = [Buf("sps%d" % i) for i in range(2)]
        misc = PS("misc", [128, 512], F32)
        bmisc = [Buf("misc%d" % i) for i in range(4)]

        print("SBUF bytes remaining per partition:", nc.sbuf_bytes_remaining)
        cnt = {"mm": 0, "tr": 0, "sp": 0, "ev": 0, "misc": 0, "pb": 0, "pt": 0, "g": 0, "rt": 0}

        def rot(key, n):
            v = cnt[key] % n
            cnt[key] += 1
            return v

        def MM(out, lhsT, rhs, start, stop, reads, writes):
            P.op("pe", lambda e, o=out, l=lhsT, r=rhs, s=start, t=stop: e.matmul(o, lhsT=l, rhs=r, start=s, stop=t),
                 reads, writes)

        def TR(out, in_, idn, reads, writes):
            P.op("pe", lambda e, o=out, i=in_, d=idn: e.transpose(out=o, in_=i, identity=d), reads, writes)

        def ACTV(out, in_, func, reads, writes, bias=None, scale=1.0, accum=None):
            def f(e, o=out, i=in_, fn=func, b=bias, s=scale, a=accum):
                kw = {}
                if b is not None:
                    kw["bias"] = b
                if a is not None:
                    kw["accum_out"] = a
                return e.activation(out=o, in_=i, func=fn, scale=s, **kw)
            P.op("act", f, reads, writes)

        def CPY(eng, out, in_, reads, writes):
            if eng == "act":
                P.op("act", lambda e, o=out, i=in_: e.copy(out=o, in_=i), reads, writes)
            else:
                P.op(eng, lambda e, o=out, i=in_: e.tensor_copy(out=o, in_=i), reads, writes)

        def TS(eng, out, in0, s1, s2, op0, op1, reads, writes):
            if op1 is None:
                P.op(eng, lambda e, o=out, i=in0, a=s1, p0=op0: e.tensor_scalar(out=o, in0=i, scalar1=a, scalar2=None, op0=p0),
                     reads, writes)
            else:
                P.op(eng, lambda e, o=out, i=in0, a=s1, b=s2, p0=op0, p1=op1:
                     e.tensor_scalar(out=o, in0=i, scalar1=a, scalar2=b, op0=p0, op1=p1), reads, writes)

        def TTO(eng, out, in0, in1, op, reads, writes):
            P.op(eng, lambda e, o=out, a=in0, b=in1, p=op: e.tensor_tensor(out=o, in0=a, in1=b, op=p), reads, writes)

        def STT(eng, out, in0, sc, in1, op0, op1, reads, writes):
            P.op(eng, lambda e, o=out, a=in0, s=sc, b=in1, p0=op0, p1=op1:
                 e.scalar_tensor_tensor(out=o, in0=a, scalar=s, in1=b, op0=p0, op1=p1), reads, writes)

        def MSET(eng, out, val, writes):
            P.op(eng, lambda e, o=out, v=val: e.memset(o, v), (), writes)

        def DMA(out, in_, reads, writes, queue="sp"):
            P.dma(lambda e, o=out, i=in_: e.dma_start(out=o, in_=i), reads, writes, queue)

        def evac_eng():
            return ("act", "dve")[rot("ev", 2)]

        plan = []
        stream_state = {"next_issue": 0, "pos": 0}

        def issue_item(i):
            kind, args = plan[i]
            rb = i % RING
            dst = ring[rb]
            if kind == "slab":
                s = args
                DMA(dst[:, :], wbf_d[s, :, :], [bwd[s]], bring[rb])
            elif kind == "cdiag":
                DMA(dst[:, :], cd_d[args, :, :], [bcd[args]], bring[rb])
            else:
                h, b0, nb = args
                DMA(dst[:, 0:nb * 256], kt_d[h, :, b0 * 256:(b0 + nb) * 256],
                    [bkv[g] for g in range(b0 // 2, (b0 + nb + 1) // 2)], [bringK[rb]])
                DMA(dst[:, 4096:4096 + nb * 256].rearrange("p (c d) -> p c d", d=128),
                    v_d[h, :, b0 * 2:(b0 + nb) * 2, :],
                    [bkv[g] for g in range(b0 // 2, (b0 + nb + 1) // 2)], [bringV[rb]])

        def fetch(kind, args):
            if P.dry:
                plan.append((kind, args))
                return 0
            i = stream_state["pos"]
            stream_state["pos"] += 1
            assert plan[i] == (kind, args), (plan[i], kind, args)
            while stream_state["next_issue"] < min(len(plan), i + RING - 2) or stream_state["next_issue"] <= i:
                issue_item(stream_state["next_issue"])
                stream_state["next_issue"] += 1
            return i % RING

        def norm_transpose(X, bX, npart, tokoff):
            MSET("dve", st[0:npart, 0:1], 0.0, [B["st"]])
            ACTV(xs[0:npart, :], X[0:npart, :], AF.Square, [bX, B["st"]], bxs_all + [B["st"]], accum=st[0:npart, 0:1])
            ACTV(st[0:npart, 1:2], st[0:npart, 0:1], AF.Sqrt, [B["st"], B["zcol"]], [B["st"]], bias=epsc[0:npart, 0:1], scale=1.0 / D)
            P.op("dve", lambda e, o=st[0:npart, 1:2], i=st[0:npart, 1:2]: e.reciprocal(out=o, in_=i), [B["st"]], [B["st"]])
            TS("dve", xs[0:npart, :], X[0:npart, :], st[0:npart, 1:2], None, ALU.mult, None, [bX, B["st"]], bxs_all)
            for half in range(2):
                tb = rot("tr", 2)
                for c in range(8):
                    dc = half * 8 + c
                    TR(tr[tb][:, c * 128:c * 128 + npart], xs[0:npart, dc * 128:(dc + 1) * 128], ident[0:npart, 0:npart],
                       bxs_all + [B["ident"]], [btr[tb]])
                src = tr[tb][:, :].rearrange("p (c t) -> p c t", c=8)[:, :, 0:npart]
                CPY(evac_eng(), nT[:, half * 8:(half + 1) * 8, tokoff:tokoff + npart], src, [btr[tb]], [bnT])

        allGA_ = bGA
        allYA_ = bYA

        def emit_all():
            MSET("pool", ident[:, :], 1.0, [B["ident"]])
            P.op("pool", lambda e: e.affine_select(out=ident[:, :], in_=ident[:, :], pattern=[[-1, 128]],
                                                   compare_op=ALU.is_equal, fill=0.0, base=0, channel_multiplier=1),
                 [B["ident"]], [B["ident"]])
            MSET("pool", onesf[:, :], 1.0 / 1024.0, [B["onesf"]])
            MSET("pool", zcol[:, :], 0.0, [B["zcol"]])
            MSET("pool", epsc[:, :], EPS, [B["zcol"]])
            DMA(gvec[:, :], gv_d[:, :], [], [B["gvec"]])
            DMA(cw[:, :], cw_d[:, :], [], [B["cw"]])
            DMA(cp[:, :], cp_d[:, :], [], [B["cp"]])
            DMA(c31m[:, :], c31_d[:, :], [], [B["c31m"]])
            DMA(relaug[:, :], rel_d[:, :], [], [B["relaug"]])
            DMA(ohs[:, :], oh_d[:, :], [], [B["ohs"]])
            for c0 in range(0, LFP, 512):
                w = min(512, LFP - c0)
                mb = rot("mm", 3)
                MM(mm[mb][0:8, 0:w], relaug[:, :], ohs[:, c0:c0 + w], True, True, [B["relaug"], B["ohs"]], [bmm[mb]])
                ACTV(fps[:, c0:c0 + w], mm[mb][0:8, 0:w], AF.Copy, [bmm[mb]], [B["fps"]], scale=SQ)
            DMA(fp_d[:, :], fps[:, :], [B["fps"]], [B["fpd"]], queue="pool")
            MSET("pool", TT[:, :, :], 0.0, [B["TT"]])
            for h in range(NH):
                TS("dve", TT[:, h, :], TT[:, h, :], c31m[:, h:h + 1], None, ALU.add, None,
                   [B["TT"], B["c31m"]], [B["TT"]])
            MSET("pool", TT[0:64, :, 58:64], 0.0, [B["TT"]])
            MSET("pool", TT[0:64, :, 64:65], -NEG, [B["TT"]])
            MSET("pool", TT[0:64, :, 65:136], NEG, [B["TT"]])
            MSET("pool", TT[64:128, :, 59:65], 0.0, [B["TT"]])
            MSET("pool", TT[64:128, :, 65:66], -NEG, [B["TT"]])
            MSET("pool", TT[64:128, :, 66:136], NEG, [B["TT"]])

            for j in range(4):
                for ccl in range(2):
                    for k in range(KCONV):
                        cc = j * 2 + ccl
                        di = (ccl * KCONV + k) * 128
                        TS("dve", ring[j][:, di:di + 128], ident[:, :], cw[:, cc * KCONV + k:cc * KCONV + k + 1], None,
                           ALU.mult, None, [B["ident"], B["cw"]], bring[j])
                DMA(cd_d[j, :, 0:2 * KCONV * 128], ring[j][:, 0:2 * KCONV * 128], bring[j], [bcd[j]], queue="pool")
            gcol = {}
            for s_ in range(0, 10):
                gcol[s_] = 0
            gcol[14] = 16
            gcol[15] = 32
            gcol[16] = 32
            for s_ in range(18, 34):
                gcol[s_] = 48
            stg = [GA[:, 0:2048], GA[:, 2048:4096], YA[:, 0:2048], YA[:, 2048:4096]]
            GAb_ = GA[:, :].bitcast(BF16)
            pob = [GAb_[:, 8192:10240], GAb_[:, 10240:12288]]
            pk = {"k": 0}
            P.op("pool", lambda e: e.memset(dummy[:, :], 0.0), [bYA[0], bGA[0]] + allGA_ + allYA_, pstg + pout + [B["dummy"]])

            def prep_load(s_, qs):
                k = pk["k"]
                pk["k"] += 1
                DMA(stg[k % 4], wall_d[s_, :, qs * 2048:(qs + 1) * 2048], [], [pstg[k % 4]], queue="sp")
                return (s_, qs, k)

            def prep_compute(item, direct=None):
                s_, qs, k = item
                si = k % 4
                oi = k % 2
                if direct is not None:
                    for dl in range(4):
                        eng = ("dve", "act")[(k + dl) % 2]
                        gc = gvec[:, gcol[s_] + qs * 4 + dl:gcol[s_] + qs * 4 + dl + 1]
                        o = ring[direct][:, qs * 2048 + dl * 512:qs * 2048 + (dl + 1) * 512]
                        i_ = stg[si][:, dl * 512:(dl + 1) * 512]
                        if eng == "act":
                            ACTV(o, i_, AF.Copy, [pstg[si], B["gvec"]], bring[direct], scale=gc)
                        else:
                            TS(eng, o, i_, gc, None, ALU.mult, None, [pstg[si], B["gvec"]], bring[direct])
                    return
                if s_ in gcol:
                    for dl in range(4):
                        eng = ("dve", "act")[(k + dl) % 2]
                        gc = gvec[:, gcol[s_] + qs * 4 + dl:gcol[s_] + qs * 4 + dl + 1]
                        o = pob[oi][:, dl * 512:(dl + 1) * 512]
                        i_ = stg[si][:, dl * 512:(dl + 1) * 512]
                        if eng == "act":
                            ACTV(o, i_, AF.Copy, [pstg[si], B["gvec"]], [pout[oi]], scale=gc)
                        else:
                            TS(eng, o, i_, gc, None, ALU.mult, None, [pstg[si], B["gvec"]], [pout[oi]])
                else:
                    for q in range(2):
                        eng = ("dve", "act")[(k + q) % 2]
                        CPY(eng, pob[oi][:, q * 1024:(q + 1) * 1024], stg[si][:, q * 1024:(q + 1) * 1024],
                            [pstg[si]], [pout[oi]])
                DMA(wbf_d[s_, :, qs * 2048:(qs + 1) * 2048], pob[oi], [pout[oi]], [bwd[s_]], queue="pool")

            pending = []

            def prep_tick(n_new=2):
                while pending:
                    prep_compute(pending.pop(0))
                for _ in range(n_new):
                    if rest_items:
                        pending.append(prep_load(*rest_items.pop(0)))

            for s_ in (2, 3, 4, 5):
                for qs in range(4):
                    prep_compute(prep_load(s_, qs), direct=s_ - 2)
            rest_items = [(s_, qs) for s_ in range(NSLAB) if s_ not in (2, 3, 4, 5) for qs in range(4)]

            MSET("pool", ksum[:, :, :], 0.0, [B["ksum"]])
            Wk = [ring[0][:, :].rearrange("p (c f) -> p c f", c=16), ring[1][:, :].rearrange("p (c f) -> p c f", c=16)]
            Wv = [ring[2][:, :].rearrange("p (c f) -> p c f", c=16), ring[3][:, :].rearrange("p (c f) -> p c f", c=16)]
            ntile_a = S // 128
            for i in range(min(3, ntile_a)):
                DMA(H[i % 4][:, :], xa_d[i, :, :], [], [bH[i % 4]])
            for ag in range(NAG):
                for t in range(4):
                    ti = ag * 4 + t
                    prep_tick()
                    if ti + 3 < ntile_a:
                        DMA(H[(ti + 3) % 4][:, :], xa_d[ti + 3, :, :], [], [bH[(ti + 3) % 4]])
                    norm_transpose(H[ti % 4], bH[ti % 4], 128, t * 128)
                for h in range(NH):
                    mb = rot("mm", 3)
                    for dc in range(DC):
                        MM(mm[mb][:, :], Wk[h // 4][:, dc, (h % 4) * 128:(h % 4 + 1) * 128], nT[:, dc, 0:512],
                           dc == 0, dc == DC - 1, bring[h // 4] + [bnT], [bmm[mb]])
                    for bl in range(2):
                        ACTV(QT[:, h, bl * 256:(bl + 1) * 256], mm[mb][:, bl * 256:(bl + 1) * 256], AF.Copy,
                             [bmm[mb], B["ksum"]], [bQT[h], B["ksum"]],
                             accum=ksum[:, h, ag * 2 + bl:ag * 2 + bl + 1])
                for t in range(4):
                    for sl in range(2):
                        mb = rot("mm", 3)
                        for dc in range(DC):
                            MM(mm[mb][:, :], nT[:, dc, t * 128:(t + 1) * 128], Wv[sl][:, dc, :],
                               dc == 0, dc == DC - 1, bring[2 + sl] + [bnT], [bmm[mb]])
                        CPY("dve", CT[:, t * 2 + sl, :], mm[mb][:, :], [bmm[mb]], [bCT[t * 2 + sl]])
                DMA(kt_d[:, :, ag * 512:(ag + 1) * 512].rearrange("h p t -> p h t"), QT[:, :, :],
                    bQT, [bkv[ag]], queue="sp")
                for t in range(4):
                    DMA(v_d[:, :, ag * 4 + t, :].rearrange("h p d -> p h d"),
                        CT2[:, t * 1024:(t + 1) * 1024].rearrange("p (h d) -> p h d", d=128),
                        bCT[2 * t:2 * t + 2], [bkv[ag]], queue="sp")
            while rest_items or pending:
                prep_tick()
            P.op("pool", lambda e: e.memset(dummy[:, :], 0.0), pstg + pout, pstg + pout + allGA_ + allYA_ + [B["dummy"]])
            TS("dve", kb32[:, :, :], ksum[:, :, :], 1.0 / 256.0, None, ALU.mult, None, [B["ksum"]], [B["kb32"]])
            CPY("dve", khi[:, :, :], kb32[:, :, :], [B["kb32"]], [B["khi"]])
            CPY("dve", ksum[:, :, :], khi[:, :, :], [B["khi"]], [B["ksum"]])
            TTO("dve", klo[:, :, :], kb32[:, :, :], ksum[:, :, :], ALU.subtract, [B["kb32"], B["ksum"]], [B["klo"]])

            for i in range(2):
                DMA(H[i][:, :], mem_d[i, :, :], [], [bH[i]])
                norm_transpose(H[i], bH[i], 128, i * 128)
            rk = fetch("slab", 15)
            wkc = ring[rk][:, :].rearrange("p (c f) -> p c f", c=16)
            for h4 in range(4):
                mb = rot("mm", 3)
                for dc in range(DC):
                    MM(mm[mb][:, 0:256], wkc[:, dc, h4 * 128:(h4 + 1) * 128], nT[:, dc, 0:256], dc == 0, dc == DC - 1,
                       bring[rk] + [bnT], [bmm[mb]])
                CPY(evac_eng(), kcT[:, h4, :], mm[mb][:, 0:256], [bmm[mb]], [B["kcT"]])
            rv = fetch("slab", 16)
            wvc = ring[rv][:, :].rearrange("p (c f) -> p c f", c=16)
            for i in range(2):
                mb = rot("mm", 3)
                for dc in range(DC):
                    MM(mm[mb][:, :], nT[:, dc, i * 128:(i + 1) * 128], wvc[:, dc, :], dc == 0, dc == DC - 1,
                       bring[rv] + [bnT], [bmm[mb]])
                CPY(evac_eng(), vcs[:, i, :], mm[mb][:, :], [bmm[mb]], [B["vcs"]])

            for m in range(NG):
                emit_group(m)

        def emit_group(m):
            allGA = bGA
            allYA = bYA
            for t in range(4):
                DMA(H[t][:, :], xo_d[m * 4 + t, :, :], [], [bH[t]])
            for t in range(4):
                norm_transpose(H[t], bH[t], 128, t * 128)
            for i in range(2):
                hx = YA[:, i * 2048:(i + 1) * 2048]
                DMA(hx, xh_d[m * 2 + i, :, :], [], [bYA[i * 4 + j] for j in range(4)])
                norm_transpose_from(hx, [bYA[i * 4 + j] for j in range(4)], i * 128)
            slab_of = {}
            for cc in range(8):
                if cc % 4 == 0:
                    slab_of["v"] = fetch("slab", 6 + cc // 4)
                    slab_of["g"] = fetch("slab", 8 + cc // 4)
                wv_ = ring[slab_of["v"]][:, :].rearrange("p (c f) -> p c f", c=16)
                wg_ = ring[slab_of["g"]][:, :].rearrange("p (c f) -> p c f", c=16)
                fo = (cc % 4) * 128
                mv = rot("mm", 3)
                mg = rot("mm", 3)
                for dc in range(DC):
                    MM(mm[mv][:, :], wv_[:, dc, fo:fo + 128], nT[:, dc, 0:512], dc == 0, dc == DC - 1,
                       bring[slab_of["v"]] + [bnT], [bmm[mv]])
                    MM(misc[:, 0:256], wv_[:, dc, fo:fo + 128], nTh[:, dc, :], dc == 0, dc == DC - 1,
                       bring[slab_of["v"]] + bCT, [bmisc[0], bmisc[1]])
                for dc in range(DC):
                    MM(mm[mg][:, :], wg_[:, dc, fo:fo + 128], nT[:, dc, 0:512], dc == 0, dc == DC - 1,
                       bring[slab_of["g"]] + [bnT], [bmm[mg]])
                    MM(misc[:, 256:512], wg_[:, dc, fo:fo + 128], nTh[:, dc, :], dc == 0, dc == DC - 1,
                       bring[slab_of["g"]] + bCT, [bmisc[2], bmisc[3]])
                ACTV(sgt[:, 0:512], mm[mg][:, :], AF.Sigmoid, [bmm[mg]], [B["sgt"]])
                ACTV(sgt[:, 512:768], misc[:, 256:512], AF.Sigmoid, [bmisc[2], bmisc[3]], [B["sgt"]])
                TTO("dve", G4[:, cc, :, 32:96], mm[mv][:, :].rearrange("p (q t) -> p q t", q=8),
                    sgt[:, 0:512].rearrange("p (q t) -> p q t", q=8), ALU.mult, [bmm[mv], B["sgt"]], [bGA[cc]])
                TTO("dve", G4[:, cc, :, 0:32], misc[:, 0:256].rearrange("p (q t) -> p q t", q=8),
                    sgt[:, 512:768].rearrange("p (q t) -> p q t", q=8), ALU.mult,
                    [bmisc[0], bmisc[1], B["sgt"]], [bGA[cc]])
            for cp_ in range(4):
                rcd = fetch("cdiag", cp_)
                for ccl in range(2):
                    cc = cp_ * 2 + ccl
                    mb = rot("mm", 3)
                    for k in range(KCONV):
                        di = (ccl * KCONV + k) * 128
                        MM(mm[mb][:, :].rearrange("p (q t) -> p q t", q=8), ring[rcd][:, di:di + 128],
                           G4[:, cc, :, k + 2:k + 66], k == 0, k == KCONV - 1, [bGA[cc]] + bring[rcd], [bmm[mb]])
                    if cc % 2 == 0:
                        ACTV(Y3[:, cc, :], mm[mb][:, :], AF.Identity, [bmm[mb], B["cp"]], [bYA[cc]], bias=cp[:, cc:cc + 1])
                    else:
                        TS("dve", Y3[:, cc, :], mm[mb][:, :], cp[:, cc:cc + 1], None, ALU.add, None,
                           [bmm[mb], B["cp"]], [bYA[cc]])
            for h in range(NH):
                if h % 4 == 0:
                    rq = fetch("slab", h // 4)
                    wq_ = ring[rq][:, :].rearrange("p (c f) -> p c f", c=16)
                mb = rot("mm", 3)
                for dc in range(DC):
                    MM(mm[mb][:, :], wq_[:, dc, (h % 4) * 128:(h % 4 + 1) * 128], nT[:, dc, 0:512], dc == 0, dc == DC - 1,
                       bring[rq] + [bnT], [bmm[mb]])
                CPY(evac_eng(), QT[:, h, :], mm[mb][:, :], [bmm[mb]], [bQT[h]])
            GS = GA[:, 0:4096].rearrange("p (c t) -> p c t", c=8)
            for cc in range(8):
                TTO("pool", GS[:, cc, :], Y3[:, cc, :], Y3[:, cc, :], ALU.mult, [bYA[cc]] + allGA, allGA)
            m1 = rot("mm", 3)
            m2 = rot("mm", 3)
            for cc in range(8):
                MM(mm[m1][:, :], onesf[:, :], Y3[:, cc, :], cc == 0, cc == 7, [B["onesf"], bYA[cc]], [bmm[m1]])
            for cc in range(8):
                MM(mm[m2][:, :], onesf[:, :], GS[:, cc, :], cc == 0, cc == 7, [B["onesf"]] + allGA, [bmm[m2]])
            CPY("dve", mean_sb[:, :], mm[m1][:, :], [bmm[m1]], [B["mean_sb"]])
            TTO("dve", rstd_sb[:, :], mean_sb[:, :], mean_sb[:, :], ALU.mult, [B["mean_sb"]], [B["rstd_sb"]])
            TTO("dve", rstd_sb[:, :], mm[m2][:, :], rstd_sb[:, :], ALU.subtract, [bmm[m2], B["rstd_sb"]], [B["rstd_sb"]])
            ACTV(rstd_sb[:, :], rstd_sb[:, :], AF.Sqrt, [B["rstd_sb"], B["zcol"]], [B["rstd_sb"]], bias=epsc[:, 0:1])
            P.op("dve", lambda e, o=rstd_sb[:, :], i=rstd_sb[:, :]: e.reciprocal(out=o, in_=i), [B["rstd_sb"]], [B["rstd_sb"]])
            for cc in range(8):
                eng = "pool" if cc % 2 == 0 else "dve"
                TTO(eng, Y3[:, cc, :], Y3[:, cc, :], mean_sb[:, :], ALU.subtract, [bYA[cc], B["mean_sb"]], [bYA[cc]])
                TTO(eng, Y3[:, cc, :], Y3[:, cc, :], rstd_sb[:, :], ALU.mult, [bYA[cc], B["rstd_sb"]], [bYA[cc]])
                ACTV(CT[:, cc, :], Y3[:, cc, :], AF.Silu, [bYA[cc], B["cp"]], [bCT[cc]],
                     bias=cp[:, 16 + cc:17 + cc], scale=cp[:, 8 + cc:9 + cc])
            for h in range(NH):
                for t in range(4):
                    tau = m * 4 + t
                    ncol = 2 * tau + 2
                    gi = rot("g", 2)
                    rg = rot("misc", 4)
                    MSET("pool", gsb[gi][:, :], -1e30, [bgsb[gi]])
                    gp = misc[:, rg * 128:rg * 128 + 64]
                    MM(gp[:, 0:ncol], QT[:, h, t * 128:(t + 1) * 128], khi[:, h, 0:ncol], True, False,
                       [bQT[h], B["khi"]], [bmisc[rg]])
                    MM(gp[:, 0:ncol], QT[:, h, t * 128:(t + 1) * 128], klo[:, h, 0:ncol], False, True,
                       [bQT[h], B["klo"]], [bmisc[rg]])
                    if tau > 0:
                        CPY("dve", gsb[gi][0:64, 0:2 * tau], gp[0:64, 0:2 * tau], [bmisc[rg]], [bgsb[gi]])
                    CPY("dve", gsb[gi][64:128, 0:2 * tau + 1], gp[64:128, 0:2 * tau + 1], [bmisc[rg]], [bgsb[gi]])
                    P.op("dve", lambda e, o=mx8[gi][:, :], i=gsb[gi][:, 0:max(8, ncol)]: e.max(out=o, in_=i),
                         [bgsb[gi]], [bmx8[gi]])
                    TS("dve", mx8[gi][:, 2:3], mx8[gi][:, 2:3], -1e29, None, ALU.max, None, [bmx8[gi]], [bmx8[gi]])
                    TS("dve", m30[gi][:, 0:ncol], gsb[gi][:, 0:ncol], mx8[gi][:, 2:3], 1e30, ALU.subtract, ALU.mult,
                       [bgsb[gi], bmx8[gi]], [bm30[gi]])
                    TS("dve", m30[gi][:, 0:ncol], m30[gi][:, 0:ncol], -1.0, 0.0, ALU.max, ALU.min, [bm30[gi]], [bm30[gi]])
                    STT("dve", bmv[:, h, t, 0:ncol], m30[gi][:, 0:ncol], -NEG, TT[:, h, 64 - 2 * tau:64 - 2 * tau + ncol],
                        ALU.mult, ALU.add, [bm30[gi], B["TT"]], [bbm[h][t]])
            nblk_g = 8 * m + 8
            for h in range(NH):
                hb = h % 2
                DMA(HK[0:64, hb, :, :], bass.AP(fp_h, h * LFP, [[1, 64], [256, 8], [1, 256]]),
                    [B["fpd"]], [bHK[hb]] + allGA)
                DMA(HK[64:128, hb, :, :], bass.AP(fp_h, h * LFP + 256, [[1, 64], [256, 8], [1, 256]]),
                    [B["fpd"]], [bHK[hb]] + allGA)
                steps = []
                for kc in range((nblk_g + 15) // 16):
                    b0 = kc * 16
                    nb = min(16, nblk_g - b0)
                    for t in range(4):
                        tau = m * 4 + t
                        blks = list(range(b0, min(b0 + nb, 2 * tau + 2)))
                        npair = len(blks) // 2
                        for pi in range(npair):
                            steps.append(dict(kc=kc, b0=b0, nb=nb, t=t, tau=tau, pi=pi, npair=npair,
                                              kb=(blks[2 * pi], blks[2 * pi + 1])))
                cur = {"kc": -1, "rb": 0}

                def stA(sx):
                    if sx["kc"] != cur["kc"]:
                        cur["kc"] = sx["kc"]
                        cur["rb"] = fetch("kv", (h, sx["b0"], sx["nb"]))
                    rb = cur["rb"]
                    sx["rb"] = rb
                    t, tau, b0 = sx["t"], sx["tau"], sx["b0"]
                    KTc = ring[rb][:, 0:4096]
                    sb_ = rot("sp", 2)
                    sx["sb"] = sb_
                    kb0, kb1 = sx["kb"]
                    lo = (kb0 - b0) * 256
                    qT = QT[:, h, t * 128:(t + 1) * 128]
                    if 2 * tau + 1 - kb0 <= 7:
                        for bi, kb in enumerate(sx["kb"]):
                            so = bi * 256
                            MM(sp_[sb_][:, so:so + 256], qT, KTc[:, lo + so:lo + so + 256], True, False,
                               [bQT[h], bringK[rb]], [bsp[sb_]])
                            MM(sp_[sb_][:, so:so + 256], ident[:, :], HK[:, hb, 2 * tau + 1 - kb, :], False, True,
                               [B["ident"], bHK[hb]] + allGA, [bsp[sb_]])
                    else:
                        MM(sp_[sb_][:, :], qT, KTc[:, lo:lo + 512], True, True, [bQT[h], bringK[rb]], [bsp[sb_]])
                    pb = rot("pb", 2)
                    sx["pb"] = pb
                    for bi, kb in enumerate(sx["kb"]):
                        ACTV(Pb[pb][:, bi * 256:(bi + 1) * 256], sp_[sb_][:, bi * 256:(bi + 1) * 256], AF.Exp,
                             [bsp[sb_], bbm[h][t]], [bPb[pb]], bias=bmv[:, h, t, kb:kb + 1], scale=SCALE)

                def stC(sx):
                    pb = sx["pb"]
                    kb0 = sx["kb"][0]
                    P.op("dve", lambda e, o=rs[sx["t"]][:, kb0:kb0 + 2], i=Pb[pb].rearrange("p (b k) -> p b k", b=2):
                         e.tensor_reduce(out=o, in_=i, axis=AX.X, op=ALU.add), [bPb[pb]], [brs[sx["t"]]])
                    tb = rot("tr", 2)
                    for q4 in range(4):
                        TR(tr[tb][:, q4 * 128:(q4 + 1) * 128], Pb[pb][:, q4 * 128:(q4 + 1) * 128], ident[:, :],
                           [bPb[pb], B["ident"]], [btr[tb]])
                    pt = rot("pt", 2)
                    sx["pt"] = pt
                    CPY("dve", PTs[pt][:, :], tr[tb][:, 0:512], [btr[tb]], [bPTs[pt]])

                def stE(sx):
                    rb, t, b0, pt = sx["rb"], sx["t"], sx["b0"], sx["pt"]
                    Vc = ring[rb][:, 4096:8192].rearrange("p (c d) -> p c d", d=128)
                    if sx["pi"] == 0:
                        cur["ob%d" % t] = rot("misc", 4)
                    ob = cur["ob%d" % t]
                    Ops = misc[:, ob * 128:(ob + 1) * 128]
                    for q4 in range(4):
                        kb = sx["kb"][q4 // 2]
                        vch = (kb - b0) * 2 + (q4 % 2)
                        MM(Ops, PTs[pt][:, q4 * 128:(q4 + 1) * 128], Vc[:, vch, :],
                           sx["pi"] == 0 and q4 == 0, sx["pi"] == sx["npair"] - 1 and q4 == 3,
                           [bPTs[pt], bringV[rb]], [bmisc[ob]])
                    if sx["pi"] == sx["npair"] - 1:
                        if sx["kc"] == 0:
                            CPY("dve", Oacc[t][:, :], Ops, [bmisc[ob]], [bOacc[t]])
                        else:
                            TTO("dve", Oacc[t][:, :], Oacc[t][:, :], Ops, ALU.add, [bmisc[ob], bOacc[t]], [bOacc[t]])

                ns = len(steps)
                for i in range(ns + 2):
                    if i < ns:
                        stA(steps[i])
                    if 0 <= i - 1 < ns:
                        stC(steps[i - 1])
                    if 0 <= i - 2 < ns:
                        stE(steps[i - 2])
                for t in range(4):
                    tau = m * 4 + t
                    P.op("dve", lambda e, o=st[:, 4 + t:5 + t], i=rs[t][:, 0:2 * tau + 2]:
                         e.tensor_reduce(out=o, in_=i, axis=AX.X, op=ALU.add), [brs[t]], [B["st"]])
                    P.op("dve", lambda e, o=st[:, 8 + t:9 + t], i=st[:, 4 + t:5 + t]: e.reciprocal(out=o, in_=i),
                         [B["st"]], [B["st"]])
                    TS("dve", Av[:, t, h * 128:(h + 1) * 128], Oacc[t][:, :], st[:, 8 + t:9 + t], None, ALU.mult, None,
                       [bOacc[t], B["st"]] + allYA[4:8], [bA[t]] + allYA[4:8])
            for h in range(NH):
                tb = rot("tr", 2)
                for t in range(4):
                    TR(tr[tb][:, t * 128:(t + 1) * 128], Av[:, t, h * 128:(h + 1) * 128], ident[:, :],
                       [bA[t], B["ident"]] + allYA[4:8], [btr[tb]])
                CPY(evac_eng(), QT[:, h, :], tr[tb][:, 0:512], [btr[tb]], [bQT[h]])
            for ds in range(4):
                rw = fetch("slab", 10 + ds)
                wo_ = ring[rw][:, :].rearrange("p (c f) -> p c f", c=16)
                for t in range(4):
                    mb = rot("mm", 3)
                    for ic in range(16):
                        lhs = QT[:, ic, t * 128:(t + 1) * 128] if ic < 8 else CT[:, ic - 8, t * 128:(t + 1) * 128]
                        MM(mm[mb][:, :], lhs, wo_[:, ic, :], ic == 0, ic == 15,
                           [bQT[ic] if ic < 8 else bCT[ic - 8]] + bring[rw], [bmm[mb]])
                    TTO("dve", H[t][:, ds * 512:(ds + 1) * 512], H[t][:, ds * 512:(ds + 1) * 512], mm[mb][:, :], ALU.add,
                        [bH[t], bmm[mb]], [bH[t]])
            for t in range(4):
                norm_transpose(H[t], bH[t], 128, t * 128)
            rq = fetch("slab", 14)
            wqc = ring[rq][:, :].rearrange("p (c f) -> p c f", c=16)
            for h4 in range(4):
                mb = rot("mm", 3)
                for dc in range(DC):
                    MM(mm[mb][:, :], wqc[:, dc, h4 * 128:(h4 + 1) * 128], nT[:, dc, 0:512], dc == 0, dc == DC - 1,
                       bring[rq] + [bnT], [bmm[mb]])
                CPY(evac_eng(), QT[:, h4, :], mm[mb][:, :], [bmm[mb]], [bQT[h4]])
            for t in range(4):
                MSET("pool", rs[t][:, 0:4], 0.0, [brs[t]])
                for hp in range(2):
                    sb_ = rot("sp", 2)
                    pb = rot("pb", 2)
                    for hh in range(2):
                        h4 = hp * 2 + hh
                        MM(sp_[sb_][:, hh * 256:(hh + 1) * 256], QT[:, h4, t * 128:(t + 1) * 128], kcT[:, h4, :], True, True,
                           [bQT[h4], B["kcT"]], [bsp[sb_]])
                    for hh in range(2):
                        h4 = hp * 2 + hh
                        ACTV(Pb[pb][:, hh * 256:(hh + 1) * 256], sp_[sb_][:, hh * 256:(hh + 1) * 256], AF.Exp,
                             [bsp[sb_], brs[t], B["zcol"]], [bPb[pb], brs[t]], bias=zcol[:, 0:1], scale=SCALE,
                             accum=rs[t][:, h4:h4 + 1])
                    tb = rot("tr", 2)
                    for q4 in range(4):
                        TR(tr[tb][:, q4 * 128:(q4 + 1) * 128], Pb[pb][:, q4 * 128:(q4 + 1) * 128], ident[:, :],
                           [bPb[pb], B["ident"]], [btr[tb]])
                    pt = rot("pt", 2)
                    CPY("dve", PTs[pt][:, :], tr[tb][:, 0:512], [btr[tb]], [bPTs[pt]])
                    for hh in range(2):
                        h4 = hp * 2 + hh
                        ob = rot("misc", 4)
                        Ops = misc[:, ob * 128:(ob + 1) * 128]
                        for mc in range(2):
                            MM(Ops, PTs[pt][:, (hh * 2 + mc) * 128:(hh * 2 + mc + 1) * 128], vcs[:, mc, h4 * 128:(h4 + 1) * 128],
                               mc == 0, mc == 1, [bPTs[pt], B["vcs"]], [bmisc[ob]])
                        P.op("dve", lambda e, o=st[:, 12:13], i=rs[t][:, h4:h4 + 1]: e.reciprocal(out=o, in_=i),
                             [brs[t]], [B["st"]])
                        TS("dve", Av[:, t, h4 * 128:(h4 + 1) * 128], Ops, st[:, 12:13], None, ALU.mult, None,
                           [bmisc[ob], B["st"]] + allYA[4:8], [bA[t]] + allYA[4:8])
            for h4 in range(4):
                tb = rot("tr", 2)
                for t in range(4):
                    TR(tr[tb][:, t * 128:(t + 1) * 128], Av[:, t, h4 * 128:(h4 + 1) * 128], ident[:, :],
                       [bA[t], B["ident"]] + allYA[4:8], [btr[tb]])
                CPY(evac_eng(), QT[:, 4 + h4, :], tr[tb][:, 0:512], [btr[tb]], [bQT[4 + h4]])
            rw = fetch("slab", 17)
            woc = ring[rw][:, :].rearrange("p (c f) -> p c f", c=4)
            for t in range(4):
                for ds in range(4):
                    mb = rot("mm", 3)
                    for ic in range(4):
                        MM(mm[mb][:, :], QT[:, 4 + ic, t * 128:(t + 1) * 128], woc[:, ic, ds * 512:(ds + 1) * 512],
                           ic == 0, ic == 3, [bQT[4 + ic]] + bring[rw], [bmm[mb]])
                    TTO("dve", H[t][:, ds * 512:(ds + 1) * 512], H[t][:, ds * 512:(ds + 1) * 512], mm[mb][:, :], ALU.add,
                        [bH[t], bmm[mb]], [bH[t]])
            for t in range(4):
                norm_transpose(H[t], bH[t], 128, t * 128)
            for fh in range(2):
                for sl in range(8):
                    rw = fetch("slab", 18 + fh * 8 + sl)
                    w1_ = ring[rw][:, :].rearrange("p (c f) -> p c f", c=16)
                    for f4 in range(4):
                        fl = sl * 4 + f4
                        mb = rot("mm", 3)
                        for dc in range(DC):
                            MM(mm[mb][:, :], w1_[:, dc, f4 * 128:(f4 + 1) * 128], nT[:, dc, 0:512], dc == 0, dc == DC - 1,
                               bring[rw] + [bnT], [bmm[mb]])
                        ri = rot("rt", 2)
                        ACTV(rtmp[ri][:, :], mm[mb][:, :], AF.Relu, [bmm[mb]], [brtmp[ri]])
                        dst = h1a[:, fl, :] if fl < 16 else h1b[:, fl - 16, :]
                        dbuf = allYA if fl < 16 else allGA
                        TTO("pool", dst, rtmp[ri][:, :], rtmp[ri][:, :], ALU.mult, [brtmp[ri]] + dbuf, dbuf)
                for ds in range(4):
                    r2 = [fetch("slab", 34 + ds * 4 + fh * 2 + q) for q in range(2)]
                    for t in range(4):
                        mb = rot("mm", 3)
                        for fl in range(32):
                            src = h1a[:, fl, t * 128:(t + 1) * 128] if fl < 16 else h1b[:, fl - 16, t * 128:(t + 1) * 128]
                            w2_ = ring[r2[fl // 16]][:, :].rearrange("p (c f) -> p c f", c=16)
                            MM(mm[mb][:, :], src, w2_[:, fl % 16, :], fl == 0, fl == 31,
                               (allYA if fl < 16 else allGA) + bring[r2[fl // 16]], [bmm[mb]])
                        TTO("dve", H[t][:, ds * 512:(ds + 1) * 512], H[t][:, ds * 512:(ds + 1) * 512], mm[mb][:, :], ALU.add,
                            [bH[t], bmm[mb]], [bH[t]])
            DMA(gfin_v, gfin_d[:, :], allGA, allGA)
            for t in range(4):
                MSET("dve", st[:, 0:1], 0.0, [B["st"]])
                ACTV(xs[:, :], H[t][:, :], AF.Square, [bH[t], B["st"]], bxs_all + [B["st"]], accum=st[:, 0:1])
                ACTV(st[:, 1:2], st[:, 0:1], AF.Sqrt, [B["st"], B["zcol"]], [B["st"]], bias=epsc[:, 0:1], scale=1.0 / D)
                P.op("dve", lambda e, o=st[:, 1:2], i=st[:, 1:2]: e.reciprocal(out=o, in_=i), [B["st"]], [B["st"]])
                STT("dve", H[t][:, :], H[t][:, :], st[:, 1:2], gfin_v, ALU.mult, ALU.mult,
                    [bH[t], B["st"]] + allGA, [bH[t]])
                DMA(out_d[m * 4 + t, :, :], H[t][:, :], [bH[t]], [], queue="pool")

        def norm_transpose_from(X, bXl, tokoff):
            MSET("dve", st[:, 0:1], 0.0, [B["st"]])
            ACTV(xs[:, :], X, AF.Square, bXl + [B["st"]], bxs_all + [B["st"]], accum=st[:, 0:1])
            ACTV(st[:, 1:2], st[:, 0:1], AF.Sqrt, [B["st"], B["zcol"]], [B["st"]], bias=epsc[:, 0:1], scale=1.0 / D)
            P.op("dve", lambda e, o=st[:, 1:2], i=st[:, 1:2]: e.reciprocal(out=o, in_=i), [B["st"]], [B["st"]])
            TS("dve", xs[:, :], X, st[:, 1:2], None, ALU.mult, None, bXl + [B["st"]], bxs_all)
            for half in range(2):
                tb = rot("tr", 2)
                for c in range(8):
                    dc = half * 8 + c
                    TR(tr[tb][:, c * 128:(c + 1) * 128], xs[:, dc * 128:(dc + 1) * 128], ident[:, :],
                       bxs_all + [B["ident"]], [btr[tb]])
                CPY(evac_eng(), nTh[:, half * 8:(half + 1) * 8, tokoff:tokoff + 128],
                    tr[tb][:, :].rearrange("p (c t) -> p c t", c=8), [btr[tb]], bCT)

        P.dry = True
        saved = dict(cnt)
        emit_all()
        P.dry = False
        cnt.update(saved)
        emit_all()
        assert stream_state["pos"] == len(plan)
        P.emit(nc)
    return nc


def _t5_bucket_np(d):
    d = np.asarray(d, dtype=np.int64)
    max_exact = 16
    nf = np.maximum(d, max_exact).astype(np.float32)
    large = max_exact + (np.log(nf / np.float32(max_exact)) / np.float32(math.log(2048 / max_exact))
                         * np.float32(16)).astype(np.int32)
    large = np.minimum(large, 31)
    return np.where(d < max_exact, d, large)


def _slab(W, f0):
    return np.ascontiguousarray(W[:, f0:f0 + 512].reshape(16, 128, 512).transpose(1, 0, 2)).reshape(128, 8192)


def _prep_shared(inp):
    w_in = inp["w_in"][0]
    slabs = [_slab(w_in, f0) for f0 in range(0, 5120, 512)]
    slabs += [_slab(inp["w_out"][0], f0) for f0 in range(0, 2048, 512)]
    slabs.append(_slab(inp["wq_c"][0], 0))
    slabs.append(_slab(inp["wk_c"][0], 0))
    slabs.append(_slab(inp["wv_c"][0], 0))
    slabs.append(np.ascontiguousarray(inp["wo_c"][0].reshape(4, 128, 2048).transpose(1, 0, 2)).reshape(128, 8192))
    w1 = inp["w1"][0]
    slabs += [_slab(w1, f0) for f0 in range(0, 8192, 512)]
    w2 = inp["w2"][0]
    for ds in range(4):
        for fq in range(4):
            slabs.append(_slab(w2[fq * 2048:(fq + 1) * 2048], ds * 512))
    wall = np.stack(slabs).astype(np.float32)
    pc = lambda v: np.ascontiguousarray(v.reshape(-1, 128).T)
    gvec = np.concatenate([pc(inp["g_mix"][0]), pc(inp["g_cross"][0]), pc(inp["g_mem"][0]), pc(inp["g_mlp"][0])], axis=1)
    gfin = np.ascontiguousarray(np.broadcast_to(inp["g_final"][None, :], (128, D)))
    cw = np.ascontiguousarray(inp["conv_w"][0].reshape(KCONV, 8, 128).transpose(2, 1, 0)).reshape(128, 8 * KCONV)
    cp = np.concatenate([pc(inp["conv_b"][0]), pc(inp["conv_ln_g"][0]), pc(inp["conv_ln_b"][0])], axis=1)
    rel = inp["rel_bias"]
    relaug = np.concatenate([rel, np.full((1, 8), NEG, np.float32)], axis=0)
    c31 = np.ascontiguousarray(np.broadcast_to(rel[31][None, :], (128, 8)))
    return dict(wall=wall, gvec=gvec.astype(np.float32), gfin=gfin.astype(np.float32), convw=cw.astype(np.float32),
                convp=cp.astype(np.float32), relaug=relaug.astype(np.float32), c31=c31.astype(np.float32))


def _oh_table(r):
    oh = np.zeros((33, LFP), np.float32)
    y = np.arange(LFP)
    x = y + r * 64
    neg = x < 511
    oh[32, neg] = 1.0
    d = np.maximum(x - 511, 0)
    bk = _t5_bucket_np(d)
    oh[bk[~neg], y[~neg]] = 1.0
    return oh


def _core_inputs(inp, b, r, NBLK, shared, xa_cache):
    S = NBLK * 256
    x = inp["x"][b]
    xb = x.reshape(NBLK, 256, D)
    own = xb[:, r * 64:(r + 1) * 64, :]
    xo = np.ascontiguousarray(own.reshape(NBLK // 2, 128, D))
    xpad = np.concatenate([np.zeros((32, D), np.float32), x], axis=0)
    starts = (np.arange(NBLK) * 256 + r * 64)
    idx = starts[:, None] + np.arange(32)[None, :]
    xh = np.ascontiguousarray(xpad[idx].reshape(NBLK // 4, 128, D))
    if b not in xa_cache:
        xa_cache[b] = np.ascontiguousarray(xb[:, ::-1, :].reshape(S // 128, 128, D))
    d = dict(shared)
    d.update(xo=xo, xh=xh, xa=xa_cache[b], mem=np.ascontiguousarray(inp["mem"][b].reshape(2, 128, D)), oh=_oh_table(r))
    return d


_NC_CACHE = {}


def kernel(**inputs):
    inp = {k: np.asarray(v) for k, v in inputs.items()}
    Bn, S, _ = inp["x"].shape
    NBLK = S // 256
    shared = _prep_shared(inp)
    xa_cache = {}
    in_maps = []
    for b in range(Bn):
        for r in range(4):
            in_maps.append(_core_inputs(inp, b, r, NBLK, shared, xa_cache))
    if NBLK not in _NC_CACHE:
        _NC_CACHE[NBLK] = build_program(NBLK)
    nc = _NC_CACHE[NBLK]
    res = run_bass_kernel_spmd(nc, in_maps, core_ids=list(range(len(in_maps))))
    out = np.zeros((Bn, S, D), np.float32)
    ov = out.reshape(Bn, NBLK, 256, D)
    for b in range(Bn):
        for r in range(4):
            o = np.asarray(res.results[b * 4 + r]["out"]).reshape(NBLK, 64, D)
            ov[b, :, r * 64:(r + 1) * 64, :] = o
    return out
```
